# Optimizing a Trainium2 kernel written in Bass

```python
import math
import jax, jax.numpy as jnp
from jax import lax
import numpy as np

D_MODEL = 4096
BATCH = 4
SEQ = 2048
DEPTH = 1
DEC_BATCH = 2
DEC_SEQ = 4096
PAST_LEN = 128

N_MEM = 256
XA_HEADS = 4
XA_HEAD_DIM = D_MODEL // XA_HEADS
MLA_HEADS = 16
MLA_NOPE = 128
MLA_ROPE = 64
MLA_V = 128
Q_LORA = 1024
KV_LORA = 512
ROPE_THETA = 10000.0
Q_BLOCK = 128
MLA_WIDTH = MLA_HEADS * MLA_V
GLA_HEADS = 4
GLA_DK = 256
GLA_DV = 512
GLA_KEY = GLA_HEADS * GLA_DK
GLA_VAL = GLA_HEADS * GLA_DV
GLA_GATE_RANK = 16
GLA_GATE_NORM = 16.0
GLA_CHUNK = 64
D_FF = 4 * D_MODEL
LN_EPS = 1e-5
RMS_EPS = 1e-6
DN_ALPHA = (2.0 * DEPTH) ** 0.25
DN_BETA = (8.0 * DEPTH) ** -0.25
IN_SIZES = (Q_LORA, KV_LORA, MLA_ROPE, GLA_KEY, GLA_KEY, GLA_VAL, GLA_VAL, 2 * GLA_GATE_RANK, 2 * D_MODEL)
IN_VALUE_BLOCK = 5
IN_TOTAL = sum(IN_SIZES)

kernel_name = "hybrid_mla_gla_deepnorm_encoder"


def _layer_norm(x, g, b):
    xf = x.astype(jnp.float32)
    mu = jnp.mean(xf, -1, keepdims=True)
    var = jnp.mean(jnp.square(xf - mu), -1, keepdims=True)
    y = (xf - mu) * lax.rsqrt(var + LN_EPS)
    return (y * g.astype(jnp.float32) + b.astype(jnp.float32)).astype(x.dtype)


def _rms_norm(x, g):
    xf = x.astype(jnp.float32)
    y = xf * lax.rsqrt(jnp.mean(jnp.square(xf), -1, keepdims=True) + RMS_EPS)
    return (y * g.astype(jnp.float32)).astype(x.dtype)


def _rope_tables(seq_len):
    inv = 1.0 / (ROPE_THETA ** (jnp.arange(0, MLA_ROPE, 2, dtype=jnp.float32) / MLA_ROPE))
    ang = jnp.arange(seq_len, dtype=jnp.float32)[:, None] * inv[None, :]
    return jnp.cos(ang)[:, None, :], jnp.sin(ang)[:, None, :]


def _apply_rope(x, cos, sin):
    xf = x.astype(jnp.float32)
    x1, x2 = jnp.split(xf, 2, axis=-1)
    return jnp.concatenate([x1 * cos - x2 * sin, x2 * cos + x1 * sin], axis=-1).astype(x.dtype)


def _mla(c_q, c_kv, k_rope, q_norm, w_uq, kv_norm, w_ukv, cos, sin):
    b, s, _ = c_q.shape
    q = (_rms_norm(c_q, q_norm) @ w_uq).reshape(b, s, MLA_HEADS, MLA_NOPE + MLA_ROPE)
    q_nope = q[..., :MLA_NOPE]
    q_rope = _apply_rope(q[..., MLA_NOPE:], cos, sin)
    kv = (_rms_norm(c_kv, kv_norm) @ w_ukv).reshape(b, s, MLA_HEADS, MLA_NOPE + MLA_V)
    k_nope, v = kv[..., :MLA_NOPE], kv[..., MLA_NOPE:]
    k_r = _apply_rope(k_rope[:, :, None, :], cos, sin)[:, :, 0]
    scale = (MLA_NOPE + MLA_ROPE) ** -0.5
    n_blk = s // Q_BLOCK

    def block(qs):
        qn, qr = qs
        sc = (jnp.einsum('bqhd,bkhd->bhqk', qn, k_nope)
              + jnp.einsum('bqhr,bkr->bhqk', qr, k_r))
        p = jax.nn.softmax(sc.astype(jnp.float32) * scale, axis=-1).astype(v.dtype)
        return jnp.einsum('bhqk,bkhd->bqhd', p, v)

    to_blocks = lambda t: jnp.moveaxis(t.reshape(b, n_blk, Q_BLOCK, *t.shape[2:]), 1, 0)
    o = lax.map(block, (to_blocks(q_nope), to_blocks(q_rope)))
    return jnp.moveaxis(o, 0, 1).reshape(b, s, MLA_WIDTH)


def _gla_direction(q, k, v, log_a):
    b, h, s, dk = q.shape
    dv = v.shape[-1]
    n = s // GLA_CHUNK
    rs = lambda t: t.reshape(b, h, n, GLA_CHUNK, t.shape[-1])
    q, k, v, log_a = rs(q), rs(k), rs(v), rs(log_a)
    cum = jnp.cumsum(log_a, axis=3)
    last = cum[:, :, :, -1:, :]
    q_dec = q * jnp.exp(cum)
    k_intra = k * jnp.exp(-cum)
    k_to_end = k * jnp.exp(last - cum)
    mask = jnp.tril(jnp.ones((GLA_CHUNK, GLA_CHUNK), dtype=bool))
    attn = jnp.where(mask, jnp.einsum('bhncd,bhnjd->bhncj', q_dec, k_intra), 0.0)
    o_intra = jnp.einsum('bhncj,bhnjv->bhncv', attn, v)
    chunk_decay = jnp.exp(last[:, :, :, 0, :])

    def step(state, inp):
        q_c, k_c, v_c, dec_c = inp
        out = jnp.einsum('bhcd,bhdv->bhcv', q_c, state)
        new = dec_c[..., None] * state + jnp.einsum('bhcd,bhcv->bhdv', k_c, v_c)
        return new, out

    init = jnp.zeros((b, h, dk, dv), jnp.float32)
    mv = lambda t: jnp.moveaxis(t, 2, 0)
    _, o_inter = lax.scan(step, init, (mv(q_dec), mv(k_to_end), mv(v), mv(chunk_decay)))
    o = o_intra + jnp.moveaxis(o_inter, 0, 2)
    return o.reshape(b, h, s, dv)


def _gla(gq, gk, gv, gr, g_lr, w2, gb, norm_g):
    b, s, _ = gq.shape
    heads = lambda t, d: jnp.moveaxis(t.reshape(b, s, GLA_HEADS, d), 2, 1).astype(jnp.float32)
    q = heads(gq, GLA_DK) * (GLA_DK ** -0.5)
    k = heads(gk, GLA_DK)
    v = heads(gv, GLA_DV)
    lr = g_lr.reshape(b, s, 2, GLA_GATE_RANK)
    gate_pre = jnp.einsum('bstr,trk->tbsk', lr, w2) + gb[:, None, None, :]
    log_a = jax.nn.log_sigmoid(gate_pre.astype(jnp.float32)) / GLA_GATE_NORM
    flip = lambda t: jnp.flip(t, axis=2)
    fwd = _gla_direction(q, k, v, heads(log_a[0], GLA_DK))
    bwd = flip(_gla_direction(flip(q), flip(k), flip(v), flip(heads(log_a[1], GLA_DK))))
    o = _rms_norm(jnp.moveaxis(fwd + bwd, 1, 2), norm_g)
    o = o.reshape(b, s, GLA_VAL) * jax.nn.silu(gr.astype(jnp.float32))
    return o.astype(gq.dtype)


def _mixer(h, w_in, b_merge, q_norm, w_uq, kv_norm, w_ukv, gate_w2, gate_b, gla_norm,
           w_br_mla, w_br_gla, w_out, cos, sin):
    b, s, _ = h.shape
    offs = np.cumsum(IN_SIZES)[:-1].tolist()
    c_q, c_kv, k_rope, gq, gk, gv, gr, g_lr, g_merge = jnp.split(h @ w_in, offs, axis=-1)
    o_mla = _mla(c_q, c_kv, k_rope, q_norm, w_uq, kv_norm, w_ukv, cos, sin)
    o_gla = _gla(gq, gk, gv, gr, g_lr, gate_w2, gate_b, gla_norm)
    gates = jax.nn.sigmoid((g_merge + b_merge).astype(jnp.float32)).astype(h.dtype).reshape(b, s, 2, D_MODEL)
    merged = gates[:, :, 0] * (o_mla @ w_br_mla) + gates[:, :, 1] * (o_gla @ w_br_gla)
    return merged @ w_out


def _cross_attn(h, mem, wq, wkv, wo):
    b, s, _ = h.shape
    q = (h @ wq).reshape(b, s, XA_HEADS, XA_HEAD_DIM)
    kv = (mem @ wkv).reshape(b, mem.shape[1], 2, XA_HEADS, XA_HEAD_DIM)
    k, v = kv[:, :, 0], kv[:, :, 1]
    sc = jnp.einsum('bqhd,bmhd->bhqm', q, k).astype(jnp.float32) * (XA_HEAD_DIM ** -0.5)
    p = jax.nn.softmax(sc, axis=-1).astype(h.dtype)
    o = jnp.einsum('bhqm,bmhd->bqhd', p, v).reshape(b, s, D_MODEL)
    return o @ wo


def _encode(x, mem, ln_in_g, ln_in_b, w_in, b_merge, mla_q_norm, w_uq, mla_kv_norm, w_ukv,
            gla_gate_w2, gla_gate_b, gla_norm, w_branch_mla, w_branch_gla, w_mix_out,
            ln1_g, ln1_b, xa_wq, xa_wkv, xa_wo, ln2_g, ln2_b, mlp_w1, mlp_w2, ln3_g, ln3_b):
    cos, sin = _rope_tables(x.shape[1])
    h = _layer_norm(x, ln_in_g, ln_in_b)
    for l in range(DEPTH):
        mix = _mixer(h, w_in[l], b_merge[l], mla_q_norm[l], w_uq[l], mla_kv_norm[l], w_ukv[l],
                     gla_gate_w2[l], gla_gate_b[l], gla_norm[l], w_branch_mla[l], w_branch_gla[l],
                     w_mix_out[l], cos, sin)
        h = _layer_norm(DN_ALPHA * h + mix, ln1_g[l], ln1_b[l])
        h = _layer_norm(DN_ALPHA * h + _cross_attn(h, mem, xa_wq[l], xa_wkv[l], xa_wo[l]), ln2_g[l], ln2_b[l])
        ff = jnp.square(jax.nn.relu(h @ mlp_w1[l])) @ mlp_w2[l]
        h = _layer_norm(DN_ALPHA * h + ff, ln3_g[l], ln3_b[l])
    return h


def _w(k, shape, fan_in, scale=1.0):
    return jax.random.normal(k, shape, jnp.float32) * (scale * fan_in ** -0.5)


def setup_inputs(seed: int = 0) -> dict:
    key = jax.random.key(seed)
    ks = iter(jax.random.split(key, 64))
    gain = lambda shape: 1.0 + 0.02 * jax.random.normal(next(ks), shape, jnp.float32)
    bias = lambda shape: 0.02 * jax.random.normal(next(ks), shape, jnp.float32)
    L = DEPTH
    x_prompt = jax.random.normal(next(ks), (BATCH, SEQ, D_MODEL), jnp.float32)
    x_sample = jax.random.normal(next(ks), (DEC_BATCH, DEC_SEQ, D_MODEL), jnp.float32)
    mem_prompt = jax.random.normal(next(ks), (BATCH, N_MEM, D_MODEL), jnp.float32)
    mem_sample = jax.random.normal(next(ks), (DEC_BATCH, N_MEM, D_MODEL), jnp.float32)
    ln_in_g = gain((D_MODEL,))
    ln_in_b = bias((D_MODEL,))
    w_in = jnp.concatenate(
        [_w(next(ks), (L, D_MODEL, n), D_MODEL, DN_BETA if i == IN_VALUE_BLOCK else 1.0)
         for i, n in enumerate(IN_SIZES)], axis=-1)
    b_merge = bias((L, 2 * D_MODEL))
    mla_q_norm = gain((L, Q_LORA))
    w_uq = _w(next(ks), (L, Q_LORA, MLA_HEADS * (MLA_NOPE + MLA_ROPE)), Q_LORA)
    mla_kv_norm = gain((L, KV_LORA))
    w_uk = _w(next(ks), (L, KV_LORA, MLA_HEADS, MLA_NOPE), KV_LORA)
    w_uv = _w(next(ks), (L, KV_LORA, MLA_HEADS, MLA_V), KV_LORA, DN_BETA)
    w_ukv = jnp.concatenate([w_uk, w_uv], axis=-1).reshape(L, KV_LORA, MLA_HEADS * (MLA_NOPE + MLA_V))
    gla_gate_w2 = _w(next(ks), (L, 2, GLA_GATE_RANK, GLA_KEY), GLA_GATE_RANK)
    gla_gate_b = bias((L, 2, GLA_KEY))
    gla_norm = gain((L, GLA_DV))
    w_branch_mla = _w(next(ks), (L, MLA_WIDTH, D_MODEL), MLA_WIDTH)
    w_branch_gla = _w(next(ks), (L, GLA_VAL, D_MODEL), GLA_VAL)
    w_mix_out = _w(next(ks), (L, D_MODEL, D_MODEL), D_MODEL, DN_BETA)
    ln1_g = gain((L, D_MODEL))
    ln1_b = bias((L, D_MODEL))
    xa_wq = _w(next(ks), (L, D_MODEL, D_MODEL), D_MODEL)
    xa_wkv = jnp.concatenate([_w(next(ks), (L, D_MODEL, D_MODEL), D_MODEL),
                              _w(next(ks), (L, D_MODEL, D_MODEL), D_MODEL, DN_BETA)], axis=-1)
    xa_wo = _w(next(ks), (L, D_MODEL, D_MODEL), D_MODEL, DN_BETA)
    ln2_g = gain((L, D_MODEL))
    ln2_b = bias((L, D_MODEL))
    mlp_w1 = _w(next(ks), (L, D_MODEL, D_FF), D_MODEL)
    mlp_w2 = _w(next(ks), (L, D_FF, D_MODEL), D_FF, DN_BETA)
    ln3_g = gain((L, D_MODEL))
    ln3_b = bias((L, D_MODEL))
    return {"x_prompt": x_prompt, "x_sample": x_sample, "mem_prompt": mem_prompt, "mem_sample": mem_sample,
            "ln_in_g": ln_in_g, "ln_in_b": ln_in_b, "w_in": w_in, "b_merge": b_merge,
            "mla_q_norm": mla_q_norm, "w_uq": w_uq, "mla_kv_norm": mla_kv_norm, "w_ukv": w_ukv,
            "gla_gate_w2": gla_gate_w2, "gla_gate_b": gla_gate_b, "gla_norm": gla_norm,
            "w_branch_mla": w_branch_mla, "w_branch_gla": w_branch_gla, "w_mix_out": w_mix_out,
            "ln1_g": ln1_g, "ln1_b": ln1_b, "xa_wq": xa_wq, "xa_wkv": xa_wkv, "xa_wo": xa_wo,
            "ln2_g": ln2_g, "ln2_b": ln2_b, "mlp_w1": mlp_w1, "mlp_w2": mlp_w2,
            "ln3_g": ln3_g, "ln3_b": ln3_b}


def reference(x_prompt, x_sample, mem_prompt, mem_sample, ln_in_g, ln_in_b, w_in, b_merge,
              mla_q_norm, w_uq, mla_kv_norm, w_ukv, gla_gate_w2, gla_gate_b, gla_norm,
              w_branch_mla, w_branch_gla, w_mix_out, ln1_g, ln1_b, xa_wq, xa_wkv, xa_wo,
              ln2_g, ln2_b, mlp_w1, mlp_w2, ln3_g, ln3_b):
    params = (ln_in_g, ln_in_b, w_in, b_merge, mla_q_norm, w_uq, mla_kv_norm, w_ukv,
              gla_gate_w2, gla_gate_b, gla_norm, w_branch_mla, w_branch_gla, w_mix_out,
              ln1_g, ln1_b, xa_wq, xa_wkv, xa_wo, ln2_g, ln2_b, mlp_w1, mlp_w2, ln3_g, ln3_b)
    y_prompt = _encode(x_prompt, mem_prompt, *params)
    y_sample = _encode(x_sample, mem_sample, *params)
    return (y_prompt, y_sample)
```

```python
import numpy as np
from contextlib import ExitStack
import concourse.bass as bass
import concourse.mybir as mybir
from concourse.bass_utils import run_bass_kernel_spmd

F32 = mybir.dt.float32
BF16 = mybir.dt.bfloat16
AF = mybir.ActivationFunctionType
ALU = mybir.AluOpType

D = 4096
T = 2048
TT = 4096
NMEM = 256
DFF = 16384
LN_EPS = 1e-5
RMS_EPS = 1e-6
ALPHA = 2.0 ** 0.25
O_CQ, O_CKV, O_KR, O_GQ, O_GK, O_GV, O_GR, O_LR, O_MG = 0, 1024, 1536, 1600, 2624, 3648, 5696, 7744, 7776
N_IN = 15968

KN = {}
ENGS = ("pe", "act", "dve", "pool", "sp")


def flat(ws):
    if ws is None:
        return
    if isinstance(ws, tuple) and len(ws) == 2 and isinstance(ws[0], Sem):
        yield ws
        return
    for w in ws:
        yield from flat(w)


def dedupe(waits):
    d = {}
    for (s, v) in flat(waits):
        if v > 0 and (s.name not in d or d[s.name][1] < v):
            d[s.name] = (s.h, v)
    return list(d.values())


class Sem:
    def __init__(self, h, name):
        self.h = h
        self.v = 0
        self.name = name


class Builder:
    def __init__(self, nc, es):
        self.nc = nc
        self.es = es
        self.q = {k: [] for k in ENGS}
        self.phase_mode = False
        self.pool = []
        self.phase_sems = []
        self.bar = self.sem("bar")
        self.nbar = 0

    def sem(self, name):
        if self.phase_mode and self.pool:
            sm = self.pool.pop()
        else:
            sm = Sem(self.es.enter_context(self.nc.semaphore(name)), name)
        if self.phase_mode:
            self.phase_sems.append(sm)
        return sm

    def recycle(self):
        self.pool.extend(self.phase_sems)
        self.phase_sems = []

    def op(self, eng, fn, waits=(), inc=None, amt=1):
        ws = dedupe(waits)
        tick = None
        h = None
        if inc is not None:
            inc.v += amt
            tick = (inc, inc.v)
            h = inc.h

        def run(e, fn=fn, ws=ws, h=h, amt=amt):
            for sh, v in ws:
                e.wait_ge(sh, v)
            ins = fn(e)
            if h is not None:
                ins.then_inc(h, amt)
        self.q[eng].append(run)
        return tick

    def dma(self, eng, out, in_, waits=(), inc=None):
        return self.op(eng, lambda e: e.dma_start(out=out, in_=in_), waits=waits, inc=inc, amt=16)

    def wait(self, eng, waits):
        ws = dedupe(waits)

        def run(e, ws=ws):
            for sh, v in ws:
                e.wait_ge(sh, v)
        self.q[eng].append(run)

    def maybe_flush(self, limit=5000):
        if max(len(v) for v in self.q.values()) >= limit:
            self.flush("auto")

    def flush(self, name):
        nc = self.nc
        self.nbar += 1
        target = 5 * self.nbar
        bar = self.bar.h
        q = self.q
        self.q = {k: [] for k in ENGS}
        self.bar.v = target

        def body(lst):
            def f(e):
                for r in lst:
                    r(e)
                e.drain().then_inc(bar, 1)
                e.wait_ge(bar, target)
            return f
        with nc.Block() as blk:
            blk.tensor(body(q["pe"]))
            blk.scalar(body(q["act"]))
            blk.vector(body(q["dve"]))
            blk.gpsimd(body(q["pool"]))
            blk.sync(body(q["sp"]))


def kc_view(ap2d):
    return ap2d.rearrange("(kc p) n -> p kc n", p=128)


class Ctx:
    pass


def gemm(b, cx, name, A, K, tok0, ntok, segs, orient, epi, TM=1024):
    KC = K // 128
    assert KC <= 32
    ncol = sum(s[2] for s in segs)
    GW = 256
    ngrp = (ncol + GW - 1) // GW
    ntt = ntok // TM
    tsw = min(512, TM)
    nts = TM // tsw
    Akc = kc_view(A)
    at = cx.at
    for tt in range(ntt):
        t0 = tok0 + tt * TM
        at_ticks = []
        step = 8
        for k0 in range(0, KC, step):
            k1 = min(KC, k0 + step)
            tk = b.dma("sp", at[:, k0:k1, 0:TM], Akc[:, k0:k1, t0:t0 + TM], waits=[cx.at_free], inc=cx.at_sem)
            at_ticks.append((k0, k1, tk))
        for g in range(ngrp):
            c0 = g * GW
            c1 = min(ncol, c0 + GW)
            u = cx.wu
            cx.wu += 1
            slot = u % cx.WR
            wt = cx.wt[slot]
            wsem = cx.wsem[slot]
            wfree = cx.wfree[slot]
            off = 0
            pos = 0
            wtick = None
            for (W, sc0, sn) in segs:
                lo = max(c0, pos)
                hi = min(c1, pos + sn)
                if lo < hi:
                    Wk = kc_view(W)
                    src0 = sc0 + (lo - pos)
                    ln = hi - lo
                    kstep = 16 if ln * 4 >= 512 else 32
                    for k0 in range(0, KC, kstep):
                        k1 = min(KC, k0 + kstep)
                        wtick = b.dma("pool", wt[:, k0:k1, off:off + ln], Wk[:, k0:k1, src0:src0 + ln],
                                      waits=[wfree], inc=wsem)
                    off += ln
                pos += sn
            grp = cx.pu % 2
            cx.pu += 1
            banks = cx.psum[grp * 4:(grp + 1) * 4]
            pfree = cx.pfree[grp]
            chunks = []
            first = True
            ngc = (c1 - c0 + 127) // 128
            last_tick = None
            if orient == "fm":
                for ci in range(ngc):
                    cw = min(128, c1 - c0 - ci * 128)
                    for ts in range(nts):
                        ps = banks[ci * nts + ts]
                        for kc in range(KC):
                            waits = []
                            if first:
                                waits = [wtick, pfree] + [t for (_, _, t) in at_ticks]
                                first = False
                            is_last = (ci == ngc - 1 and ts == nts - 1 and kc == KC - 1)
                            last_tick = b.op(
                                "pe",
                                lambda e, ps=ps, wt=wt, kc=kc, ci=ci, cw=cw, ts=ts: e.matmul(
                                    ps[0:cw, 0:tsw], lhsT=wt[:, kc, ci * 128:ci * 128 + cw],
                                    rhs=at[:, kc, ts * tsw:(ts + 1) * tsw], start=(kc == 0), stop=(kc == KC - 1)),
                                waits=waits, inc=(cx.pe_sem if is_last else None))
                        chunks.append((ps, c0 + ci * 128, cw, t0 + ts * tsw, tsw))
            else:
                gw = c1 - c0
                ntc = TM // 128
                for tc in range(ntc):
                    ps = banks[tc // 2]
                    hh = (tc % 2) * 256
                    for kc in range(KC):
                        waits = []
                        if first:
                            waits = [wtick, pfree] + [t for (_, _, t) in at_ticks]
                            first = False
                        is_last = (tc == ntc - 1 and kc == KC - 1)
                        last_tick = b.op(
                            "pe",
                            lambda e, ps=ps, wt=wt, kc=kc, tc=tc, hh=hh, gw=gw: e.matmul(
                                ps[:, hh:hh + gw], lhsT=at[:, kc, tc * 128:(tc + 1) * 128],
                                rhs=wt[:, kc, 0:gw], start=(kc == 0), stop=(kc == KC - 1)),
                            waits=waits, inc=(cx.pe_sem if is_last else None))
                    chunks.append((ps[:, hh:hh + gw], c0, gw, t0 + tc * 128, 128))
            cx.wfree[slot] = last_tick
            at_ticks_done = last_tick
            b.maybe_flush()
            cx.pfree[grp] = epi(last_tick, chunks)
        cx.at_free = at_ticks_done


def build_program(dbg=None, stop_after=None, p0_tiles=None):
    nc = bass.Bass("TRN2", target_bir_lowering=False)
    dt = lambda n, s, d, k="ExternalInput": nc.dram_tensor(n, s, d, kind=k).ap()
    I = {}
    I["x"] = dt("x", [TT, D], F32)
    I["mem"] = dt("mem", [NMEM, D], F32)
    I["ropeC"] = dt("ropeC", [128, TT], F32)
    I["ropeS"] = dt("ropeS", [128, TT], F32)
    I["cvec"] = dt("cvec", [128, 4], F32)
    I["ident"] = dt("ident", [128, 128], F32)
    I["w2b"] = dt("w2b", [2, 17, 1024], F32)
    I["M1"] = dt("M1", [2, 128, 130], F32)
    I["M2"] = dt("M2", [2, 128, 128], F32)
    I["maskT"] = dt("maskT", [2, 64, 256], F32)
    I["gnorm_bc"] = dt("gnorm_bc", [64, 2048], F32)
    I["w_in"] = dt("w_in", [D, N_IN], F32)
    I["w_uq"] = dt("w_uq", [1024, 3072], F32)
    I["w_ukv"] = dt("w_ukv", [512, 4096], F32)
    I["w_br_mla"] = dt("w_br_mla", [2048, D], F32)
    I["w_br_gla"] = dt("w_br_gla", [2048, D], F32)
    I["w_mix_out"] = dt("w_mix_out", [D, D], F32)
    I["xa_wq"] = dt("xa_wq", [D, D], F32)
    I["xa_wkv"] = dt("xa_wkv", [D, 2 * D], F32)
    I["xa_wo"] = dt("xa_wo", [D, D], F32)
    I["mlp_w1"] = dt("mlp_w1", [D, DFF], F32)
    I["mlp_w2"] = dt("mlp_w2", [DFF, D], F32)
    for nm, n in (("ln_in_g", D), ("ln_in_b", D), ("ln1_g", D), ("ln1_b", D), ("ln2_g", D), ("ln2_b", D),
                  ("ln3_g", D), ("ln3_b", D), ("b_merge", 2 * D), ("q_norm", 1024), ("kv_norm", 512)):
        I[nm] = dt(nm, [128, n // 128], F32)
    y = dt("y", [T, D], F32, "ExternalOutput")

    S = {}

    def scratch(n, s, d):
        kind = "ExternalOutput" if (dbg and n in dbg) else "Internal"
        S[n] = nc.dram_tensor(n, s, d, kind=kind).ap()
        return S[n]
    scratch("hT_bf", [D, TT], BF16)
    scratch("hT_f32", [D, T], F32)
    scratch("cqT", [1024, T], F32)
    scratch("gqT", [1024, T], F32)
    scratch("ckvT", [512, TT], F32)
    scratch("krT", [64, TT], F32)
    scratch("gkT", [1024, T], F32)
    scratch("glrT", [64, TT], F32)
    scratch("gk", [TT, 1024], F32)
    scratch("gv", [TT, 2048], BF16)
    scratch("gr", [T, 2048], F32)

    with ExitStack() as es:
        b = Builder(nc, es)
        cx = Ctx()
        cx.WR = 3
        cx.wt = [es.enter_context(nc.sbuf_tensor(f"wt{i}", [128, 32, 256], BF16)) for i in range(cx.WR)]
        cx.wsem = [b.sem(f"wsem{i}") for i in range(cx.WR)]
        cx.wfree = [None] * cx.WR
        cx.wu = 0
        cx.pu = 0
        cx.psum = [es.enter_context(nc.psum_tensor(f"ps{i}", [128, 512], F32)) for i in range(8)]
        cx.psbf = None
        cx.pfree = [None, None]
        cx.pe_sem = b.sem("pe_sem")
        cx.at_sem = b.sem("at_sem")
        cx.at_free = None
        ident = es.enter_context(nc.sbuf_tensor("sb_ident", [128, 128], F32))
        cvec = es.enter_context(nc.sbuf_tensor("sb_cvec", [128, 4], F32))
        eps_ln = es.enter_context(nc.sbuf_tensor("eps_ln", [128, 1], F32))
        csem = b.sem("csem")
        tk_id = b.dma("sp", ident[:, :], I["ident"][:, :], inc=b.sem("c_id"))
        tk_cv = b.dma("sp", cvec[:, :], I["cvec"][:, :], inc=b.sem("c_cv"))
        sA = b.sem("sA")
        sD = b.sem("sD")
        sP = b.sem("sP")
        sT = b.sem("sT")
        tk_eps = b.op("dve", lambda e: e.memset(eps_ln[:, :], LN_EPS), inc=sD)
        b.phase_mode = True

        with ExitStack() as ps_:
            sb = lambda n, s, d: ps_.enter_context(nc.sbuf_tensor(n, s, d))
            xt = [sb(f"p0_xt{i}", [128, D], F32) for i in range(2)]
            xn = [sb(f"p0_xn{i}", [128, D], F32) for i in range(2)]
            hf = [sb(f"p0_hf{i}", [128, 32, 128], F32) for i in range(2)]
            hb = [sb(f"p0_hb{i}", [128, 32, 128], BF16) for i in range(2)]
            gcol = sb("p0_g", [128, 32], F32)
            bcol = sb("p0_b", [128, 32], F32)
            stats = sb("p0_stats", [128, 8, 6], F32)
            mv = sb("p0_mv", [128, 2], F32)
            sd = sb("p0_sd", [128, 1], F32)
            rstd = sb("p0_rstd", [128, 1], F32)
            xsem = [b.sem(f"p0_xsem{i}") for i in range(2)]
            stsem = [b.sem(f"p0_stsem{i}") for i in range(2)]
            tk_g = b.dma("sp", gcol[:, :], I["ln_in_g"][:, :], inc=b.sem("c_g0"))
            tk_b = b.dma("sp", bcol[:, :], I["ln_in_b"][:, :], inc=b.sem("c_b0"))
            xt_free = [None, None]
            xn_free = [None, None]
            st_free = [None, None]
            ev_free = [None, None]
            hT_bf_v = kc_view(S["hT_bf"])
            hT_f_v = kc_view(S["hT_f32"])
            NTILE = p0_tiles or (TT // 128)
            for i in range(NTILE):
                s2 = i % 2
                tkx = b.dma("sp", xt[s2][:, :], I["x"][i * 128:(i + 1) * 128, :], waits=[xt_free[s2]], inc=xsem[s2])
                for c in range(8):
                    tks = b.op("dve", lambda e, c=c, s2=s2: e.bn_stats(out=stats[:, c, :], in_=xt[s2][:, c * 512:(c + 1) * 512]),
                               waits=[tkx] if c == 0 else [], inc=sD if c == 7 else None)
                tka = b.op("dve", lambda e: e.bn_aggr(out=mv[:, :], in_=stats[:, :, :]), waits=[tks], inc=sD)
                tksd = b.op("act", lambda e: e.activation(out=sd[:, :], in_=mv[:, 1:2], func=AF.Sqrt, bias=eps_ln[:, :], scale=1.0),
                            waits=[tka, tk_eps], inc=sA)
                tkr = b.op("dve", lambda e: e.reciprocal(out=rstd[:, :], in_=sd[:, :]), waits=[tksd], inc=sD)
                tkxn = b.op("dve", lambda e, s2=s2: e.tensor_scalar(out=xn[s2][:, :], in0=xt[s2][:, :], scalar1=mv[:, 0:1],
                                                                     scalar2=rstd[:, :], op0=ALU.subtract, op1=ALU.mult),
                            waits=[tkr, xn_free[s2]], inc=sD)
                xt_free[s2] = tkxn
                evt = None
                for hh in range(2):
                    banks = cx.psum[hh * 4:(hh + 1) * 4]
                    for j in range(16):
                        fc = hh * 16 + j
                        w = []
                        if j == 0:
                            w = [tkxn, ev_free[hh], tk_id]
                        tkp = b.op("pe", lambda e, fc=fc, j=j, banks=banks, s2=s2: e.transpose(
                            out=banks[j // 4][:, (j % 4) * 128:(j % 4 + 1) * 128],
                            in_=xn[s2][:, fc * 128:(fc + 1) * 128], identity=ident[:, :]),
                            waits=w, inc=cx.pe_sem if j == 15 else None)
                    for j in range(16):
                        fc = hh * 16 + j
                        w = []
                        if j == 0:
                            w = [tkp, tk_g, tk_b]
                            if hh == 0:
                                w.append(st_free[s2])
                        evt = b.op("act", lambda e, fc=fc, j=j, banks=banks, s2=s2: e.activation(
                            out=hf[s2][:, fc, :], in_=banks[j // 4][:, (j % 4) * 128:(j % 4 + 1) * 128],
                            func=AF.Identity, bias=bcol[:, fc:fc + 1], scale=gcol[:, fc:fc + 1]),
                            waits=w, inc=sA if j == 15 else None)
                    ev_free[hh] = evt
                xn_free[s2] = tkp
                tkc = b.op("pool", lambda e, s2=s2: e.tensor_copy(out=hb[s2][:, :, :], in_=hf[s2][:, :, :]), waits=[evt], inc=sP)
                tst = b.dma("sp", hT_bf_v[:, :, i * 128:(i + 1) * 128], hb[s2][:, :, :], waits=[tkc], inc=stsem[s2])
                if i < T // 128:
                    tst = b.dma("sp", hT_f_v[:, :, i * 128:(i + 1) * 128], hf[s2][:, :, :], waits=[tkc], inc=stsem[s2])
                st_free[s2] = tst
            b.wait("sp", [st_free[0], st_free[1]])
            cx.pfree = [ev_free[0], ev_free[1]]
            b.flush("p0")
        b.recycle()
        if stop_after == "p0":
            return nc

        Win = I["w_in"]
        scratch("ckvnT", [512, TT], BF16)
        scratch("cqnT", [1024, T], BF16)
        scratch("kropeT", [64, TT], BF16)
        scratch("qnT", [2048, T], BF16)
        scratch("qrT", [1024, T], BF16)
        scratch("knT", [2048, TT], BF16)
        scratch("vmla", [TT, 2048], BF16)
        scratch("omlaT", [2048, T], BF16)
        scratch("oglaT", [2048, T], BF16)
        scratch("gateT", [2 * D, T], F32)
        scratch("bmT", [D, T], F32)
        scratch("bgT", [D, T], F32)
        scratch("mergedT", [D, T], BF16)
        scratch("mixT", [D, T], F32)
        scratch("h1T_f32", [D, T], F32)
        scratch("h1T_bf", [D, T], BF16)
        scratch("qxT", [D, T], BF16)
        scratch("memT", [D, NMEM], BF16)
        scratch("kxT", [D, NMEM], BF16)
        scratch("vx", [NMEM, D], BF16)
        scratch("oxT", [D, T], BF16)
        scratch("aoT", [D, T], F32)
        scratch("h2T_f32", [D, T], F32)
        scratch("h2T_bf", [D, T], BF16)
        scratch("ffT", [DFF, T], BF16)
        for q_ in range(4):
            scratch(f"mlpP{q_}", [D, T], F32)

        class GP:
            def __init__(self, tag, extra=None):
                self.stack = ExitStack()
                self.tag = tag
                sbp = lambda n, s_, d: self.stack.enter_context(nc.sbuf_tensor(f"{tag}_{n}", s_, d))
                self.sb = sbp
                cx.at = sbp("at", [128, 32, 1024], BF16)
                cx.at_free = None
                self.stg = [sbp(f"stg{i}", [128, 2048], F32) for i in range(2)]
                self.stb = [sbp(f"stb{i}", [128, 2048], BF16) for i in range(2)]
                self.stg_sem = [b.sem(f"{tag}_stgsem{i}") for i in range(2)]
                self.stg_free = [None, None]
                self.n = 0
                self.ldb = {}
                self.ld_free = {}
                self.ld_sem = {}

            def ld(self, which, si, ci, src_ap, cw, nt):
                key = (which, si)
                if key not in self.ldb:
                    self.ldb[key] = self.sb(f"ld{which}{si}", [128, 2048], F32)
                    self.ld_sem[key] = [b.sem(f"{self.tag}_ld{which}{si}_{c}") for c in range(4)]
                    self.ld_free[key] = [None] * 4
                dst = self.ldb[key][0:cw, ci * 512:ci * 512 + nt]
                tk = b.dma("sp", dst, src_ap, waits=[self.ld_free[key][ci]], inc=self.ld_sem[key][ci])
                return dst, tk

            def ld_done(self, which, si, ci, tk):
                self.ld_free[(which, si)][ci] = tk

            def epi(self, orient, dests, dtype=F32, fn=None):
                me = self

                def epi(pe_tick, chunks):
                    si = me.n % 2
                    me.n += 1
                    st, stb = me.stg[si], me.stb[si]
                    ev = []
                    outs = []
                    for ci, (ps, vc0, cw, t0, nt) in enumerate(chunks):
                        eng = ("act" if ci % 2 == 0 else "dve") if orient == "fm" else ("act" if (ci // 2) % 2 == 0 else "dve")
                        w = [pe_tick, me.stg_free[si]]
                        buf = stb if dtype == BF16 else st
                        if orient == "fm":
                            o = buf[0:cw, ci * 512:ci * 512 + nt]
                            tmp = st[0:cw, ci * 512:ci * 512 + nt]
                            src = ps[0:cw, 0:nt]
                        else:
                            o = buf[:, ci * 256:ci * 256 + cw]
                            tmp = st[:, ci * 256:ci * 256 + cw]
                            src = ps
                        if fn is not None:
                            tk = fn(eng, o, src, tmp, (vc0, cw, t0, nt), w, si, ci)
                        elif eng == "act":
                            tk = b.op("act", lambda e, o=o, src=src: e.activation(out=o, in_=src, func=AF.Copy), waits=w, inc=sA)
                        else:
                            tk = b.op("dve", lambda e, o=o, src=src: e.tensor_copy(out=o, in_=src), waits=w, inc=sD)
                        ev.append(tk)
                        outs.append(o)
                    tst = None
                    for ci, (ps, vc0, cw, t0, nt) in enumerate(chunks):
                        if orient == "fm":
                            for (dv0, dn, dap, drow0) in dests:
                                lo = max(vc0, dv0)
                                hi = min(vc0 + cw, dv0 + dn)
                                if lo < hi:
                                    buf = stb if dtype == BF16 else st
                                    srcv = buf[lo - vc0:hi - vc0, ci * 512:ci * 512 + nt]
                                    tst = b.dma("sp", dap[drow0 + lo - dv0:drow0 + hi - dv0, t0:t0 + nt], srcv,
                                                waits=ev, inc=me.stg_sem[si])
                        else:
                            dap, cb = dests
                            tst = b.dma("sp", dap[t0:t0 + nt, vc0 - cb:vc0 - cb + cw], outs[ci], waits=ev, inc=me.stg_sem[si])
                    me.stg_free[si] = tst
                    return ev
                return epi

            def close(self, name):
                b.wait("sp", [self.stg_free[0], self.stg_free[1]])
                b.flush(name)
                self.stack.close()

        gp = GP("p1")
        if KN.get("only_tm"):
            gemm(b, cx, "in_tm_gk", S["hT_bf"], D, 0, KN.get("tm_tok", TT), [(Win, O_GK, KN.get("tm_cols", 1024))], "tm", gp.epi("tm", (S["gk"], 0)))
            gp.close("p1")
            return nc
        gemm(b, cx, "in_fm_own", S["hT_bf"], D, 0, KN.get("p1_tok", T),
             [(Win, O_CQ, 1024), (Win, O_GQ, 1024), (Win, O_GK, 1024)], "fm",
             gp.epi("fm", [(0, 1024, S["cqT"], 0), (1024, 1024, S["gqT"], 0), (2048, 1024, S["gkT"], 0)]))
        if stop_after == "p1a":
            gp.close("p1")
            return nc
        gemm(b, cx, "in_fm_all", S["hT_bf"], D, 0, TT,
             [(Win, O_CKV, 512), (Win, O_KR, 64), (Win, O_LR, 32)], "fm",
             gp.epi("fm", [(0, 512, S["ckvT"], 0), (512, 64, S["krT"], 0), (576, 16, S["glrT"], 0), (592, 16, S["glrT"], 32)]))
        if stop_after == "p1b":
            gp.close("p1")
            return nc
        if not KN.get("skip_tm"):
            gemm(b, cx, "in_tm_gk", S["hT_bf"], D, 0, TT, [(Win, O_GK, 1024)], "tm", gp.epi("tm", (S["gk"], 0)))
        if not KN.get("skip_tm") and not KN.get("skip_tm2"):
            gemm(b, cx, "in_tm_gv", S["hT_bf"], D, 0, TT, [(Win, O_GV, 2048)], "tm", gp.epi("tm", (S["gv"], 0), dtype=BF16))
            gemm(b, cx, "in_tm_gr", S["hT_bf"], D, 0, T, [(Win, O_GR, 2048)], "tm", gp.epi("tm", (S["gr"], 0)))
        bmg = gp.sb("bmg", [128, 64], F32)
        tk_bmg = b.dma("sp", bmg[:, :], I["b_merge"][:, :], inc=b.sem("c_bmg"))

        def fn_sig(eng, o, src, tmp, info, w, si, ci):
            fc = info[0] // 128
            return b.op("act", lambda e: e.activation(out=o, in_=src, func=AF.Sigmoid, bias=bmg[:, fc:fc + 1], scale=1.0),
                        waits=w + [tk_bmg], inc=sA)
        gemm(b, cx, "in_gate", S["hT_bf"], D, 0, T, [(Win, O_MG, 2 * D)], "fm",
             gp.epi("fm", [(0, 2 * D, S["gateT"], 0)], fn=fn_sig))
        gp.close("p1")
        b.recycle()
        if stop_after == "p1":
            return nc

        def phase_reset():
            b.recycle()
            cx.pfree = [None, None]
            cx.wfree = [None] * cx.WR
            cx.at_free = None

        def rms_fm(tag, src, F, N, gname, dst):
            FC = F // 128
            with ExitStack() as st_:
                sbp = lambda n, s_, d: st_.enter_context(nc.sbuf_tensor(f"sb{tag}_{n}", s_, d))
                xin = [sbp(f"x{i}", [128, FC, 512], F32) for i in range(2)]
                sq = sbp("sq", [128, FC, 512], BF16)
                ones = sbp("ones", [128, 128], BF16)
                sd = sbp("sd", [128, 512], F32)
                R = sbp("R", [128, 512], F32)
                ob = [sbp(f"o{i}", [128, FC, 512], BF16) for i in range(2)]
                gc = sbp("g", [128, FC], F32)
                eps = sbp("eps", [128, 1], F32)
                xs = [b.sem(f"{tag}_xs{i}") for i in range(2)]
                os_ = [b.sem(f"{tag}_os{i}") for i in range(2)]
                tkg = b.dma("sp", gc[:, :], I[gname][:, :], inc=b.sem(f"{tag}_cg"))
                tk1 = b.op("dve", lambda e: e.memset(ones[:, :], 1.0), inc=sD)
                tk1 = b.op("dve", lambda e: e.memset(eps[:, :], RMS_EPS), inc=sD)
                x_free = [None, None]
                o_free = [None, None]
                sq_free = None
                ps_free = None
                sd_free = None
                srcv, dstv = kc_view(src), kc_view(dst)
                ps = cx.psum[0]
                for i in range(N // 512):
                    s2 = i % 2
                    tkx = b.dma("sp", xin[s2][:, :, :], srcv[:, :, i * 512:(i + 1) * 512], waits=[x_free[s2]], inc=xs[s2])
                    tksq = b.op("act", lambda e, s2=s2: e.activation(out=sq[:, :, :], in_=xin[s2][:, :, :], func=AF.Square),
                                waits=[tkx, sq_free], inc=sA)
                    for fc in range(FC):
                        tkp = b.op("pe", lambda e, fc=fc: e.matmul(ps[:, :], lhsT=ones[:, :], rhs=sq[:, fc, :],
                                                                    start=(fc == 0), stop=(fc == FC - 1)),
                                   waits=[tksq, tk1, ps_free] if fc == 0 else [], inc=cx.pe_sem if fc == FC - 1 else None)
                    sq_free = tkp
                    tksd = b.op("act", lambda e: e.activation(out=sd[:, :], in_=ps[:, :], func=AF.Sqrt, bias=eps[:, :], scale=1.0 / F),
                                waits=[tkp, sd_free], inc=sA)
                    ps_free = tksd
                    tkr = b.op("dve", lambda e: e.reciprocal(out=R[:, :], in_=sd[:, :]), waits=[tksd], inc=sD)
                    sd_free = tkr
                    for fc in range(FC):
                        tko = b.op("dve", lambda e, fc=fc, s2=s2: e.scalar_tensor_tensor(
                            out=ob[s2][:, fc, :], in0=xin[s2][:, fc, :], scalar=gc[:, fc:fc + 1], in1=R[:, :],
                            op0=ALU.mult, op1=ALU.mult),
                            waits=[tkr, tkg, o_free[s2]] if fc == 0 else [], inc=sD if fc == FC - 1 else None)
                    x_free[s2] = tko
                    o_free[s2] = b.dma("sp", dstv[:, :, i * 512:(i + 1) * 512], ob[s2][:, :, :], waits=[tko], inc=os_[s2])
                b.wait("sp", o_free)
                b.flush(tag)
            phase_reset()

        rms_fm("p2q", S["cqT"], 1024, T, "q_norm", S["cqnT"])
        rms_fm("p2kv", S["ckvT"], 512, TT, "kv_norm", S["ckvnT"])
        with ExitStack() as st_:
            sbp = lambda n, s_, d: st_.enter_context(nc.sbuf_tensor(f"p2r_{n}", s_, d))
            kr = [sbp(f"kr{i}", [64, 512], F32) for i in range(2)]
            rc = [sbp(f"rc{i}", [64, 512], F32) for i in range(2)]
            rs = [sbp(f"rs{i}", [64, 512], F32) for i in range(2)]
            pr = sbp("pr", [64, 512], F32)
            t2 = sbp("t2", [64, 512], F32)
            ko = [sbp(f"ko{i}", [64, 512], BF16) for i in range(2)]
            ls = [b.sem(f"p2r_ls{i}") for i in range(2)]
            ss = [b.sem(f"p2r_ss{i}") for i in range(2)]
            l_free = [None, None]
            o_free = [None, None]
            for i in range(TT // 512):
                s2 = i % 2
                sl = slice(i * 512, (i + 1) * 512)
                b.dma("sp", kr[s2][:, :], S["krT"][:, sl], waits=[l_free[s2]], inc=ls[s2])
                b.dma("sp", rc[s2][:, :], I["ropeC"][0:64, sl], waits=[l_free[s2]], inc=ls[s2])
                tkl = b.dma("sp", rs[s2][:, :], I["ropeS"][0:64, sl], waits=[l_free[s2]], inc=ls[s2])
                tk = b.op("dve", lambda e, s2=s2: e.tensor_tensor(out=pr[:, :], in0=kr[s2][:, :], in1=rc[s2][:, :], op=ALU.mult),
                          waits=[tkl], inc=sD)
                tk = b.op("dve", lambda e, s2=s2: e.tensor_copy(out=t2[0:32, :], in_=kr[s2][32:64, :]), waits=[tk], inc=sD)
                tk = b.op("dve", lambda e, s2=s2: e.tensor_copy(out=t2[32:64, :], in_=kr[s2][0:32, :]), waits=[tk], inc=sD)
                tk = b.op("dve", lambda e, s2=s2: e.tensor_tensor(out=t2[:, :], in0=t2[:, :], in1=rs[s2][:, :], op=ALU.mult),
                          waits=[tk], inc=sD)
                l_free[s2] = tk
                tk = b.op("dve", lambda e, s2=s2: e.tensor_tensor(out=ko[s2][:, :], in0=pr[:, :], in1=t2[:, :], op=ALU.add),
                          waits=[tk, o_free[s2]], inc=sD)
                o_free[s2] = b.dma("sp", S["kropeT"][:, sl], ko[s2][:, :], waits=[tk], inc=ss[s2])
            b.wait("sp", o_free)
            b.flush("p2r")
        phase_reset()
        if stop_after == "p2":
            return nc

        gp = GP("p3")
        ropeC = gp.sb("ropeC", [128, T], F32)
        ropeS = gp.sb("ropeS", [128, T], F32)
        rtmp = gp.sb("rtmp", [128, 2048], F32)
        tk_rq = b.dma("sp", ropeC[:, :], I["ropeC"][:, 0:T], inc=b.sem("c_rqc"))
        tk_rq2 = b.dma("sp", ropeS[:, :], I["ropeS"][:, 0:T], inc=b.sem("c_rqs"))
        Wq, Wkv = I["w_uq"], I["w_ukv"]

        def fn_q(eng, o, src, tmp, info, w, si, ci):
            vc0, cw, t0, nt = info
            ci = (t0 % 1024) // 512 + 2 * ((vc0 // 128) % 2)
            tk = None
            tks_ = []
            for r0 in (0, 64):
                if (vc0 + r0) % 192 < 128:
                    tk = b.op("act", lambda e, r0=r0: e.activation(out=o[r0:r0 + 64, :], in_=src[r0:r0 + 64, :], func=AF.Copy),
                              waits=w, inc=sA)
                else:
                    sw = rtmp[:, ci * 512:ci * 512 + nt]
                    tk = b.op("dve", lambda e, r0=r0: e.tensor_tensor(out=tmp[r0:r0 + 64, :], in0=src[r0:r0 + 64, :],
                                                                       in1=ropeC[r0:r0 + 64, t0:t0 + nt], op=ALU.mult),
                              waits=w + [tk_rq, tk_rq2], inc=sD)
                    tk = b.op("dve", lambda e, r0=r0, sw=sw: e.tensor_copy(out=sw[r0:r0 + 32, :], in_=src[r0 + 32:r0 + 64, :]), waits=[tk], inc=sD)
                    tk = b.op("dve", lambda e, r0=r0, sw=sw: e.tensor_copy(out=sw[r0 + 32:r0 + 64, :], in_=src[r0:r0 + 32, :]), waits=[tk], inc=sD)
                    tk = b.op("dve", lambda e, r0=r0, sw=sw: e.tensor_tensor(out=sw[r0:r0 + 64, :], in0=sw[r0:r0 + 64, :],
                                                                              in1=ropeS[r0:r0 + 64, t0:t0 + nt], op=ALU.mult), waits=[tk], inc=sD)
                    tk = b.op("dve", lambda e, r0=r0, sw=sw: e.tensor_tensor(out=o[r0:r0 + 64, :], in0=tmp[r0:r0 + 64, :],
                                                                              in1=sw[r0:r0 + 64, :], op=ALU.add), waits=[tk], inc=sD)
                tks_.append(tk)
            return tks_
        qd = []
        for h in range(16):
            qd += [(h * 192, 128, S["qnT"], h * 128), (h * 192 + 128, 64, S["qrT"], h * 64)]
        gemm(b, cx, "q_up", S["cqnT"], 1024, 0, T, [(Wq, 0, 3072)], "fm", gp.epi("fm", qd, dtype=BF16, fn=fn_q))
        gemm(b, cx, "kn_up", S["ckvnT"], 512, 0, TT, [(Wkv, h * 256, 128) for h in range(16)], "fm",
             gp.epi("fm", [(0, 2048, S["knT"], 0)], dtype=BF16))
        gemm(b, cx, "v_up", S["ckvnT"], 512, 0, TT, [(Wkv, h * 256 + 128, 128) for h in range(16)], "tm",
             gp.epi("tm", (S["vmla"], 0), dtype=BF16))
        gp.close("p3")
        phase_reset()
        if stop_after == "p3":
            return nc

        def attn(tag, nh, qparts, kparts, vsrc, nk, ndv, scale, masked, out_ap):
            NKC = nk // 128
            NP = len(qparts(0))
            with ExitStack() as st_:
                sbp = lambda n, s_, d: st_.enter_context(nc.sbuf_tensor(f"sb{tag}_{n}", s_, d))
                kt = [sbp(f"kt{i}", [128, NP, nk], BF16) for i in range(2)]
                vt = [sbp(f"vt{i}", [128, NKC, ndv * 128], BF16) for i in range(2)]
                qt = [sbp(f"qt{i}", [128, NP, 512], BF16) for i in range(2)]
                pb = [sbp(f"pb{i}", [128, NKC, 512], BF16) for i in range(2)]
                rl = sbp("rl", [128, 512], F32)
                ob = [sbp(f"ob{i}", [128, ndv, 512], BF16) for i in range(2)]
                ones = sbp("ones", [128, 128], BF16)
                zero = sbp("zero", [128, 1], F32)
                tk1 = b.op("dve", lambda e: e.memset(ones[:, :], 1.0), inc=sD)
                tk1 = b.op("dve", lambda e: e.memset(zero[:, :], 0.0), inc=sD)
                ks = [b.sem(f"{tag}_ks{i}") for i in range(2)]
                qs = [b.sem(f"{tag}_qs{i}") for i in range(2)]
                oss = [b.sem(f"{tag}_os{i}") for i in range(2)]
                ps_s = cx.psum[0:3]
                ps_o = cx.psum[3:5]
                ps_l = cx.psum[5:7]
                s_free = [None] * 3
                o_freeP = [None, None]
                l_freeP = [None, None]
                k_free = [None, None]
                q_free = [None, None]
                pb_free = [None, None]
                ob_free = [None, None]
                rl_free = None
                nS = 0
                nO = 0
                nQ = 0
                for h in range(nh):
                    hs = h % 2
                    tkk = None
                    for p, (ap, r0, rows) in enumerate(kparts(h)):
                        tkk = b.dma("sp", kt[hs][0:rows, p, :], ap[r0:r0 + rows, 0:nk], waits=[k_free[hs]], inc=ks[hs])
                    vv = vsrc(h).rearrange("(kc p) n -> p kc n", p=128)
                    for k0 in range(0, NKC, 8):
                        k1 = min(NKC, k0 + 8)
                        tkk = b.dma("sp", vt[hs][:, k0:k1, :], vv[:, k0:k1, :], waits=[k_free[hs]], inc=ks[hs])
                    last_pe = None
                    for qi in range(T // 512):
                        q2 = nQ % 2
                        nQ += 1
                        tkq = None
                        for p, (ap, r0, rows) in enumerate(qparts(h)):
                            tkq = b.dma("sp", qt[q2][0:rows, p, :], ap[r0:r0 + rows, qi * 512:(qi + 1) * 512],
                                        waits=[q_free[q2]], inc=qs[q2])
                        tkp_last = None
                        for kc in range(NKC):
                            sb_ = nS % 3
                            nS += 1
                            for p, (ap, r0, rows) in enumerate(kparts(h)):
                                tks = b.op("pe", lambda e, p=p, rows=rows, kc=kc, sb_=sb_, hs=hs, q2=q2: e.matmul(
                                    ps_s[sb_][:, :], lhsT=kt[hs][0:rows, p, kc * 128:(kc + 1) * 128],
                                    rhs=qt[q2][0:rows, p, :], start=(p == 0), stop=(p == NP - 1)),
                                    waits=[tkk, tkq, s_free[sb_]] if p == 0 else [],
                                    inc=cx.pe_sem if p == NP - 1 else None)
                            bias = cvec[:, 0:1] if (masked and kc >= NKC // 2) else zero[:, :]
                            tkp_last = b.op("act", lambda e, kc=kc, sb_=sb_, q2=q2, bias=bias: e.activation(
                                out=pb[q2][:, kc, :], in_=ps_s[sb_][:, :], func=AF.Exp, bias=bias, scale=scale),
                                waits=[tks, tk1, tk_cv, pb_free[q2]] if kc == 0 else [tks], inc=sA)
                            s_free[sb_] = tkp_last
                        q_free[q2] = tks
                        l2 = nO % 2
                        for kc in range(NKC):
                            tkl = b.op("pe", lambda e, kc=kc, l2=l2, q2=q2: e.matmul(
                                ps_l[l2][:, :], lhsT=ones[:, :], rhs=pb[q2][:, kc, :], start=(kc == 0), stop=(kc == NKC - 1)),
                                waits=[tkp_last, l_freeP[l2]] if kc == 0 else [], inc=cx.pe_sem if kc == NKC - 1 else None)
                        tkrl = b.op("dve", lambda e, l2=l2: e.reciprocal(out=rl[:, :], in_=ps_l[l2][:, :]), waits=[tkl, rl_free], inc=sD)
                        l_freeP[l2] = tkrl
                        o2 = nQ % 2
                        tko = None
                        for dvc in range(ndv):
                            ob2 = nO % 2
                            nO += 1
                            for kc in range(NKC):
                                tkpv = b.op("pe", lambda e, kc=kc, dvc=dvc, ob2=ob2, hs=hs, q2=q2: e.matmul(
                                    ps_o[ob2][:, :], lhsT=vt[hs][:, kc, dvc * 128:(dvc + 1) * 128], rhs=pb[q2][:, kc, :],
                                    start=(kc == 0), stop=(kc == NKC - 1)),
                                    waits=[o_freeP[ob2]] if kc == 0 else [], inc=cx.pe_sem if kc == NKC - 1 else None)
                            tko = b.op("dve", lambda e, dvc=dvc, ob2=ob2, o2=o2: e.tensor_tensor(
                                out=ob[o2][:, dvc, :], in0=ps_o[ob2][:, :], in1=rl[:, :], op=ALU.mult),
                                waits=[tkpv, tkrl, ob_free[o2]] if dvc == 0 else [tkpv], inc=sD)
                            o_freeP[ob2] = tko
                        rl_free = tko
                        pb_free[q2] = tkpv
                        last_pe = tkpv
                        outv = out_ap[h * ndv * 128:(h + 1) * ndv * 128, qi * 512:(qi + 1) * 512].rearrange("(c p) t -> p c t", p=128)
                        ob_free[o2] = b.dma("sp", outv, ob[o2][:, :, :], waits=[tko], inc=oss[o2])
                    k_free[hs] = last_pe
                b.wait("sp", ob_free)
                b.flush(tag)
            phase_reset()

        attn("p4", 16,
             lambda h: [(S["qnT"], h * 128, 128), (S["qrT"], h * 64, 64)],
             lambda h: [(S["knT"], h * 128, 128), (S["kropeT"], 0, 64)],
             lambda h: S["vmla"][:, h * 128:(h + 1) * 128], TT, 1, 192.0 ** -0.5, True, S["omlaT"])
        if stop_after == "p4":
            return nc

        scratch("qdT", [2, 1024, T], BF16)
        scratch("kiT", [2, 1024, T], BF16)
        scratch("kte", [2, TT, 1024], BF16)
        scratch("ofwd", [T, 2048], F32)
        scratch("ogla", [T, 2048], F32)
        with ExitStack() as st5:
            sb5 = lambda n, s_, d: st5.enter_context(nc.sbuf_tensor(f"p5_{n}", s_, d))
            dec = sb5("dec", [128, 2, 8, 64], F32)
            with ExitStack() as st_:
                sbp = lambda n, s_, d: st_.enter_context(nc.sbuf_tensor(f"p5a_{n}", s_, d))
                lrA = [sbp(f"lrA{d_}", [32, TT], F32) for d_ in range(2)]
                w2b = [sbp(f"w2b{d_}", [32, 1024], F32) for d_ in range(2)]
                M1 = [sbp(f"M1{d_}", [128, 130], F32) for d_ in range(2)]
                M2 = [sbp(f"M2{d_}", [128, 128], F32) for d_ in range(2)]
                onec = sbp("onec", [128, 1], F32)
                la = sbp("la", [128, 1024], F32)
                E1 = sbp("E1", [128, 8, 128], F32)
                E2 = sbp("E2", [128, 8, 128], F32)
                E3 = sbp("E3", [128, 1024], F32)
                gqb = [sbp(f"gqb{i}", [128, 8, 128], F32) for i in range(2)]
                gkb = [sbp(f"gkb{i}", [128, 8, 128], F32) for i in range(2)]
                gkt = [sbp(f"gkt{i}", [128, 1024], F32) for i in range(2)]
                qdo = [sbp(f"qdo{i}", [128, 8, 128], BF16) for i in range(2)]
                kio = [sbp(f"kio{i}", [128, 8, 128], BF16) for i in range(2)]
                kto = [sbp(f"kto{i}", [128, 1024], BF16) for i in range(2)]
                lsem = [b.sem(f"p5a_l{i}") for i in range(2)]
                osem = [b.sem(f"p5a_o{i}") for i in range(2)]
                tkc = b.op("dve", lambda e: e.memset(onec[:, :], 1.0), inc=sD)
                tkm = []
                for d_ in range(2):
                    t_ = b.op("dve", lambda e, d_=d_: e.memset(lrA[d_][:, :], 1.0), inc=sD)
                    tkm.append(b.dma("sp", lrA[d_][0:16, :], S["glrT"][d_ * 32:d_ * 32 + 16, :], waits=[t_], inc=b.sem(f"c5lr{d_}")))
                    tkm.append(b.dma("sp", w2b[d_][0:17, :], I["w2b"][d_, :, :], inc=b.sem(f"c5w{d_}")))
                    tkm.append(b.dma("sp", M1[d_][:, :], I["M1"][d_, :, :], inc=b.sem(f"c5m1{d_}")))
                    tkm.append(b.dma("sp", M2[d_][:, :], I["M2"][d_, :, :], inc=b.sem(f"c5m2{d_}")))
                psg = cx.psum[0:2]
                psc = cx.psum[2:6]
                psd = cx.psum[6:8]
                l_free = [None, None]
                o_free = [None, None]
                psg_free = None
                psc_free = None
                psd_free = None
                la_free = None
                e_free = None
                e3_free = None
                gqT_v, gkT_v = kc_view(S["gqT"]), kc_view(S["gkT"])
                it = 0
                for blk in range(TT // 128):
                    own = blk < T // 128
                    s2 = blk % 2
                    tsl = slice(blk * 128, (blk + 1) * 128)
                    tkl = None
                    if own:
                        b.dma("sp", gqb[s2][:, :, :], gqT_v[:, :, tsl], waits=[l_free[s2]], inc=lsem[s2])
                        b.dma("sp", gkb[s2][:, :, :], gkT_v[:, :, tsl], waits=[l_free[s2]], inc=lsem[s2])
                    tkl = b.dma("sp", gkt[s2][:, :], S["gk"][tsl, :], waits=[l_free[s2]], inc=lsem[s2])
                    for d_ in range(2):
                        o2 = it % 2
                        it += 1
                        for hf_ in range(2):
                            tkg = b.op("pe", lambda e, d_=d_, hf_=hf_, tsl=tsl: e.matmul(
                                psg[hf_][:, :], lhsT=lrA[d_][0:17, tsl], rhs=w2b[d_][0:17, hf_ * 512:(hf_ + 1) * 512],
                                start=True, stop=True), waits=[tkm, psg_free] if hf_ == 0 else [], inc=cx.pe_sem if hf_ == 1 else None)
                        for hf_ in range(2):
                            tke = b.op("act", lambda e, hf_=hf_: e.activation(out=la[:, hf_ * 512:(hf_ + 1) * 512], in_=psg[hf_][:, :],
                                                                               func=AF.Exp, scale=-1.0),
                                       waits=[tkg, la_free] if hf_ == 0 else [], inc=sA)
                        psg_free = tke
                        tkla = b.op("act", lambda e: e.activation(out=la[:, :], in_=la[:, :], func=AF.Ln, bias=onec[:, :], scale=1.0),
                                    waits=[tke, tkc], inc=sA)
                        for fc in range(8):
                            tkcm = b.op("pe", lambda e, d_=d_, fc=fc: e.matmul(
                                psc[fc // 2][:, (fc % 2) * 256:(fc % 2) * 256 + 130], lhsT=la[:, fc * 128:(fc + 1) * 128], rhs=M1[d_][:, :],
                                start=True, stop=True), waits=[tkla, psc_free] if fc == 0 else [], inc=cx.pe_sem if fc == 7 else None)
                        for hf_ in range(2):
                            tkdm = b.op("pe", lambda e, d_=d_, hf_=hf_: e.matmul(
                                psd[hf_][:, :], lhsT=M2[d_][:, :], rhs=la[:, hf_ * 512:(hf_ + 1) * 512], start=True, stop=True),
                                waits=[psd_free] if hf_ == 0 else [], inc=cx.pe_sem if hf_ == 1 else None)
                        la_free = tkdm
                        for bk in range(4):
                            tkdc = b.op("act", lambda e, d_=d_, bk=bk, blk=blk: e.activation(
                                out=dec[:, d_, 2 * bk:2 * bk + 2, 2 * blk:2 * blk + 2],
                                in_=psc[bk][:, :].rearrange("p (a c) -> p a c", a=2)[:, :, 128:130], func=AF.Exp),
                                waits=[tkcm] if bk == 0 else [], inc=sA)
                        last_c = tkdc
                        if own:
                            for bk in range(4):
                                src = psc[bk][:, :].rearrange("p (a c) -> p a c", a=2)[:, :, 0:128]
                                b.op("act", lambda e, bk=bk, src=src: e.activation(out=E1[:, 2 * bk:2 * bk + 2, :], in_=src, func=AF.Exp),
                                     waits=[e_free] if bk == 0 else [])
                                last_c = b.op("act", lambda e, bk=bk, src=src: e.activation(out=E2[:, 2 * bk:2 * bk + 2, :], in_=src, func=AF.Exp, scale=-1.0),
                                              inc=sA)
                        psc_free = last_c
                        for hf_ in range(2):
                            tke3 = b.op("act", lambda e, hf_=hf_: e.activation(out=E3[:, hf_ * 512:(hf_ + 1) * 512], in_=psd[hf_][:, :], func=AF.Exp),
                                        waits=[tkdm, e3_free] if hf_ == 0 else [], inc=sA)
                        psd_free = tke3
                        tko = None
                        if own:
                            b.op("dve", lambda e, s2=s2, o2=o2: e.scalar_tensor_tensor(
                                out=qdo[o2][:, :, :], in0=gqb[s2][:, :, :], scalar=1.0 / 16.0, in1=E1[:, :, :], op0=ALU.mult, op1=ALU.mult),
                                waits=[last_c, tkl, o_free[o2]])
                            tko = b.op("dve", lambda e, s2=s2, o2=o2: e.tensor_tensor(
                                out=kio[o2][:, :, :], in0=gkb[s2][:, :, :], in1=E2[:, :, :], op=ALU.mult), inc=sD)
                            e_free = tko
                        tko = b.op("dve", lambda e, s2=s2, o2=o2: e.tensor_tensor(
                            out=kto[o2][:, :], in0=gkt[s2][:, :], in1=E3[:, :], op=ALU.mult),
                            waits=[tke3, tkl, o_free[o2]], inc=sD)
                        e3_free = tko
                        if own:
                            b.dma("sp", kc_view(S["qdT"][d_])[:, :, tsl], qdo[o2][:, :, :], waits=[tko], inc=osem[o2])
                            b.dma("sp", kc_view(S["kiT"][d_])[:, :, tsl], kio[o2][:, :, :], waits=[tko], inc=osem[o2])
                        o_free[o2] = b.dma("sp", S["kte"][d_][tsl, :], kto[o2][:, :], waits=[tko], inc=osem[o2])
                    l_free[s2] = tko
                    b.maybe_flush(2000)
                b.wait("sp", o_free)
                b.flush("p5a")
            phase_reset()
            if stop_after == "p5a":
                return nc

            with ExitStack() as st_:
                sbp = lambda n, s_, d: st_.enter_context(nc.sbuf_tensor(f"p5b_{n}", s_, d))
                Sf = sbp("S", [128, 8, 512], F32)
                Sb = [sbp(f"Sb{i}", [128, 8, 512], BF16) for i in range(2)]
                vb = [sbp(f"v{i}", [64, 2048], BF16) for i in range(3)]
                kb = [sbp(f"k{i}", [64, 1024], BF16) for i in range(3)]
                qd = [sbp(f"qd{i}", [128, 8, 64], BF16) for i in range(2)]
                ki = [sbp(f"ki{i}", [128, 8, 64], BF16) for i in range(2)]
                Ab = sbp("Ab", [64, 256], BF16)
                mk = [sbp(f"mk{d_}", [64, 256], F32) for d_ in range(2)]
                ost = [sbp(f"ost{i}", [64, 2048], F32) for i in range(2)]
                ofl = [sbp(f"ofl{i}", [64, 2048], F32) for i in range(2)]
                grl = [sbp(f"grl{i}", [64, 2048], F32) for i in range(2)]
                gnb = sbp("gnb", [64, 2048], F32)
                ogb = [sbp(f"ogb{i}", [64, 2048], F32) for i in range(2)]
                ssq = sbp("ssq", [64, 4], F32)
                rr = sbp("rr", [64, 4], F32)
                junk = sbp("junk", [64, 512], F32)
                epsr = sbp("epsr", [64, 1], F32)
                tk_e = b.op("dve", lambda e: e.memset(epsr[:, :], RMS_EPS), inc=sD)
                tkmk = [b.dma("sp", mk[d_][:, :], I["maskT"][d_, :, :], inc=b.sem(f"c5mk{d_}")) for d_ in range(2)]
                tkgn = b.dma("sp", gnb[:, :], I["gnorm_bc"][:, :], inc=b.sem("c5gn"))
                ldsem = [b.sem(f"p5b_ld{i}") for i in range(3)]
                qsem = [b.sem(f"p5b_q{i}") for i in range(2)]
                xsem = [b.sem(f"p5b_x{i}") for i in range(2)]
                stsem = [b.sem(f"p5b_st{i}") for i in range(2)]
                ps_a = cx.psum[0]
                ps_o = cx.psum[1:5]
                ps_kv = cx.psum[5:8]
                ld_free = [None] * 3
                q_free = [None, None]
                x_free = [None, None]
                st_free = [None, None]
                og_free = [None, None]
                a_free = None
                ab_free = None
                o_freeP = None
                kv_free = [None] * 3
                sb_ready = None
                sb_rd = [None, None]
                s_last = None
                cast_prev = [None] * 8
                cast_last = None
                nld = 0
                nq = 0
                nkv = 0
                nx = 0
                cur = 0
                for d_ in range(2):
                    tkz = b.op("dve", lambda e: e.memset(Sf[:, :, :], 0.0), waits=[s_last, cast_last], inc=sD)
                    tkz2 = b.op("dve", lambda e, cur=cur: e.memset(Sb[cur][:, :, :], 0.0), waits=[sb_rd[cur], cast_last], inc=sD)
                    s_last = tkz2
                    sb_ready = tkz2
                    cast_prev = [None] * 8
                    order = list(range(32)) if d_ == 0 else list(range(31, -1, -1))
                    seq = [("other", c) for c in order] + [("flag", 0)] + [("own", c) for c in order]
                    for (kind, c) in seq:
                        if kind == "flag":
                            tkf = b.op("dve", lambda e, d_=d_: e.tensor_scalar(out=Sf[:, :, :], in0=Sf[:, :, :], scalar1=cvec[:, 1 + d_:2 + d_],
                                                                               scalar2=None, op0=ALU.mult), waits=[s_last, tk_cv, cast_last], inc=sD)
                            s_last = tkf
                            tkf2 = b.op("act", lambda e, cur=cur: e.activation(out=Sb[cur][:, :, :], in_=Sf[:, :, :], func=AF.Copy),
                                        waits=[tkf, sb_rd[cur]], inc=sA)
                            sb_ready = tkf2
                            cast_last = tkf2
                            cast_prev = [tkf2] * 8
                            continue
                        gch = c + (32 if kind == "other" else 0)
                        r0 = gch * 64
                        l3 = nld % 3
                        nld += 1
                        b.dma("sp", vb[l3][:, :], S["gv"][r0:r0 + 64, :], waits=[ld_free[l3]], inc=ldsem[l3])
                        tkld = b.dma("sp", kb[l3][:, :], S["kte"][d_][r0:r0 + 64, :], waits=[ld_free[l3]], inc=ldsem[l3])
                        nxt = 1 - cur
                        if kind == "own":
                            q2 = nq % 2
                            nq += 1
                            b.dma("sp", qd[q2][:, :, :], kc_view(S["qdT"][d_])[:, :, r0:r0 + 64], waits=[q_free[q2]], inc=qsem[q2])
                            tkq = b.dma("sp", ki[q2][:, :, :], kc_view(S["kiT"][d_])[:, :, r0:r0 + 64], waits=[q_free[q2]], inc=qsem[q2])
                            for h in range(4):
                                for dc in range(2):
                                    tka = b.op("pe", lambda e, h=h, dc=dc, q2=q2: e.matmul(
                                        ps_a[0:64, h * 64:(h + 1) * 64], lhsT=ki[q2][:, h * 2 + dc, :], rhs=qd[q2][:, h * 2 + dc, :],
                                        start=(dc == 0), stop=(dc == 1)),
                                        waits=[tkq, a_free] if (h == 0 and dc == 0) else [], inc=cx.pe_sem if (h == 3 and dc == 1) else None)
                            tkab = b.op("dve", lambda e, d_=d_: e.tensor_tensor(out=Ab[:, :], in0=ps_a[0:64, 0:256], in1=mk[d_][:, :], op=ALU.mult),
                                        waits=[tka, tkmk, ab_free], inc=sD)
                            a_free = tkab
                            for h in range(4):
                                for dc in range(2):
                                    b.op("pe", lambda e, h=h, dc=dc, q2=q2, cur=cur: e.matmul(
                                        ps_o[h][0:64, :], lhsT=qd[q2][:, h * 2 + dc, :], rhs=Sb[cur][:, h * 2 + dc, :],
                                        start=(dc == 0), stop=False),
                                        waits=[sb_ready, o_freeP] if (h == 0 and dc == 0) else [])
                                tko_ = b.op("pe", lambda e, h=h, l3=l3: e.matmul(
                                    ps_o[h][0:64, :], lhsT=Ab[0:64, h * 64:(h + 1) * 64], rhs=vb[l3][0:64, h * 512:(h + 1) * 512],
                                    start=False, stop=True), waits=[tkab, tkld] if h == 0 else [], inc=cx.pe_sem if h == 3 else None)
                            ab_free = tko_
                            q_free[q2] = tko_
                            sb_rd[cur] = tko_
                            x2 = nx % 2
                            nx += 1
                            if d_ == 0:
                                for h in range(4):
                                    tkev = b.op("act", lambda e, h=h, x2=x2: e.activation(out=ost[x2][:, h * 512:(h + 1) * 512], in_=ps_o[h][0:64, :], func=AF.Copy),
                                                waits=[tko_, st_free[x2]] if h == 0 else [], inc=sA)
                                o_freeP = tkev
                                st_free[x2] = b.dma("sp", S["ofwd"][r0:r0 + 64, :], ost[x2][:, :], waits=[tkev], inc=stsem[x2])
                            else:
                                b.dma("sp", ofl[x2][:, :], S["ofwd"][r0:r0 + 64, :], waits=[x_free[x2]], inc=xsem[x2])
                                tkx = b.dma("sp", grl[x2][:, :], S["gr"][r0:r0 + 64, :], waits=[x_free[x2]], inc=xsem[x2])
                                for h in range(4):
                                    tkad = b.op("dve", lambda e, h=h, x2=x2: e.tensor_tensor(
                                        out=ost[x2][:, h * 512:(h + 1) * 512], in0=ps_o[h][0:64, :], in1=ofl[x2][:, h * 512:(h + 1) * 512], op=ALU.add),
                                        waits=[tko_, tkx, st_free[x2]] if h == 0 else [], inc=sD)
                                o_freeP = tkad
                                for h in range(4):
                                    tksq = b.op("act", lambda e, h=h, x2=x2: e.activation(out=junk[:, :], in_=ost[x2][:, h * 512:(h + 1) * 512],
                                                                                            func=AF.Square, accum_out=ssq[:, h:h + 1]),
                                                waits=[tkad] if h == 0 else [], inc=sA)
                                tksd = b.op("act", lambda e: e.activation(out=rr[:, :], in_=ssq[:, :], func=AF.Sqrt, bias=epsr[:, :], scale=1.0 / 512.0),
                                            waits=[tksq, tk_e], inc=sA)
                                tksl = b.op("act", lambda e, x2=x2: e.activation(out=grl[x2][:, :], in_=grl[x2][:, :], func=AF.Silu), waits=[tkx], inc=sA)
                                tkrc = b.op("dve", lambda e: e.reciprocal(out=rr[:, :], in_=rr[:, :]), waits=[tksd], inc=sD)
                                tkg2 = b.op("dve", lambda e, x2=x2: e.tensor_tensor(out=grl[x2][:, :], in0=grl[x2][:, :], in1=gnb[:, :], op=ALU.mult),
                                            waits=[tksl, tkgn], inc=sD)
                                for h in range(4):
                                    tkfin = b.op("dve", lambda e, h=h, x2=x2: e.scalar_tensor_tensor(
                                        out=ogb[x2][:, h * 512:(h + 1) * 512], in0=ost[x2][:, h * 512:(h + 1) * 512], scalar=rr[:, h:h + 1],
                                        in1=grl[x2][:, h * 512:(h + 1) * 512], op0=ALU.mult, op1=ALU.mult),
                                        waits=[tkrc, tkg2, og_free[x2]] if h == 0 else [], inc=sD)
                                x_free[x2] = tkfin
                                st_free[x2] = tkfin
                                og_free[x2] = b.dma("sp", S["ogla"][r0:r0 + 64, :], ogb[x2][:, :], waits=[tkfin], inc=stsem[x2])
                        tkc_last = None
                        for h in range(4):
                            for dc in range(2):
                                hd = h * 2 + dc
                                k3 = nkv % 3
                                nkv += 1
                                tkkv = b.op("pe", lambda e, h=h, hd=hd, l3=l3, k3=k3: e.matmul(
                                    ps_kv[k3][:, :], lhsT=kb[l3][0:64, hd * 128:(hd + 1) * 128], rhs=vb[l3][0:64, h * 512:(h + 1) * 512],
                                    start=True, stop=True), waits=[tkld, kv_free[k3]], inc=cx.pe_sem)
                                tks = b.op("dve", lambda e, hd=hd, k3=k3, d_=d_, gch=gch: e.scalar_tensor_tensor(
                                    out=Sf[:, hd, :], in0=Sf[:, hd, :], scalar=dec[:, d_, hd, gch:gch + 1], in1=ps_kv[k3][:, :],
                                    op0=ALU.mult, op1=ALU.add), waits=[tkkv, s_last, cast_prev[hd]], inc=sD)
                                kv_free[k3] = tks
                                tkc_last = b.op("act", lambda e, hd=hd, nxt=nxt: e.activation(out=Sb[nxt][:, hd, :], in_=Sf[:, hd, :], func=AF.Copy),
                                                waits=[tks, sb_rd[nxt]] if hd == 0 else [tks], inc=sA)
                                cast_prev[hd] = tkc_last
                                cast_last = tkc_last
                        s_last = tks
                        ld_free[l3] = tkkv
                        sb_ready = tkc_last
                        cur = nxt
                        b.maybe_flush(2500)
                b.wait("sp", [st_free, og_free])
                b.flush("p5b")
            phase_reset()
        if stop_after == "p5b":
            return nc

        def tm2fm(tag, src, ncols, dst, nrows=T):
            NC_ = ncols // 128
            with ExitStack() as st_:
                sbp = lambda n, s_, d: st_.enter_context(nc.sbuf_tensor(f"sb{tag}_{n}", s_, d))
                xin = [sbp(f"x{i}", [128, ncols], F32) for i in range(2)]
                xo = [sbp(f"o{i}", [128, NC_, 128], BF16) for i in range(2)]
                ls = [b.sem(f"{tag}_l{i}") for i in range(2)]
                ss = [b.sem(f"{tag}_s{i}") for i in range(2)]
                l_free = [None, None]
                o_free = [None, None]
                pfree = [None] * 8
                dstv = kc_view(dst)
                nb = 0
                for i in range(nrows // 128):
                    s2 = i % 2
                    tkl = b.dma("sp", xin[s2][:, :], src[i * 128:(i + 1) * 128, :], waits=[l_free[s2]], inc=ls[s2])
                    tke = None
                    for q4 in range(NC_ // 4):
                        pi = nb % 8
                        nb += 1
                        bank = cx.psum[pi]
                        for j in range(4):
                            fc = q4 * 4 + j
                            tkt = b.op("pe", lambda e, fc=fc, j=j, s2=s2, bank=bank: e.transpose(
                                out=bank[:, j * 128:(j + 1) * 128], in_=xin[s2][:, fc * 128:(fc + 1) * 128], identity=ident[:, :]),
                                waits=[tkl, tk_id, pfree[pi]] if j == 0 else [], inc=cx.pe_sem if j == 3 else None)
                        eng = "dve" if q4 % 2 == 0 else "act"
                        o_ = xo[s2][:, q4 * 4:(q4 + 1) * 4, :]
                        src_ = bank[:, :].rearrange("p (a c) -> p a c", a=4)
                        if eng == "dve":
                            tke = b.op("dve", lambda e, o_=o_, src_=src_: e.tensor_copy(out=o_, in_=src_), waits=[tkt, o_free[s2]], inc=sD)
                        else:
                            tke = b.op("act", lambda e, o_=o_, src_=src_: e.activation(out=o_, in_=src_, func=AF.Copy), waits=[tkt, o_free[s2]], inc=sA)
                        pfree[pi] = tke
                    l_free[s2] = tkt
                    o_free[s2] = b.dma("sp", dstv[:, :, i * 128:(i + 1) * 128], xo[s2][:, :, :], waits=[(sD, sD.v), (sA, sA.v)], inc=ss[s2])
                b.wait("sp", o_free)
                b.flush(tag)
            phase_reset()

        tm2fm("p5t", S["ogla"], 2048, S["oglaT"])
        if stop_after == "p5":
            return nc

        scratch("pre1T", [D, T], F32)
        scratch("pre2T", [D, T], F32)
        scratch("pre3T", [D, T], F32)

        def fn_axpy(gp, src_dram, scale, which="A"):
            def fn(eng, o, src, tmp, info, w, si, ci):
                vc0, cw, t0, nt = info
                ldt, tkl = gp.ld(which, si, ci, src_dram[vc0:vc0 + cw, t0:t0 + nt], cw, nt)
                tk = b.op("dve", lambda e: e.scalar_tensor_tensor(out=o, in0=ldt, scalar=scale, in1=src, op0=ALU.mult, op1=ALU.add),
                          waits=w + [tkl], inc=sD)
                gp.ld_done(which, si, ci, tk)
                return tk
            return fn

        gp = GP("p6")

        def fn_bm(eng, o, src, tmp, info, w, si, ci):
            vc0, cw, t0, nt = info
            g0, tkl = gp.ld("A", si, ci, S["gateT"][vc0:vc0 + cw, t0:t0 + nt], cw, nt)
            tk = b.op("dve", lambda e: e.tensor_tensor(out=o, in0=src, in1=g0, op=ALU.mult), waits=w + [tkl], inc=sD)
            gp.ld_done("A", si, ci, tk)
            return tk

        def fn_bg(eng, o, src, tmp, info, w, si, ci):
            vc0, cw, t0, nt = info
            g1, tkl = gp.ld("A", si, ci, S["gateT"][D + vc0:D + vc0 + cw, t0:t0 + nt], cw, nt)
            bm, tkl2 = gp.ld("B", si, ci, S["bmT"][vc0:vc0 + cw, t0:t0 + nt], cw, nt)
            tk = b.op("dve", lambda e: e.tensor_tensor(out=tmp, in0=src, in1=g1, op=ALU.mult), waits=w + [tkl], inc=sD)
            gp.ld_done("A", si, ci, tk)
            tk = b.op("dve", lambda e: e.tensor_tensor(out=o, in0=tmp, in1=bm, op=ALU.add), waits=[tk, tkl2], inc=sD)
            gp.ld_done("B", si, ci, tk)
            return tk
        gemm(b, cx, "br_mla", S["omlaT"], 2048, 0, T, [(I["w_br_mla"], 0, D)], "fm", gp.epi("fm", [(0, D, S["bmT"], 0)], fn=fn_bm))
        b.wait("sp", gp.stg_free)
        b.flush("p6mid")
        gemm(b, cx, "br_gla", S["oglaT"], 2048, 0, T, [(I["w_br_gla"], 0, D)], "fm",
             gp.epi("fm", [(0, D, S["mergedT"], 0)], dtype=BF16, fn=fn_bg))
        gp.close("p6")
        phase_reset()
        if stop_after == "p6":
            return nc

        gp = GP("p7")
        gemm(b, cx, "mix_out", S["mergedT"], D, 0, T, [(I["w_mix_out"], 0, D)], "fm",
             gp.epi("fm", [(0, D, S["pre1T"], 0)], fn=fn_axpy(gp, S["hT_f32"], ALPHA)))
        gp.close("p7")
        phase_reset()

        def ln_fm(tag, pre, gname, bname, out_f32, out_bf, out_tm=None):
            TL = 256
            with ExitStack() as st_:
                sbp = lambda n, s_, d: st_.enter_context(nc.sbuf_tensor(f"sb{tag}_{n}", s_, d))
                X = [sbp(f"X{i}", [128, 32, TL], F32) for i in range(2)]
                Q = sbp("Q", [128, 32, TL], F32)
                Yb = sbp("Yb", [128, 32, TL], BF16)
                onesF = sbp("ones", [128, 128], F32)
                m = sbp("m", [128, TL], F32)
                msq = sbp("msq", [128, TL], F32)
                var = sbp("var", [128, TL], F32)
                rstd = sbp("rstd", [128, TL], F32)
                gc = sbp("g", [128, 32], F32)
                bc = sbp("b", [128, 32], F32)
                eps = sbp("eps", [128, 1], F32)
                yst = [sbp(f"yst{i}", [128, D], F32) for i in range(2)] if out_tm is not None else None
                tk1 = b.op("dve", lambda e: e.memset(onesF[:, :], 1.0), inc=sD)
                tk1 = b.op("dve", lambda e: e.memset(eps[:, :], LN_EPS), inc=sD)
                tkg = [b.dma("sp", gc[:, :], I[gname][:, :], inc=b.sem(f"{tag}_cg")),
                       b.dma("sp", bc[:, :], I[bname][:, :], inc=b.sem(f"{tag}_cb"))]
                xs = [b.sem(f"{tag}_xs{i}") for i in range(2)]
                osm = b.sem(f"{tag}_os")
                ysm = [b.sem(f"{tag}_ys{i}") for i in range(2)]
                x_free = [None, None]
                q_free = None
                yb_free = None
                ps_sum, ps_sq = cx.psum[0], cx.psum[1]
                st_free_ = None
                sq_free = None
                y_free = [None, None]
                tp_free = [None] * 4
                ntp = 0
                prev = kc_view(pre)
                for i in range(T // TL):
                    s2 = i % 2
                    tsl = slice(i * TL, (i + 1) * TL)
                    tkx = b.dma("sp", X[s2][:, :, :], prev[:, :, tsl], waits=[x_free[s2]], inc=xs[s2])
                    tkq = b.op("act", lambda e, s2=s2: e.activation(out=Q[:, :, :], in_=X[s2][:, :, :], func=AF.Square), waits=[tkx, q_free], inc=sA)
                    for fc in range(32):
                        tks = b.op("pe", lambda e, fc=fc, s2=s2: e.matmul(ps_sum[:, 0:TL], lhsT=onesF[:, :], rhs=X[s2][:, fc, :],
                                                                           start=(fc == 0), stop=(fc == 31)),
                                   waits=[tkx, tk1, st_free_] if fc == 0 else [], inc=cx.pe_sem if fc == 31 else None)
                    for fc in range(32):
                        tks2 = b.op("pe", lambda e, fc=fc: e.matmul(ps_sq[:, 0:TL], lhsT=onesF[:, :], rhs=Q[:, fc, :],
                                                                     start=(fc == 0), stop=(fc == 31)),
                                    waits=[tkq, sq_free] if fc == 0 else [], inc=cx.pe_sem if fc == 31 else None)
                    tkm = b.op("act", lambda e: e.activation(out=m[:, :], in_=ps_sum[:, 0:TL], func=AF.Copy, scale=1.0 / D), waits=[tks, y_free], inc=sA)
                    st_free_ = tkm
                    tkm2 = b.op("dve", lambda e: e.tensor_tensor(out=msq[:, :], in0=m[:, :], in1=m[:, :], op=ALU.mult), waits=[tkm], inc=sD)
                    tkv = b.op("dve", lambda e: e.scalar_tensor_tensor(out=var[:, :], in0=ps_sq[:, 0:TL], scalar=1.0 / D, in1=msq[:, :],
                                                                        op0=ALU.mult, op1=ALU.subtract), waits=[tks2, tkm2], inc=sD)
                    sq_free = tkv
                    tksd = b.op("act", lambda e: e.activation(out=var[:, :], in_=var[:, :], func=AF.Sqrt, bias=eps[:, :], scale=1.0), waits=[tkv], inc=sA)
                    tkr = b.op("dve", lambda e: e.reciprocal(out=rstd[:, :], in_=var[:, :]), waits=[tksd], inc=sD)
                    tkn = tkr
                    for fc in range(32):
                        b.op("dve", lambda e, fc=fc, s2=s2: e.tensor_tensor(out=X[s2][:, fc, :], in0=X[s2][:, fc, :], in1=m[:, :], op=ALU.subtract),
                             waits=[tkr, tks] if fc == 0 else [])
                        tkn = b.op("dve", lambda e, fc=fc, s2=s2: e.tensor_tensor(out=X[s2][:, fc, :], in0=X[s2][:, fc, :], in1=rstd[:, :], op=ALU.mult), inc=sD)
                        tky = b.op("act", lambda e, fc=fc, s2=s2: e.activation(out=Q[:, fc, :], in_=X[s2][:, fc, :], func=AF.Identity,
                                                                                bias=bc[:, fc:fc + 1], scale=gc[:, fc:fc + 1]),
                                   waits=[tkn, tkg, tks2], inc=sA)
                    x_free[s2] = tky
                    y_free = tkn
                    outs_ = []
                    if out_f32 is not None:
                        outs_.append(b.dma("sp", kc_view(out_f32)[:, :, tsl], Q[:, :, :], waits=[tky], inc=osm))
                    if out_bf is not None:
                        tkc = b.op("pool", lambda e: e.tensor_copy(out=Yb[:, :, :], in_=Q[:, :, :]), waits=[tky, yb_free], inc=sP)
                        yb_free = b.dma("sp", kc_view(out_bf)[:, :, tsl], Yb[:, :, :], waits=[tkc], inc=osm)
                        outs_.append(yb_free)
                        outs_.append(tkc)
                    if out_tm is not None:
                        for hb_ in range(TL // 128):
                            y2 = (i * (TL // 128) + hb_) % 2
                            tke = None
                            for q4 in range(8):
                                pi = 2 + ntp % 4
                                ntp += 1
                                bank = cx.psum[pi]
                                for j in range(4):
                                    fc = q4 * 4 + j
                                    tkt = b.op("pe", lambda e, fc=fc, j=j, hb_=hb_, bank=bank: e.transpose(
                                        out=bank[:, j * 128:(j + 1) * 128], in_=Q[:, fc, hb_ * 128:(hb_ + 1) * 128], identity=ident[:, :]),
                                        waits=[tky, tk_id, tp_free[pi - 2]] if j == 0 else [], inc=cx.pe_sem if j == 3 else None)
                                o_ = yst[y2][:, q4 * 512:(q4 + 1) * 512]
                                if q4 % 2 == 0:
                                    tke = b.op("dve", lambda e, o_=o_, bank=bank: e.tensor_copy(out=o_, in_=bank[:, :]), waits=[tkt, y_free_st[y2]], inc=sD)
                                else:
                                    tke = b.op("act", lambda e, o_=o_, bank=bank: e.activation(out=o_, in_=bank[:, :], func=AF.Copy), waits=[tkt, y_free_st[y2]], inc=sA)
                                tp_free[pi - 2] = tke
                            r0 = i * TL + hb_ * 128
                            y_free_st[y2] = b.dma("sp", out_tm[r0:r0 + 128, :], yst[y2][:, :], waits=[(sD, sD.v), (sA, sA.v)], inc=ysm[y2])
                            outs_.append(tkt)
                            outs_.append(y_free_st[y2])
                    q_free = outs_
                b.wait("sp", q_free)
                b.flush(tag)
            phase_reset()

        y_free_st = [None, None]
        ln_fm("ln1", S["pre1T"], "ln1_g", "ln1_b", S["h1T_f32"], S["h1T_bf"])
        if stop_after == "p7":
            return nc

        scratch("memTm", [NMEM, D], F32)
        tm2fm("p8t", I["mem"], D, S["memT"], nrows=NMEM)
        gp = GP("p8")
        gemm(b, cx, "xa_q", S["h1T_bf"], D, 0, T, [(I["xa_wq"], 0, D)], "fm", gp.epi("fm", [(0, D, S["qxT"], 0)], dtype=BF16))
        gemm(b, cx, "xa_k", S["memT"], D, 0, NMEM, [(I["xa_wkv"], 0, D)], "fm", gp.epi("fm", [(0, D, S["kxT"], 0)], dtype=BF16), TM=NMEM)
        gemm(b, cx, "xa_v", S["memT"], D, 0, NMEM, [(I["xa_wkv"], D, D)], "tm", gp.epi("tm", (S["vx"], 0), dtype=BF16), TM=NMEM)
        gp.close("p8")
        phase_reset()
        attn("p8a", 4,
             lambda h: [(S["qxT"], h * 1024 + dc * 128, 128) for dc in range(8)],
             lambda h: [(S["kxT"], h * 1024 + dc * 128, 128) for dc in range(8)],
             lambda h: S["vx"][:, h * 1024:(h + 1) * 1024], NMEM, 8, 1024.0 ** -0.5, False, S["oxT"])
        gp = GP("p9")
        gemm(b, cx, "xa_o", S["oxT"], D, 0, T, [(I["xa_wo"], 0, D)], "fm",
             gp.epi("fm", [(0, D, S["pre2T"], 0)], fn=fn_axpy(gp, S["h1T_f32"], ALPHA)))
        gp.close("p9")
        phase_reset()
        ln_fm("ln2", S["pre2T"], "ln2_g", "ln2_b", S["h2T_f32"], S["h2T_bf"])
        if stop_after == "p9":
            return nc

        gp = GP("p10")

        def fn_relu2(eng, o, src, tmp, info, w, si, ci):
            tk = b.op("act", lambda e: e.activation(out=tmp, in_=src, func=AF.Relu), waits=w, inc=sA)
            return b.op("dve", lambda e: e.tensor_tensor(out=o, in0=tmp, in1=tmp, op=ALU.mult), waits=[tk], inc=sD)
        gemm(b, cx, "mlp1", S["h2T_bf"], D, 0, T, [(I["mlp_w1"], 0, DFF)], "fm",
             gp.epi("fm", [(0, DFF, S["ffT"], 0)], dtype=BF16, fn=fn_relu2))
        gp.close("p10")
        phase_reset()
        gp = GP("p11")
        for q_ in range(4):
            src_ = S["h2T_f32"] if q_ == 0 else S["pre3T"]
            gemm(b, cx, f"mlp2_{q_}", S["ffT"][q_ * D:(q_ + 1) * D, :], D, 0, T, [(I["mlp_w2"][q_ * D:(q_ + 1) * D, :], 0, D)], "fm",
                 gp.epi("fm", [(0, D, S["pre3T"], 0)], fn=fn_axpy(gp, src_, ALPHA if q_ == 0 else 1.0)))
            b.wait("sp", gp.stg_free)
            b.flush(f"p11_{q_}")
        gp.close("p11")
        phase_reset()
        ln_fm("ln3", S["pre3T"], "ln3_g", "ln3_b", None, None, out_tm=y)
    return nc


def _cols(v):
    v = np.asarray(v, np.float32).reshape(-1)
    return np.ascontiguousarray(v.reshape(-1, 128).T)


def _rope_table(pos):
    inv = (1.0 / (10000.0 ** (np.arange(0, 64, 2, dtype=np.float32) / np.float32(64)))).astype(np.float32)
    ang = pos.astype(np.float32)[:, None] * inv[None, :]
    c = np.cos(ang).astype(np.float32).T
    s = np.sin(ang).astype(np.float32).T
    return (np.ascontiguousarray(np.concatenate([c, c, c, c], axis=0)),
            np.ascontiguousarray(np.concatenate([-s, s, -s, s], axis=0)))


def make_in_maps(inp):
    f = lambda a: np.ascontiguousarray(np.asarray(a, np.float32))
    shared = {
        "w_in": f(inp["w_in"][0]), "w_uq": f(inp["w_uq"][0]), "w_ukv": f(inp["w_ukv"][0]),
        "w_br_mla": f(inp["w_branch_mla"][0]), "w_br_gla": f(inp["w_branch_gla"][0]),
        "w_mix_out": f(inp["w_mix_out"][0]), "xa_wq": f(inp["xa_wq"][0]), "xa_wkv": f(inp["xa_wkv"][0]),
        "xa_wo": f(inp["xa_wo"][0]), "mlp_w1": f(inp["mlp_w1"][0]), "mlp_w2": f(inp["mlp_w2"][0]),
        "ln_in_g": _cols(inp["ln_in_g"]), "ln_in_b": _cols(inp["ln_in_b"]),
        "ln1_g": _cols(inp["ln1_g"]), "ln1_b": _cols(inp["ln1_b"]),
        "ln2_g": _cols(inp["ln2_g"]), "ln2_b": _cols(inp["ln2_b"]),
        "ln3_g": _cols(inp["ln3_g"]), "ln3_b": _cols(inp["ln3_b"]),
        "b_merge": _cols(inp["b_merge"]), "q_norm": _cols(inp["mla_q_norm"]), "kv_norm": _cols(inp["mla_kv_norm"]),
        "ident": np.eye(128, dtype=np.float32),
        "w2b": np.ascontiguousarray(np.concatenate([f(inp["gla_gate_w2"][0]), f(inp["gla_gate_b"][0])[:, None, :]], axis=1)),
        "gnorm_bc": np.ascontiguousarray(np.tile(f(inp["gla_norm"][0])[None, :], (64, 4))),
    }
    jj = np.arange(128)[:, None]
    cc = np.arange(128)[None, :]
    same = (jj // 64) == (cc // 64)
    M1 = np.zeros((2, 128, 130), np.float32)
    M2 = np.zeros((2, 128, 128), np.float32)
    M1[0, :, :128] = np.where(same & (jj <= cc), -1.0 / 16.0, 0.0)
    M1[1, :, :128] = np.where(same & (jj >= cc), -1.0 / 16.0, 0.0)
    for d_ in range(2):
        M1[d_, :64, 128] = -1.0 / 16.0
        M1[d_, 64:, 129] = -1.0 / 16.0
    M2[0] = np.where(same & (jj > cc), -1.0 / 16.0, 0.0)
    M2[1] = np.where(same & (jj < cc), -1.0 / 16.0, 0.0)
    j6 = np.arange(64)[:, None]
    c6 = np.arange(64)[None, :]
    mT = np.stack([np.tile((j6 <= c6).astype(np.float32), (1, 4)), np.tile((j6 >= c6).astype(np.float32), (1, 4))], axis=0)
    shared.update({"M1": M1, "M2": M2, "maskT": np.ascontiguousarray(mT)})
    maps = []
    ar = np.arange(T)
    for c in range(8):
        m = dict(shared)
        cv = np.zeros((128, 4), np.float32)
        if c < 4:
            own = f(inp["x_prompt"][c])
            m["x"] = np.ascontiguousarray(np.concatenate([own, own], axis=0))
            m["mem"] = f(inp["mem_prompt"][c])
            m["ropeC"], m["ropeS"] = _rope_table(np.concatenate([ar, ar]))
            cv[:, 0] = -30000.0
        else:
            s_, half = (c - 4) // 2, (c - 4) % 2
            xs = f(inp["x_sample"][s_])
            own = xs[half * T:(half + 1) * T]
            oth = xs[(1 - half) * T:(2 - half) * T]
            m["x"] = np.ascontiguousarray(np.concatenate([own, oth], axis=0))
            m["mem"] = f(inp["mem_sample"][s_])
            m["ropeC"], m["ropeS"] = _rope_table(np.concatenate([ar + half * T, ar + (1 - half) * T]))
            cv[:, 1] = 1.0 if half == 1 else 0.0
            cv[:, 2] = 1.0 if half == 0 else 0.0
        m["cvec"] = cv
        maps.append(m)
    return maps


def kernel(**inp):
    nc = build_program()
    maps = make_in_maps(inp)
    res = run_bass_kernel_spmd(nc, maps, core_ids=list(range(8)))
    ys = [np.asarray(r["y"], np.float32) for r in res.results]
    y_prompt = np.stack(ys[0:4], axis=0)
    y_sample = np.stack([np.concatenate([ys[4], ys[5]], axis=0), np.concatenate([ys[6], ys[7]], axis=0)], axis=0)
    return (y_prompt, y_sample)
```

```python
import numpy as np
from contextlib import ExitStack
import concourse.bass as bass
import concourse.mybir as mybir
from concourse.bass_utils import run_bass_kernel_spmd

F32 = mybir.dt.float32
BF16 = mybir.dt.bfloat16
AF = mybir.ActivationFunctionType
ALU = mybir.AluOpType

D = 4096
T = 2048
TT = 4096
NMEM = 256
DFF = 16384
LN_EPS = 1e-5
RMS_EPS = 1e-6
ALPHA = 2.0 ** 0.25
O_CQ, O_CKV, O_KR, O_GQ, O_GK, O_GV, O_GR, O_LR, O_MG = 0, 1024, 1536, 1600, 2624, 3648, 5696, 7744, 7776
N_IN = 15968

KN = {}
ENGS = ("pe", "act", "dve", "pool", "sp")


def flat(ws):
    if ws is None:
        return
    if isinstance(ws, tuple) and len(ws) == 2 and isinstance(ws[0], Sem):
        yield ws
        return
    for w in ws:
        yield from flat(w)


def dedupe(waits):
    d = {}
    for (s, v) in flat(waits):
        if v > 0 and (s.name not in d or d[s.name][1] < v):
            d[s.name] = (s.h, v)
    return list(d.values())


class Sem:
    def __init__(self, h, name):
        self.h = h
        self.v = 0
        self.name = name


class Builder:
    def __init__(self, nc, es):
        self.nc = nc
        self.es = es
        self.q = {k: [] for k in ENGS}
        self.phase_mode = False
        self.pool = []
        self.phase_sems = []
        self.bar = self.sem("bar")
        self.nbar = 0

    def sem(self, name):
        if self.phase_mode and self.pool:
            sm = self.pool.pop()
        else:
            sm = Sem(self.es.enter_context(self.nc.semaphore(name)), name)
        if self.phase_mode:
            self.phase_sems.append(sm)
        return sm

    def recycle(self):
        self.pool.extend(self.phase_sems)
        self.phase_sems = []

    def op(self, eng, fn, waits=(), inc=None, amt=1):
        ws = dedupe(waits)
        tick = None
        h = None
        if inc is not None:
            inc.v += amt
            tick = (inc, inc.v)
            h = inc.h

        def run(e, fn=fn, ws=ws, h=h, amt=amt):
            for sh, v in ws:
                e.wait_ge(sh, v)
            ins = fn(e)
            if h is not None:
                ins.then_inc(h, amt)
        self.q[eng].append(run)
        return tick

    def dma(self, eng, out, in_, waits=(), inc=None):
        return self.op(eng, lambda e: e.dma_start(out=out, in_=in_), waits=waits, inc=inc, amt=16)

    def wait(self, eng, waits):
        ws = dedupe(waits)

        def run(e, ws=ws):
            for sh, v in ws:
                e.wait_ge(sh, v)
        self.q[eng].append(run)

    def maybe_flush(self, limit=5000):
        if max(len(v) for v in self.q.values()) >= limit:
            self.flush("auto")

    def flush(self, name):
        nc = self.nc
        self.nbar += 1
        self.names = getattr(self, "names", []) + [name]
        target = 5 * self.nbar
        bar = self.bar.h
        q = self.q
        self.q = {k: [] for k in ENGS}
        self.bar.v = target

        def body(lst):
            def f(e):
                for r in lst:
                    r(e)
                e.drain().then_inc(bar, 1)
                e.wait_ge(bar, target)
            return f
        with nc.Block() as blk:
            blk.tensor(body(q["pe"]))
            blk.scalar(body(q["act"]))
            blk.vector(body(q["dve"]))
            blk.gpsimd(body(q["pool"]))
            blk.sync(body(q["sp"]))


def kc_view(ap2d):
    return ap2d.rearrange("(kc p) n -> p kc n", p=128)


class Ctx:
    pass


def gemm(b, cx, name, A, K, tok0, ntok, segs, orient, epi, TM=1024):
    KC = K // 128
    assert KC <= 32
    ncol = sum(s[2] for s in segs)
    GW = 256
    ngrp = (ncol + GW - 1) // GW
    ntt = ntok // TM
    tsw = min(512, TM)
    nts = TM // tsw
    Akc = kc_view(A)
    at = cx.at
    for tt in range(ntt):
        t0 = tok0 + tt * TM
        at_ticks = []
        step = 8
        for k0 in range(0, KC, step):
            k1 = min(KC, k0 + step)
            tk = b.dma("sp", at[:, k0:k1, 0:TM], Akc[:, k0:k1, t0:t0 + TM], waits=[cx.at_free], inc=cx.at_sem)
            at_ticks.append((k0, k1, tk))
        for g in range(ngrp):
            c0 = g * GW
            c1 = min(ncol, c0 + GW)
            u = cx.wu
            cx.wu += 1
            slot = u % cx.WR
            wt = cx.wt[slot]
            wsem = cx.wsem[slot]
            wfree = cx.wfree[slot]
            off = 0
            pos = 0
            wtick = None
            for (W, sc0, sn) in segs:
                lo = max(c0, pos)
                hi = min(c1, pos + sn)
                if lo < hi:
                    Wk = kc_view(W)
                    src0 = sc0 + (lo - pos)
                    ln = hi - lo
                    kstep = 16 if ln * 4 >= 512 else 32
                    for k0 in range(0, KC, kstep):
                        k1 = min(KC, k0 + kstep)
                        wtick = b.dma("pool", wt[:, k0:k1, off:off + ln], Wk[:, k0:k1, src0:src0 + ln],
                                      waits=[wfree], inc=wsem)
                    off += ln
                pos += sn
            grp = cx.pu % 2
            cx.pu += 1
            banks = cx.psum[grp * 4:(grp + 1) * 4]
            pfree = cx.pfree[grp]
            chunks = []
            first = True
            ngc = (c1 - c0 + 127) // 128
            last_tick = None
            if orient == "fm":
                for ci in range(ngc):
                    cw = min(128, c1 - c0 - ci * 128)
                    for ts in range(nts):
                        ps = banks[ci * nts + ts]
                        for kc in range(KC):
                            waits = []
                            if first:
                                waits = [wtick, pfree] + [t for (_, _, t) in at_ticks]
                                first = False
                            is_last = (ci == ngc - 1 and ts == nts - 1 and kc == KC - 1)
                            last_tick = b.op(
                                "pe",
                                lambda e, ps=ps, wt=wt, kc=kc, ci=ci, cw=cw, ts=ts: e.matmul(
                                    ps[0:cw, 0:tsw], lhsT=wt[:, kc, ci * 128:ci * 128 + cw],
                                    rhs=at[:, kc, ts * tsw:(ts + 1) * tsw], start=(kc == 0), stop=(kc == KC - 1)),
                                waits=waits, inc=(cx.pe_sem if is_last else None))
                        chunks.append((ps, c0 + ci * 128, cw, t0 + ts * tsw, tsw))
            else:
                gw = c1 - c0
                ntc = TM // 128
                for tc in range(ntc):
                    ps = banks[tc // 2]
                    hh = (tc % 2) * 256
                    for kc in range(KC):
                        waits = []
                        if first:
                            waits = [wtick, pfree] + [t for (_, _, t) in at_ticks]
                            first = False
                        is_last = (tc == ntc - 1 and kc == KC - 1)
                        last_tick = b.op(
                            "pe",
                            lambda e, ps=ps, wt=wt, kc=kc, tc=tc, hh=hh, gw=gw: e.matmul(
                                ps[:, hh:hh + gw], lhsT=at[:, kc, tc * 128:(tc + 1) * 128],
                                rhs=wt[:, kc, 0:gw], start=(kc == 0), stop=(kc == KC - 1)),
                            waits=waits, inc=(cx.pe_sem if is_last else None))
                    chunks.append((ps[:, hh:hh + gw], c0, gw, t0 + tc * 128, 128))
            cx.wfree[slot] = last_tick
            at_ticks_done = last_tick
            b.maybe_flush()
            cx.pfree[grp] = epi(last_tick, chunks)
        cx.at_free = at_ticks_done


def build_program(dbg=None, stop_after=None, p0_tiles=None):
    nc = bass.Bass("TRN2", target_bir_lowering=False)
    dt = lambda n, s, d, k="ExternalInput": nc.dram_tensor(n, s, d, kind=k).ap()
    I = {}
    I["x"] = dt("x", [TT, D], F32)
    I["mem"] = dt("mem", [NMEM, D], F32)
    I["ropeC"] = dt("ropeC", [128, TT], F32)
    I["ropeS"] = dt("ropeS", [128, TT], F32)
    I["cvec"] = dt("cvec", [128, 4], F32)
    I["ident"] = dt("ident", [128, 128], F32)
    I["w2b"] = dt("w2b", [2, 17, 1024], F32)
    I["M1"] = dt("M1", [2, 128, 130], F32)
    I["M2"] = dt("M2", [2, 128, 128], F32)
    I["maskT"] = dt("maskT", [2, 64, 256], F32)
    I["gnorm_bc"] = dt("gnorm_bc", [64, 2048], F32)
    I["w_in"] = dt("w_in", [D, N_IN], F32)
    I["w_uq"] = dt("w_uq", [1024, 3072], F32)
    I["w_ukv"] = dt("w_ukv", [512, 4096], F32)
    I["w_br_mla"] = dt("w_br_mla", [2048, D], F32)
    I["w_br_gla"] = dt("w_br_gla", [2048, D], F32)
    I["w_mix_out"] = dt("w_mix_out", [D, D], F32)
    I["xa_wq"] = dt("xa_wq", [D, D], F32)
    I["xa_wkv"] = dt("xa_wkv", [D, 2 * D], F32)
    I["xa_wo"] = dt("xa_wo", [D, D], F32)
    I["mlp_w1"] = dt("mlp_w1", [D, DFF], F32)
    I["mlp_w2"] = dt("mlp_w2", [DFF, D], F32)
    for nm, n in (("ln_in_g", D), ("ln_in_b", D), ("ln1_g", D), ("ln1_b", D), ("ln2_g", D), ("ln2_b", D),
                  ("ln3_g", D), ("ln3_b", D), ("b_merge", 2 * D), ("q_norm", 1024), ("kv_norm", 512)):
        I[nm] = dt(nm, [128, n // 128], F32)
    y = dt("y", [T, D], F32, "ExternalOutput")

    S = {}

    def scratch(n, s, d):
        kind = "ExternalOutput" if (dbg and n in dbg) else "Internal"
        S[n] = nc.dram_tensor(n, s, d, kind=kind).ap()
        return S[n]
    scratch("hT_bf", [D, TT], BF16)
    scratch("hT_f32", [D, T], F32)
    scratch("cqT", [1024, T], F32)
    scratch("gqT", [1024, T], F32)
    scratch("ckvT", [512, TT], F32)
    scratch("krT", [64, TT], F32)
    scratch("gkT", [1024, T], F32)
    scratch("glrT", [64, TT], F32)
    scratch("gk", [TT, 1024], F32)
    scratch("gv", [TT, 2048], BF16)
    scratch("gr", [T, 2048], F32)

    with ExitStack() as es:
        b = Builder(nc, es)
        cx = Ctx()
        cx.WR = 3
        cx.wt = [es.enter_context(nc.sbuf_tensor(f"wt{i}", [128, 32, 256], BF16)) for i in range(cx.WR)]
        cx.wsem = [b.sem(f"wsem{i}") for i in range(cx.WR)]
        cx.wfree = [None] * cx.WR
        cx.wu = 0
        cx.pu = 0
        cx.psum = [es.enter_context(nc.psum_tensor(f"ps{i}", [128, 512], F32)) for i in range(8)]
        cx.psbf = None
        cx.pfree = [None, None]
        cx.pe_sem = b.sem("pe_sem")
        cx.at_sem = b.sem("at_sem")
        cx.at_free = None
        ident = es.enter_context(nc.sbuf_tensor("sb_ident", [128, 128], F32))
        cvec = es.enter_context(nc.sbuf_tensor("sb_cvec", [128, 4], F32))
        eps_ln = es.enter_context(nc.sbuf_tensor("eps_ln", [128, 1], F32))
        csem = b.sem("csem")
        tk_id = b.dma("sp", ident[:, :], I["ident"][:, :], inc=b.sem("c_id"))
        tk_cv = b.dma("sp", cvec[:, :], I["cvec"][:, :], inc=b.sem("c_cv"))
        sA = b.sem("sA")
        sD = b.sem("sD")
        sP = b.sem("sP")
        sT = b.sem("sT")
        tk_eps = b.op("dve", lambda e: e.memset(eps_ln[:, :], LN_EPS), inc=sD)
        b.phase_mode = True

        with ExitStack() as ps_:
            sb = lambda n, s, d: ps_.enter_context(nc.sbuf_tensor(n, s, d))
            xt = [sb(f"p0_xt{i}", [128, D], F32) for i in range(2)]
            xn = [sb(f"p0_xn{i}", [128, D], F32) for i in range(2)]
            hf = [sb(f"p0_hf{i}", [128, 32, 128], F32) for i in range(2)]
            hb = [sb(f"p0_hb{i}", [128, 32, 128], BF16) for i in range(2)]
            gcol = sb("p0_g", [128, 32], F32)
            bcol = sb("p0_b", [128, 32], F32)
            stats = sb("p0_stats", [128, 8, 6], F32)
            mv = sb("p0_mv", [128, 2], F32)
            sd = sb("p0_sd", [128, 1], F32)
            rstd = sb("p0_rstd", [128, 1], F32)
            xsem = [b.sem(f"p0_xsem{i}") for i in range(2)]
            stsem = [b.sem(f"p0_stsem{i}") for i in range(2)]
            tk_g = b.dma("sp", gcol[:, :], I["ln_in_g"][:, :], inc=b.sem("c_g0"))
            tk_b = b.dma("sp", bcol[:, :], I["ln_in_b"][:, :], inc=b.sem("c_b0"))
            xt_free = [None, None]
            xn_free = [None, None]
            st_free = [None, None]
            ev_free = [None, None]
            hT_bf_v = kc_view(S["hT_bf"])
            hT_f_v = kc_view(S["hT_f32"])
            NTILE = p0_tiles or (TT // 128)
            stats2 = [stats, sb("p0_stats1", [128, 8, 6], F32)]
            mv2 = [mv, sb("p0_mv1", [128, 2], F32)]
            sd2 = [sd, sb("p0_sd1", [128, 1], F32)]
            rstd2 = [rstd, sb("p0_rstd1", [128, 1], F32)]
            tkxn_of = {}

            def front(i):
                s2 = i % 2
                st_, mv_, sd_, rs_ = stats2[s2], mv2[s2], sd2[s2], rstd2[s2]
                tkx = b.dma("sp", xt[s2][:, :], I["x"][i * 128:(i + 1) * 128, :], waits=[xt_free[s2]], inc=xsem[s2])
                for c in range(8):
                    tks = b.op("dve", lambda e, c=c, s2=s2: e.bn_stats(out=st_[:, c, :], in_=xt[s2][:, c * 512:(c + 1) * 512]),
                               waits=[tkx] if c == 0 else [], inc=sD if c == 7 else None)
                tka = b.op("dve", lambda e: e.bn_aggr(out=mv_[:, :], in_=st_[:, :, :]), waits=[tks], inc=sD)
                tksd = b.op("act", lambda e: e.activation(out=sd_[:, :], in_=mv_[:, 1:2], func=AF.Sqrt, bias=eps_ln[:, :], scale=1.0),
                            waits=[tka, tk_eps], inc=sA)
                tkr = b.op("dve", lambda e: e.reciprocal(out=rs_[:, :], in_=sd_[:, :]), waits=[tksd], inc=sD)
                tkxn = b.op("dve", lambda e, s2=s2: e.tensor_scalar(out=xn[s2][:, :], in0=xt[s2][:, :], scalar1=mv_[:, 0:1],
                                                                     scalar2=rs_[:, :], op0=ALU.subtract, op1=ALU.mult),
                            waits=[tkr, xn_free[s2]], inc=sD)
                xt_free[s2] = tkxn
                tkxn_of[i] = tkxn

            def back(i):
                s2 = i % 2
                tkxn = tkxn_of[i]
                evt = None
                tkp = None
                for hh in range(2):
                    banks = cx.psum[hh * 4:(hh + 1) * 4]
                    for j in range(16):
                        fc = hh * 16 + j
                        w = []
                        if j == 0:
                            w = [tkxn, ev_free[hh], tk_id]
                        tkp = b.op("pe", lambda e, fc=fc, j=j, banks=banks, s2=s2: e.transpose(
                            out=banks[j // 4][:, (j % 4) * 128:(j % 4 + 1) * 128],
                            in_=xn[s2][:, fc * 128:(fc + 1) * 128], identity=ident[:, :]),
                            waits=w, inc=cx.pe_sem if j == 15 else None)
                    for j in range(16):
                        fc = hh * 16 + j
                        w = []
                        if j == 0:
                            w = [tkp, tk_g, tk_b]
                            if hh == 0:
                                w.append(st_free[s2])
                        evt = b.op("act", lambda e, fc=fc, j=j, banks=banks, s2=s2: e.activation(
                            out=hf[s2][:, fc, :], in_=banks[j // 4][:, (j % 4) * 128:(j % 4 + 1) * 128],
                            func=AF.Identity, bias=bcol[:, fc:fc + 1], scale=gcol[:, fc:fc + 1]),
                            waits=w, inc=sA if j == 15 else None)
                    ev_free[hh] = evt
                xn_free[s2] = tkp
                tkc = b.op("pool", lambda e, s2=s2: e.tensor_copy(out=hb[s2][:, :, :], in_=hf[s2][:, :, :]), waits=[evt], inc=sP)
                tst = b.dma("sp", hT_bf_v[:, :, i * 128:(i + 1) * 128], hb[s2][:, :, :], waits=[tkc], inc=stsem[s2])
                if i < T // 128:
                    tst = b.dma("sp", hT_f_v[:, :, i * 128:(i + 1) * 128], hf[s2][:, :, :], waits=[tkc], inc=stsem[s2])
                st_free[s2] = tst

            front(0)
            for i in range(NTILE):
                if i + 1 < NTILE:
                    front(i + 1)
                back(i)
            b.wait("sp", [st_free[0], st_free[1]])
            cx.pfree = [ev_free[0], ev_free[1]]
            b.flush("p0")
        b.recycle()
        if stop_after == "p0":
            return nc

        Win = I["w_in"]
        scratch("ckvnT", [512, TT], BF16)
        scratch("cqnT", [1024, T], BF16)
        scratch("kropeT", [64, TT], BF16)
        scratch("qnT", [2048, T], BF16)
        scratch("qrT", [1024, T], BF16)
        scratch("knT", [2048, TT], BF16)
        scratch("vmla", [TT, 2048], BF16)
        scratch("omlaT", [2048, T], BF16)
        scratch("oglaT", [2048, T], BF16)
        scratch("gateT", [2 * D, T], F32)
        scratch("bmT", [D, T], F32)
        scratch("bgT", [D, T], F32)
        scratch("mergedT", [D, T], BF16)
        scratch("mixT", [D, T], F32)
        scratch("h1T_f32", [D, T], F32)
        scratch("h1T_bf", [D, T], BF16)
        scratch("qxT", [D, T], BF16)
        scratch("memT", [D, NMEM], BF16)
        scratch("kxT", [D, NMEM], BF16)
        scratch("vx", [NMEM, D], BF16)
        scratch("oxT", [D, T], BF16)
        scratch("aoT", [D, T], F32)
        scratch("h2T_f32", [D, T], F32)
        scratch("h2T_bf", [D, T], BF16)
        scratch("ffT", [DFF, T], BF16)
        for q_ in range(4):
            scratch(f"mlpP{q_}", [D, T], F32)

        class GP:
            def __init__(self, tag, extra=None):
                self.stack = ExitStack()
                self.tag = tag
                sbp = lambda n, s_, d: self.stack.enter_context(nc.sbuf_tensor(f"{tag}_{n}", s_, d))
                self.sb = sbp
                cx.at = sbp("at", [128, 32, 1024], BF16)
                cx.at_free = None
                self.stg = [sbp(f"stg{i}", [128, 2048], F32) for i in range(2)]
                self.stb = [sbp(f"stb{i}", [128, 2048], BF16) for i in range(2)]
                self.stg_sem = [b.sem(f"{tag}_stgsem{i}") for i in range(2)]
                self.stg_free = [None, None]
                self.n = 0
                self.ldb = {}
                self.ld_free = {}
                self.ld_sem = {}

            def ld(self, which, si, ci, src_ap, cw, nt):
                key = (which, si)
                if key not in self.ldb:
                    self.ldb[key] = self.sb(f"ld{which}{si}", [128, 2048], F32)
                    self.ld_sem[key] = [b.sem(f"{self.tag}_ld{which}{si}_{c}") for c in range(4)]
                    self.ld_free[key] = [None] * 4
                dst = self.ldb[key][0:cw, ci * 512:ci * 512 + nt]
                tk = b.dma("sp", dst, src_ap, waits=[self.ld_free[key][ci]], inc=self.ld_sem[key][ci])
                return dst, tk

            def ld_done(self, which, si, ci, tk):
                self.ld_free[(which, si)][ci] = tk

            def epi(self, orient, dests, dtype=F32, fn=None):
                me = self

                def epi(pe_tick, chunks):
                    si = me.n % 2
                    me.n += 1
                    st, stb = me.stg[si], me.stb[si]
                    ev = []
                    outs = []
                    for ci, (ps, vc0, cw, t0, nt) in enumerate(chunks):
                        eng = ("act" if ci % 2 == 0 else "dve") if orient == "fm" else ("act" if (ci // 2) % 2 == 0 else "dve")
                        w = [pe_tick, me.stg_free[si]]
                        buf = stb if dtype == BF16 else st
                        if orient == "fm":
                            o = buf[0:cw, ci * 512:ci * 512 + nt]
                            tmp = st[0:cw, ci * 512:ci * 512 + nt]
                            src = ps[0:cw, 0:nt]
                        else:
                            o = buf[:, ci * 256:ci * 256 + cw]
                            tmp = st[:, ci * 256:ci * 256 + cw]
                            src = ps
                        if fn is not None:
                            tk = fn(eng, o, src, tmp, (vc0, cw, t0, nt), w, si, ci)
                        elif eng == "act":
                            tk = b.op("act", lambda e, o=o, src=src: e.activation(out=o, in_=src, func=AF.Copy), waits=w, inc=sA)
                        else:
                            tk = b.op("dve", lambda e, o=o, src=src: e.tensor_copy(out=o, in_=src), waits=w, inc=sD)
                        ev.append(tk)
                        outs.append(o)
                    tst = None
                    for ci, (ps, vc0, cw, t0, nt) in enumerate(chunks):
                        if orient == "fm":
                            for (dv0, dn, dap, drow0) in dests:
                                lo = max(vc0, dv0)
                                hi = min(vc0 + cw, dv0 + dn)
                                if lo < hi:
                                    buf = stb if dtype == BF16 else st
                                    srcv = buf[lo - vc0:hi - vc0, ci * 512:ci * 512 + nt]
                                    tst = b.dma("sp", dap[drow0 + lo - dv0:drow0 + hi - dv0, t0:t0 + nt], srcv,
                                                waits=ev, inc=me.stg_sem[si])
                        else:
                            dap, cb = dests
                            tst = b.dma("sp", dap[t0:t0 + nt, vc0 - cb:vc0 - cb + cw], outs[ci], waits=ev, inc=me.stg_sem[si])
                    me.stg_free[si] = tst
                    return ev
                return epi

            def close(self, name):
                b.wait("sp", [self.stg_free[0], self.stg_free[1]])
                b.flush(name)
                self.stack.close()

        gp = GP("p1")
        if KN.get("only_tm"):
            gemm(b, cx, "in_tm_gk", S["hT_bf"], D, 0, KN.get("tm_tok", TT), [(Win, O_GK, KN.get("tm_cols", 1024))], "tm", gp.epi("tm", (S["gk"], 0)))
            gp.close("p1")
            return nc
        gemm(b, cx, "in_fm_own", S["hT_bf"], D, 0, KN.get("p1_tok", T),
             [(Win, O_CQ, 1024), (Win, O_GQ, 1024), (Win, O_GK, 1024)], "fm",
             gp.epi("fm", [(0, 1024, S["cqT"], 0), (1024, 1024, S["gqT"], 0), (2048, 1024, S["gkT"], 0)]))
        if stop_after == "p1a":
            gp.close("p1")
            return nc
        gemm(b, cx, "in_fm_all", S["hT_bf"], D, 0, TT,
             [(Win, O_CKV, 512), (Win, O_KR, 64), (Win, O_LR, 32)], "fm",
             gp.epi("fm", [(0, 512, S["ckvT"], 0), (512, 64, S["krT"], 0), (576, 16, S["glrT"], 0), (592, 16, S["glrT"], 32)]))
        if stop_after == "p1b":
            gp.close("p1")
            return nc
        if not KN.get("skip_tm"):
            gemm(b, cx, "in_tm_gk", S["hT_bf"], D, 0, TT, [(Win, O_GK, 1024)], "tm", gp.epi("tm", (S["gk"], 0)))
        if not KN.get("skip_tm") and not KN.get("skip_tm2"):
            gemm(b, cx, "in_tm_gv", S["hT_bf"], D, 0, TT, [(Win, O_GV, 2048)], "tm", gp.epi("tm", (S["gv"], 0), dtype=BF16))
            gemm(b, cx, "in_tm_gr", S["hT_bf"], D, 0, T, [(Win, O_GR, 2048)], "tm", gp.epi("tm", (S["gr"], 0)))
        bmg = gp.sb("bmg", [128, 64], F32)
        tk_bmg = b.dma("sp", bmg[:, :], I["b_merge"][:, :], inc=b.sem("c_bmg"))

        def fn_sig(eng, o, src, tmp, info, w, si, ci):
            fc = info[0] // 128
            return b.op("act", lambda e: e.activation(out=o, in_=src, func=AF.Sigmoid, bias=bmg[:, fc:fc + 1], scale=1.0),
                        waits=w + [tk_bmg], inc=sA)
        gemm(b, cx, "in_gate", S["hT_bf"], D, 0, T, [(Win, O_MG, 2 * D)], "fm",
             gp.epi("fm", [(0, 2 * D, S["gateT"], 0)], fn=fn_sig))
        gp.close("p1")
        b.recycle()
        if stop_after == "p1":
            return nc

        def phase_reset():
            b.recycle()
            cx.pfree = [None, None]
            cx.wfree = [None] * cx.WR
            cx.at_free = None

        def rms_fm(tag, src, F, N, gname, dst):
            FC = F // 128
            with ExitStack() as st_:
                sbp = lambda n, s_, d: st_.enter_context(nc.sbuf_tensor(f"sb{tag}_{n}", s_, d))
                xin = [sbp(f"x{i}", [128, FC, 512], F32) for i in range(2)]
                sq = sbp("sq", [128, FC, 512], BF16)
                ones = sbp("ones", [128, 128], BF16)
                sd = sbp("sd", [128, 512], F32)
                R = sbp("R", [128, 512], F32)
                ob = [sbp(f"o{i}", [128, FC, 512], BF16) for i in range(2)]
                gc = sbp("g", [128, FC], F32)
                eps = sbp("eps", [128, 1], F32)
                xs = [b.sem(f"{tag}_xs{i}") for i in range(2)]
                os_ = [b.sem(f"{tag}_os{i}") for i in range(2)]
                tkg = b.dma("sp", gc[:, :], I[gname][:, :], inc=b.sem(f"{tag}_cg"))
                tk1 = b.op("dve", lambda e: e.memset(ones[:, :], 1.0), inc=sD)
                tk1 = b.op("dve", lambda e: e.memset(eps[:, :], RMS_EPS), inc=sD)
                x_free = [None, None]
                o_free = [None, None]
                sq_free = None
                ps_free = None
                sd_free = None
                srcv, dstv = kc_view(src), kc_view(dst)
                ps = cx.psum[0]
                for i in range(N // 512):
                    s2 = i % 2
                    tkx = b.dma("sp", xin[s2][:, :, :], srcv[:, :, i * 512:(i + 1) * 512], waits=[x_free[s2]], inc=xs[s2])
                    tksq = b.op("act", lambda e, s2=s2: e.activation(out=sq[:, :, :], in_=xin[s2][:, :, :], func=AF.Square),
                                waits=[tkx, sq_free], inc=sA)
                    for fc in range(FC):
                        tkp = b.op("pe", lambda e, fc=fc: e.matmul(ps[:, :], lhsT=ones[:, :], rhs=sq[:, fc, :],
                                                                    start=(fc == 0), stop=(fc == FC - 1)),
                                   waits=[tksq, tk1, ps_free] if fc == 0 else [], inc=cx.pe_sem if fc == FC - 1 else None)
                    sq_free = tkp
                    tksd = b.op("act", lambda e: e.activation(out=sd[:, :], in_=ps[:, :], func=AF.Sqrt, bias=eps[:, :], scale=1.0 / F),
                                waits=[tkp, sd_free], inc=sA)
                    ps_free = tksd
                    tkr = b.op("dve", lambda e: e.reciprocal(out=R[:, :], in_=sd[:, :]), waits=[tksd], inc=sD)
                    sd_free = tkr
                    for fc in range(FC):
                        tko = b.op("dve", lambda e, fc=fc, s2=s2: e.scalar_tensor_tensor(
                            out=ob[s2][:, fc, :], in0=xin[s2][:, fc, :], scalar=gc[:, fc:fc + 1], in1=R[:, :],
                            op0=ALU.mult, op1=ALU.mult),
                            waits=[tkr, tkg, o_free[s2]] if fc == 0 else [], inc=sD if fc == FC - 1 else None)
                    x_free[s2] = tko
                    o_free[s2] = b.dma("sp", dstv[:, :, i * 512:(i + 1) * 512], ob[s2][:, :, :], waits=[tko], inc=os_[s2])
                b.wait("sp", o_free)
                b.flush(tag)
            phase_reset()

        rms_fm("p2q", S["cqT"], 1024, T, "q_norm", S["cqnT"])
        rms_fm("p2kv", S["ckvT"], 512, TT, "kv_norm", S["ckvnT"])
        with ExitStack() as st_:
            sbp = lambda n, s_, d: st_.enter_context(nc.sbuf_tensor(f"p2r_{n}", s_, d))
            kr = [sbp(f"kr{i}", [64, 512], F32) for i in range(2)]
            rc = [sbp(f"rc{i}", [64, 512], F32) for i in range(2)]
            rs = [sbp(f"rs{i}", [64, 512], F32) for i in range(2)]
            pr = sbp("pr", [64, 512], F32)
            t2 = sbp("t2", [64, 512], F32)
            ko = [sbp(f"ko{i}", [64, 512], BF16) for i in range(2)]
            ls = [b.sem(f"p2r_ls{i}") for i in range(2)]
            ss = [b.sem(f"p2r_ss{i}") for i in range(2)]
            l_free = [None, None]
            o_free = [None, None]
            for i in range(TT // 512):
                s2 = i % 2
                sl = slice(i * 512, (i + 1) * 512)
                b.dma("sp", kr[s2][:, :], S["krT"][:, sl], waits=[l_free[s2]], inc=ls[s2])
                b.dma("sp", rc[s2][:, :], I["ropeC"][0:64, sl], waits=[l_free[s2]], inc=ls[s2])
                tkl = b.dma("sp", rs[s2][:, :], I["ropeS"][0:64, sl], waits=[l_free[s2]], inc=ls[s2])
                tk = b.op("dve", lambda e, s2=s2: e.tensor_tensor(out=pr[:, :], in0=kr[s2][:, :], in1=rc[s2][:, :], op=ALU.mult),
                          waits=[tkl], inc=sD)
                tk = b.op("dve", lambda e, s2=s2: e.tensor_copy(out=t2[0:32, :], in_=kr[s2][32:64, :]), waits=[tk], inc=sD)
                tk = b.op("dve", lambda e, s2=s2: e.tensor_copy(out=t2[32:64, :], in_=kr[s2][0:32, :]), waits=[tk], inc=sD)
                tk = b.op("dve", lambda e, s2=s2: e.tensor_tensor(out=t2[:, :], in0=t2[:, :], in1=rs[s2][:, :], op=ALU.mult),
                          waits=[tk], inc=sD)
                l_free[s2] = tk
                tk = b.op("dve", lambda e, s2=s2: e.tensor_tensor(out=ko[s2][:, :], in0=pr[:, :], in1=t2[:, :], op=ALU.add),
                          waits=[tk, o_free[s2]], inc=sD)
                o_free[s2] = b.dma("sp", S["kropeT"][:, sl], ko[s2][:, :], waits=[tk], inc=ss[s2])
            b.wait("sp", o_free)
            b.flush("p2r")
        phase_reset()
        if stop_after == "p2":
            return nc

        gp = GP("p3")
        ropeC = gp.sb("ropeC", [128, T], F32)
        ropeS = gp.sb("ropeS", [128, T], F32)
        rtmp = gp.sb("rtmp", [128, 2048], F32)
        tk_rq = b.dma("sp", ropeC[:, :], I["ropeC"][:, 0:T], inc=b.sem("c_rqc"))
        tk_rq2 = b.dma("sp", ropeS[:, :], I["ropeS"][:, 0:T], inc=b.sem("c_rqs"))
        Wq, Wkv = I["w_uq"], I["w_ukv"]

        def fn_q(eng, o, src, tmp, info, w, si, ci):
            vc0, cw, t0, nt = info
            ci = (t0 % 1024) // 512 + 2 * ((vc0 // 128) % 2)
            tk = None
            tks_ = []
            for r0 in (0, 64):
                if (vc0 + r0) % 192 < 128:
                    tk = b.op("act", lambda e, r0=r0: e.activation(out=o[r0:r0 + 64, :], in_=src[r0:r0 + 64, :], func=AF.Copy),
                              waits=w, inc=sA)
                else:
                    sw = rtmp[:, ci * 512:ci * 512 + nt]
                    tk = b.op("dve", lambda e, r0=r0: e.tensor_tensor(out=tmp[r0:r0 + 64, :], in0=src[r0:r0 + 64, :],
                                                                       in1=ropeC[r0:r0 + 64, t0:t0 + nt], op=ALU.mult),
                              waits=w + [tk_rq, tk_rq2], inc=sD)
                    tk = b.op("dve", lambda e, r0=r0, sw=sw: e.tensor_copy(out=sw[r0:r0 + 32, :], in_=src[r0 + 32:r0 + 64, :]), waits=[tk], inc=sD)
                    tk = b.op("dve", lambda e, r0=r0, sw=sw: e.tensor_copy(out=sw[r0 + 32:r0 + 64, :], in_=src[r0:r0 + 32, :]), waits=[tk], inc=sD)
                    tk = b.op("dve", lambda e, r0=r0, sw=sw: e.tensor_tensor(out=sw[r0:r0 + 64, :], in0=sw[r0:r0 + 64, :],
                                                                              in1=ropeS[r0:r0 + 64, t0:t0 + nt], op=ALU.mult), waits=[tk], inc=sD)
                    tk = b.op("dve", lambda e, r0=r0, sw=sw: e.tensor_tensor(out=o[r0:r0 + 64, :], in0=tmp[r0:r0 + 64, :],
                                                                              in1=sw[r0:r0 + 64, :], op=ALU.add), waits=[tk], inc=sD)
                tks_.append(tk)
            return tks_
        qd = []
        for h in range(16):
            qd += [(h * 192, 128, S["qnT"], h * 128), (h * 192 + 128, 64, S["qrT"], h * 64)]
        gemm(b, cx, "q_up", S["cqnT"], 1024, 0, T, [(Wq, 0, 3072)], "fm", gp.epi("fm", qd, dtype=BF16, fn=fn_q))
        gemm(b, cx, "kn_up", S["ckvnT"], 512, 0, TT, [(Wkv, h * 256, 128) for h in range(16)], "fm",
             gp.epi("fm", [(0, 2048, S["knT"], 0)], dtype=BF16))
        gemm(b, cx, "v_up", S["ckvnT"], 512, 0, TT, [(Wkv, h * 256 + 128, 128) for h in range(16)], "tm",
             gp.epi("tm", (S["vmla"], 0), dtype=BF16))
        gp.close("p3")
        phase_reset()
        if stop_after == "p3":
            return nc

        def attn(tag, nh, qparts, kparts, vsrc, nk, ndv, scale, masked, out_ap):
            NKC = nk // 128
            NP = len(qparts(0))
            with ExitStack() as st_:
                sbp = lambda n, s_, d: st_.enter_context(nc.sbuf_tensor(f"sb{tag}_{n}", s_, d))
                kt = [sbp(f"kt{i}", [128, NP, nk], BF16) for i in range(2)]
                vt = [sbp(f"vt{i}", [128, NKC, ndv * 128], BF16) for i in range(2)]
                qt = [sbp(f"qt{i}", [128, NP, 512], BF16) for i in range(2)]
                pb = [sbp(f"pb{i}", [128, NKC, 512], BF16) for i in range(2)]
                rl = sbp("rl", [128, 512], F32)
                ob = [sbp(f"ob{i}", [128, ndv, 512], BF16) for i in range(2)]
                ones = sbp("ones", [128, 128], BF16)
                zero = sbp("zero", [128, 1], F32)
                tk1 = b.op("dve", lambda e: e.memset(ones[:, :], 1.0), inc=sD)
                tk1 = b.op("dve", lambda e: e.memset(zero[:, :], 0.0), inc=sD)
                ks = [b.sem(f"{tag}_ks{i}") for i in range(2)]
                qs = [b.sem(f"{tag}_qs{i}") for i in range(2)]
                oss = [b.sem(f"{tag}_os{i}") for i in range(2)]
                ps_s = cx.psum[0:3]
                ps_o = cx.psum[3:5]
                ps_l = cx.psum[5:7]
                s_free = [None] * 3
                o_freeP = [None, None]
                l_freeP = [None, None]
                k_free = [None, None]
                q_free = [None, None]
                pb_free = [None, None]
                ob_free = [None, None]
                rl_free = None
                nS = 0
                nO = 0
                nQ = 0
                for h in range(nh):
                    hs = h % 2
                    tkk = None
                    for p, (ap, r0, rows) in enumerate(kparts(h)):
                        tkk = b.dma("sp", kt[hs][0:rows, p, :], ap[r0:r0 + rows, 0:nk], waits=[k_free[hs]], inc=ks[hs])
                    vv = vsrc(h).rearrange("(kc p) n -> p kc n", p=128)
                    for k0 in range(0, NKC, 8):
                        k1 = min(NKC, k0 + 8)
                        tkk = b.dma("sp", vt[hs][:, k0:k1, :], vv[:, k0:k1, :], waits=[k_free[hs]], inc=ks[hs])
                    last_pe = None
                    for qi in range(T // 512):
                        q2 = nQ % 2
                        nQ += 1
                        tkq = None
                        for p, (ap, r0, rows) in enumerate(qparts(h)):
                            tkq = b.dma("sp", qt[q2][0:rows, p, :], ap[r0:r0 + rows, qi * 512:(qi + 1) * 512],
                                        waits=[q_free[q2]], inc=qs[q2])
                        tkp_last = None
                        for kc in range(NKC):
                            sb_ = nS % 3
                            nS += 1
                            for p, (ap, r0, rows) in enumerate(kparts(h)):
                                tks = b.op("pe", lambda e, p=p, rows=rows, kc=kc, sb_=sb_, hs=hs, q2=q2: e.matmul(
                                    ps_s[sb_][:, :], lhsT=kt[hs][0:rows, p, kc * 128:(kc + 1) * 128],
                                    rhs=qt[q2][0:rows, p, :], start=(p == 0), stop=(p == NP - 1)),
                                    waits=[tkk, tkq, s_free[sb_]] if p == 0 else [],
                                    inc=cx.pe_sem if p == NP - 1 else None)
                            bias = cvec[:, 0:1] if (masked and kc >= NKC // 2) else zero[:, :]
                            tkp_last = b.op("act", lambda e, kc=kc, sb_=sb_, q2=q2, bias=bias: e.activation(
                                out=pb[q2][:, kc, :], in_=ps_s[sb_][:, :], func=AF.Exp, bias=bias, scale=scale),
                                waits=[tks, tk1, tk_cv, pb_free[q2]] if kc == 0 else [tks], inc=sA)
                            s_free[sb_] = tkp_last
                        q_free[q2] = tks
                        l2 = nO % 2
                        for kc in range(NKC):
                            tkl = b.op("pe", lambda e, kc=kc, l2=l2, q2=q2: e.matmul(
                                ps_l[l2][:, :], lhsT=ones[:, :], rhs=pb[q2][:, kc, :], start=(kc == 0), stop=(kc == NKC - 1)),
                                waits=[tkp_last, l_freeP[l2]] if kc == 0 else [], inc=cx.pe_sem if kc == NKC - 1 else None)
                        tkrl = b.op("dve", lambda e, l2=l2: e.reciprocal(out=rl[:, :], in_=ps_l[l2][:, :]), waits=[tkl, rl_free], inc=sD)
                        l_freeP[l2] = tkrl
                        o2 = nQ % 2
                        tko = None
                        for dvc in range(ndv):
                            ob2 = nO % 2
                            nO += 1
                            for kc in range(NKC):
                                tkpv = b.op("pe", lambda e, kc=kc, dvc=dvc, ob2=ob2, hs=hs, q2=q2: e.matmul(
                                    ps_o[ob2][:, :], lhsT=vt[hs][:, kc, dvc * 128:(dvc + 1) * 128], rhs=pb[q2][:, kc, :],
                                    start=(kc == 0), stop=(kc == NKC - 1)),
                                    waits=[o_freeP[ob2]] if kc == 0 else [], inc=cx.pe_sem if kc == NKC - 1 else None)
                            tko = b.op("dve", lambda e, dvc=dvc, ob2=ob2, o2=o2: e.tensor_tensor(
                                out=ob[o2][:, dvc, :], in0=ps_o[ob2][:, :], in1=rl[:, :], op=ALU.mult),
                                waits=[tkpv, tkrl, ob_free[o2]] if dvc == 0 else [tkpv], inc=sD)
                            o_freeP[ob2] = tko
                        rl_free = tko
                        pb_free[q2] = tkpv
                        last_pe = tkpv
                        outv = out_ap[h * ndv * 128:(h + 1) * ndv * 128, qi * 512:(qi + 1) * 512].rearrange("(c p) t -> p c t", p=128)
                        ob_free[o2] = b.dma("sp", outv, ob[o2][:, :, :], waits=[tko], inc=oss[o2])
                    k_free[hs] = last_pe
                b.wait("sp", ob_free)
                b.flush(tag)
            phase_reset()

        def attn_sp(tag, nh, qparts, kparts, vsrc, nk, scale, masked, out_ap):
            NKC = nk // 128
            NP = len(qparts(0))
            NQ = T // 512
            with ExitStack() as st_:
                sbp = lambda n, s_, d: st_.enter_context(nc.sbuf_tensor(f"sb{tag}_{n}", s_, d))
                kt = [sbp(f"kt{i}", [128, NP, nk], BF16) for i in range(2)]
                vt = [sbp(f"vt{i}", [128, NKC, 128], BF16) for i in range(2)]
                qt = [sbp(f"qt{i}", [128, NP, 512], BF16) for i in range(2)]
                pb = [sbp(f"pb{i}", [128, NKC, 512], BF16) for i in range(2)]
                rl = sbp("rl", [128, 512], F32)
                ob = [sbp(f"ob{i}", [128, 512], BF16) for i in range(2)]
                ones = sbp("ones", [128, 128], BF16)
                zero = sbp("zero", [128, 1], F32)
                tk1 = b.op("dve", lambda e: e.memset(ones[:, :], 1.0), inc=sD)
                tk1 = b.op("dve", lambda e: e.memset(zero[:, :], 0.0), inc=sD)
                ks = [b.sem(f"{tag}_ks{i}") for i in range(2)]
                qs = [b.sem(f"{tag}_qs{i}") for i in range(2)]
                oss = [b.sem(f"{tag}_os{i}") for i in range(2)]
                ps_s = cx.psum[0:3]
                ps_o = cx.psum[3:5]
                ps_l = cx.psum[5:7]
                s_free = [None] * 3
                o_freeP = [None, None]
                l_freeP = [None, None]
                k_free = [None, None]
                q_free = [None, None]
                pb_free = [None, None]
                ob_free = [None, None]
                rl_free = None
                nS = 0
                N = nh * NQ
                tkk_h = {}
                exp_last = {}
                for n in range(N + 1):
                    if n < N:
                        h, qi = divmod(n, NQ)
                        hs, q2 = h % 2, n % 2
                        if qi == 0:
                            tkk = None
                            for p, (ap, r0, rows) in enumerate(kparts(h)):
                                tkk = b.dma("sp", kt[hs][0:rows, p, :], ap[r0:r0 + rows, 0:nk], waits=[k_free[hs]], inc=ks[hs])
                            vv = vsrc(h).rearrange("(kc p) n -> p kc n", p=128)
                            for k0 in range(0, NKC, 8):
                                k1 = min(NKC, k0 + 8)
                                tkk = b.dma("sp", vt[hs][:, k0:k1, :], vv[:, k0:k1, :], waits=[k_free[hs]], inc=ks[hs])
                            tkk_h[h] = tkk
                        tkk = tkk_h[h]
                        tkq = None
                        for p, (ap, r0, rows) in enumerate(qparts(h)):
                            tkq = b.dma("sp", qt[q2][0:rows, p, :], ap[r0:r0 + rows, qi * 512:(qi + 1) * 512],
                                        waits=[q_free[q2]], inc=qs[q2])
                    if n >= 1:
                        m = n - 1
                        hm, qm = divmod(m, NQ)
                        hsm, m2 = hm % 2, m % 2
                    tks = tkpv = tkl = None
                    for kc in range(NKC):
                        if n < N:
                            sb_ = nS % 3
                            nS += 1
                            for p, (ap, r0, rows) in enumerate(kparts(h)):
                                tks = b.op("pe", lambda e, p=p, rows=rows, kc=kc, sb_=sb_, hs=hs, q2=q2: e.matmul(
                                    ps_s[sb_][:, :], lhsT=kt[hs][0:rows, p, kc * 128:(kc + 1) * 128],
                                    rhs=qt[q2][0:rows, p, :], start=(p == 0), stop=(p == NP - 1)),
                                    waits=[tkk, tkq, s_free[sb_]] if p == 0 else [],
                                    inc=cx.pe_sem if p == NP - 1 else None)
                            bias = cvec[:, 0:1] if (masked and kc >= NKC // 2) else zero[:, :]
                            tke = b.op("act", lambda e, kc=kc, sb_=sb_, q2=q2, bias=bias: e.activation(
                                out=pb[q2][:, kc, :], in_=ps_s[sb_][:, :], func=AF.Exp, bias=bias, scale=scale),
                                waits=[tks, tk1, tk_cv, pb_free[q2]] if kc == 0 else [tks], inc=sA)
                            s_free[sb_] = tke
                            exp_last[n] = tke
                        if n >= 1:
                            tkl = b.op("pe", lambda e, kc=kc, m2=m2: e.matmul(
                                ps_l[m2][:, :], lhsT=ones[:, :], rhs=pb[m2][:, kc, :], start=(kc == 0), stop=(kc == NKC - 1)),
                                waits=[exp_last[m], l_freeP[m2]] if kc == 0 else [], inc=cx.pe_sem if kc == NKC - 1 else None)
                            tkpv = b.op("pe", lambda e, kc=kc, m2=m2, hsm=hsm: e.matmul(
                                ps_o[m2][:, :], lhsT=vt[hsm][:, kc, :], rhs=pb[m2][:, kc, :], start=(kc == 0), stop=(kc == NKC - 1)),
                                waits=[o_freeP[m2]] if kc == 0 else [], inc=cx.pe_sem if kc == NKC - 1 else None)
                    if n < N:
                        q_free[q2] = tks
                    if n >= 1:
                        tkrl = b.op("dve", lambda e, m2=m2: e.reciprocal(out=rl[:, :], in_=ps_l[m2][:, :]), waits=[tkl, rl_free], inc=sD)
                        l_freeP[m2] = tkrl
                        tko = b.op("dve", lambda e, m2=m2: e.tensor_tensor(out=ob[m2][:, :], in0=ps_o[m2][:, :], in1=rl[:, :], op=ALU.mult),
                                   waits=[tkpv, tkrl, ob_free[m2]], inc=sD)
                        o_freeP[m2] = tko
                        rl_free = tko
                        pb_free[m2] = tkpv
                        if qm == NQ - 1:
                            k_free[hsm] = tkpv
                        ob_free[m2] = b.dma("sp", out_ap[hm * 128:(hm + 1) * 128, qm * 512:(qm + 1) * 512], ob[m2][:, :],
                                            waits=[tko], inc=oss[m2])
                    b.maybe_flush(4000)
                b.wait("sp", ob_free)
                b.flush(tag)
            phase_reset()

        attn_sp("p4", 16,
                lambda h: [(S["qnT"], h * 128, 128), (S["qrT"], h * 64, 64)],
                lambda h: [(S["knT"], h * 128, 128), (S["kropeT"], 0, 64)],
                lambda h: S["vmla"][:, h * 128:(h + 1) * 128], TT, 192.0 ** -0.5, True, S["omlaT"])
        if stop_after == "p4":
            return nc

        scratch("qdT", [2, 1024, T], BF16)
        scratch("kiT", [2, 1024, T], BF16)
        scratch("kte", [2, TT, 1024], BF16)
        scratch("ofwd", [T, 2048], F32)
        scratch("ogla", [T, 2048], F32)
        with ExitStack() as st5:
            sb5 = lambda n, s_, d: st5.enter_context(nc.sbuf_tensor(f"p5_{n}", s_, d))
            dec = sb5("dec", [128, 2, 8, 64], F32)
            with ExitStack() as st_:
                sbp = lambda n, s_, d: st_.enter_context(nc.sbuf_tensor(f"p5a_{n}", s_, d))
                lrA = [sbp(f"lrA{d_}", [32, TT], F32) for d_ in range(2)]
                w2b = [sbp(f"w2b{d_}", [32, 1024], F32) for d_ in range(2)]
                M1 = [sbp(f"M1{d_}", [128, 130], F32) for d_ in range(2)]
                M2 = [sbp(f"M2{d_}", [128, 128], F32) for d_ in range(2)]
                onec = sbp("onec", [128, 1], F32)
                la = sbp("la", [128, 1024], F32)
                E1 = sbp("E1", [128, 8, 128], F32)
                E2 = sbp("E2", [128, 8, 128], F32)
                E3 = sbp("E3", [128, 1024], F32)
                gqb = [sbp(f"gqb{i}", [128, 8, 128], F32) for i in range(2)]
                gkb = [sbp(f"gkb{i}", [128, 8, 128], F32) for i in range(2)]
                gkt = [sbp(f"gkt{i}", [128, 1024], F32) for i in range(2)]
                qdo = [sbp(f"qdo{i}", [128, 8, 128], BF16) for i in range(2)]
                kio = [sbp(f"kio{i}", [128, 8, 128], BF16) for i in range(2)]
                kto = [sbp(f"kto{i}", [128, 1024], BF16) for i in range(2)]
                lsem = [b.sem(f"p5a_l{i}") for i in range(2)]
                osem = [b.sem(f"p5a_o{i}") for i in range(2)]
                tkc = b.op("dve", lambda e: e.memset(onec[:, :], 1.0), inc=sD)
                tkm = []
                for d_ in range(2):
                    t_ = b.op("dve", lambda e, d_=d_: e.memset(lrA[d_][:, :], 1.0), inc=sD)
                    tkm.append(b.dma("sp", lrA[d_][0:16, :], S["glrT"][d_ * 32:d_ * 32 + 16, :], waits=[t_], inc=b.sem(f"c5lr{d_}")))
                    tkm.append(b.dma("sp", w2b[d_][0:17, :], I["w2b"][d_, :, :], inc=b.sem(f"c5w{d_}")))
                    tkm.append(b.dma("sp", M1[d_][:, :], I["M1"][d_, :, :], inc=b.sem(f"c5m1{d_}")))
                    tkm.append(b.dma("sp", M2[d_][:, :], I["M2"][d_, :, :], inc=b.sem(f"c5m2{d_}")))
                psg = cx.psum[0:2]
                psc = cx.psum[2:6]
                psd = cx.psum[6:8]
                l_free = [None, None]
                o_free = [None, None]
                psg_free = None
                psc_free = None
                psd_free = None
                la_free = None
                e_free = None
                e3_free = None
                gqT_v, gkT_v = kc_view(S["gqT"]), kc_view(S["gkT"])
                it = 0
                for blk in range(TT // 128):
                    own = blk < T // 128
                    s2 = blk % 2
                    tsl = slice(blk * 128, (blk + 1) * 128)
                    tkl = None
                    if own:
                        b.dma("sp", gqb[s2][:, :, :], gqT_v[:, :, tsl], waits=[l_free[s2]], inc=lsem[s2])
                        b.dma("sp", gkb[s2][:, :, :], gkT_v[:, :, tsl], waits=[l_free[s2]], inc=lsem[s2])
                    tkl = b.dma("sp", gkt[s2][:, :], S["gk"][tsl, :], waits=[l_free[s2]], inc=lsem[s2])
                    for d_ in range(2):
                        o2 = it % 2
                        it += 1
                        for hf_ in range(2):
                            tkg = b.op("pe", lambda e, d_=d_, hf_=hf_, tsl=tsl: e.matmul(
                                psg[hf_][:, :], lhsT=lrA[d_][0:17, tsl], rhs=w2b[d_][0:17, hf_ * 512:(hf_ + 1) * 512],
                                start=True, stop=True), waits=[tkm, psg_free] if hf_ == 0 else [], inc=cx.pe_sem if hf_ == 1 else None)
                        for hf_ in range(2):
                            tke = b.op("act", lambda e, hf_=hf_: e.activation(out=la[:, hf_ * 512:(hf_ + 1) * 512], in_=psg[hf_][:, :],
                                                                               func=AF.Exp, scale=-1.0),
                                       waits=[tkg, la_free] if hf_ == 0 else [], inc=sA)
                        psg_free = tke
                        tkla = b.op("act", lambda e: e.activation(out=la[:, :], in_=la[:, :], func=AF.Ln, bias=onec[:, :], scale=1.0),
                                    waits=[tke, tkc], inc=sA)
                        for fc in range(8):
                            tkcm = b.op("pe", lambda e, d_=d_, fc=fc: e.matmul(
                                psc[fc // 2][:, (fc % 2) * 256:(fc % 2) * 256 + 130], lhsT=la[:, fc * 128:(fc + 1) * 128], rhs=M1[d_][:, :],
                                start=True, stop=True), waits=[tkla, psc_free] if fc == 0 else [], inc=cx.pe_sem if fc == 7 else None)
                        for hf_ in range(2):
                            tkdm = b.op("pe", lambda e, d_=d_, hf_=hf_: e.matmul(
                                psd[hf_][:, :], lhsT=M2[d_][:, :], rhs=la[:, hf_ * 512:(hf_ + 1) * 512], start=True, stop=True),
                                waits=[psd_free] if hf_ == 0 else [], inc=cx.pe_sem if hf_ == 1 else None)
                        la_free = tkdm
                        for bk in range(4):
                            tkdc = b.op("act", lambda e, d_=d_, bk=bk, blk=blk: e.activation(
                                out=dec[:, d_, 2 * bk:2 * bk + 2, 2 * blk:2 * blk + 2],
                                in_=psc[bk][:, :].rearrange("p (a c) -> p a c", a=2)[:, :, 128:130], func=AF.Exp),
                                waits=[tkcm] if bk == 0 else [], inc=sA)
                        last_c = tkdc
                        if own:
                            for bk in range(4):
                                src = psc[bk][:, :].rearrange("p (a c) -> p a c", a=2)[:, :, 0:128]
                                b.op("act", lambda e, bk=bk, src=src: e.activation(out=E1[:, 2 * bk:2 * bk + 2, :], in_=src, func=AF.Exp),
                                     waits=[e_free] if bk == 0 else [])
                                last_c = b.op("act", lambda e, bk=bk, src=src: e.activation(out=E2[:, 2 * bk:2 * bk + 2, :], in_=src, func=AF.Exp, scale=-1.0),
                                              inc=sA)
                        psc_free = last_c
                        for hf_ in range(2):
                            tke3 = b.op("act", lambda e, hf_=hf_: e.activation(out=E3[:, hf_ * 512:(hf_ + 1) * 512], in_=psd[hf_][:, :], func=AF.Exp),
                                        waits=[tkdm, e3_free] if hf_ == 0 else [], inc=sA)
                        psd_free = tke3
                        tko = None
                        if own:
                            b.op("dve", lambda e, s2=s2, o2=o2: e.scalar_tensor_tensor(
                                out=qdo[o2][:, :, :], in0=gqb[s2][:, :, :], scalar=1.0 / 16.0, in1=E1[:, :, :], op0=ALU.mult, op1=ALU.mult),
                                waits=[last_c, tkl, o_free[o2]])
                            tko = b.op("dve", lambda e, s2=s2, o2=o2: e.tensor_tensor(
                                out=kio[o2][:, :, :], in0=gkb[s2][:, :, :], in1=E2[:, :, :], op=ALU.mult), inc=sD)
                            e_free = tko
                        tko = b.op("dve", lambda e, s2=s2, o2=o2: e.tensor_tensor(
                            out=kto[o2][:, :], in0=gkt[s2][:, :], in1=E3[:, :], op=ALU.mult),
                            waits=[tke3, tkl, o_free[o2]], inc=sD)
                        e3_free = tko
                        if own:
                            b.dma("sp", kc_view(S["qdT"][d_])[:, :, tsl], qdo[o2][:, :, :], waits=[tko], inc=osem[o2])
                            b.dma("sp", kc_view(S["kiT"][d_])[:, :, tsl], kio[o2][:, :, :], waits=[tko], inc=osem[o2])
                        o_free[o2] = b.dma("sp", S["kte"][d_][tsl, :], kto[o2][:, :], waits=[tko], inc=osem[o2])
                    l_free[s2] = tko
                    b.maybe_flush(2000)
                b.wait("sp", o_free)
                b.flush("p5a")
            phase_reset()
            if stop_after == "p5a":
                return nc

            with ExitStack() as st_:
                sbp = lambda n, s_, d: st_.enter_context(nc.sbuf_tensor(f"p5b_{n}", s_, d))
                Sf = sbp("S", [128, 8, 512], F32)
                Sb = [sbp(f"Sb{i}", [128, 8, 512], BF16) for i in range(2)]
                vb = [sbp(f"v{i}", [64, 2048], BF16) for i in range(3)]
                kb = [sbp(f"k{i}", [64, 1024], BF16) for i in range(3)]
                qd = [sbp(f"qd{i}", [128, 8, 64], BF16) for i in range(2)]
                ki = [sbp(f"ki{i}", [128, 8, 64], BF16) for i in range(2)]
                Ab = sbp("Ab", [64, 256], BF16)
                mk = [sbp(f"mk{d_}", [64, 256], F32) for d_ in range(2)]
                ost = [sbp(f"ost{i}", [64, 2048], F32) for i in range(2)]
                ofl = [sbp(f"ofl{i}", [64, 2048], F32) for i in range(2)]
                grl = [sbp(f"grl{i}", [64, 2048], F32) for i in range(2)]
                gnb = sbp("gnb", [64, 2048], F32)
                ogb = [sbp(f"ogb{i}", [64, 2048], F32) for i in range(2)]
                ssq = sbp("ssq", [64, 4], F32)
                rr = sbp("rr", [64, 4], F32)
                junk = sbp("junk", [64, 512], F32)
                epsr = sbp("epsr", [64, 1], F32)
                tk_e = b.op("dve", lambda e: e.memset(epsr[:, :], RMS_EPS), inc=sD)
                tkmk = [b.dma("sp", mk[d_][:, :], I["maskT"][d_, :, :], inc=b.sem(f"c5mk{d_}")) for d_ in range(2)]
                tkgn = b.dma("sp", gnb[:, :], I["gnorm_bc"][:, :], inc=b.sem("c5gn"))
                ldsem = [b.sem(f"p5b_ld{i}") for i in range(3)]
                qsem = [b.sem(f"p5b_q{i}") for i in range(2)]
                xsem = [b.sem(f"p5b_x{i}") for i in range(2)]
                stsem = [b.sem(f"p5b_st{i}") for i in range(2)]
                ps_a = cx.psum[0]
                ps_o = cx.psum[1:5]
                ps_kv = cx.psum[5:8]
                ld_free = [None] * 3
                q_free = [None, None]
                x_free = [None, None]
                st_free = [None, None]
                og_free = [None, None]
                a_free = None
                ab_free = None
                o_freeP = None
                kv_free = [None] * 3
                sb_ready = None
                sb_rd = [None, None]
                s_last = None
                cast_prev = [None] * 8
                cast_last = None
                nld = 0
                nq = 0
                nkv = 0
                nx = 0
                cur = 0
                for d_ in range(2):
                    tkz = b.op("dve", lambda e: e.memset(Sf[:, :, :], 0.0), waits=[s_last, cast_last], inc=sD)
                    tkz2 = b.op("dve", lambda e, cur=cur: e.memset(Sb[cur][:, :, :], 0.0), waits=[sb_rd[cur], cast_last], inc=sD)
                    s_last = tkz2
                    sb_ready = tkz2
                    cast_prev = [None] * 8
                    order = list(range(32)) if d_ == 0 else list(range(31, -1, -1))
                    seq = [("other", c) for c in order] + [("flag", 0)] + [("own", c) for c in order]
                    for (kind, c) in seq:
                        if kind == "flag":
                            tkf = b.op("dve", lambda e, d_=d_: e.tensor_scalar(out=Sf[:, :, :], in0=Sf[:, :, :], scalar1=cvec[:, 1 + d_:2 + d_],
                                                                               scalar2=None, op0=ALU.mult), waits=[s_last, tk_cv, cast_last], inc=sD)
                            s_last = tkf
                            tkf2 = b.op("act", lambda e, cur=cur: e.activation(out=Sb[cur][:, :, :], in_=Sf[:, :, :], func=AF.Copy),
                                        waits=[tkf, sb_rd[cur]], inc=sA)
                            sb_ready = tkf2
                            cast_last = tkf2
                            cast_prev = [tkf2] * 8
                            continue
                        gch = c + (32 if kind == "other" else 0)
                        r0 = gch * 64
                        l3 = nld % 3
                        nld += 1
                        b.dma("sp", vb[l3][:, :], S["gv"][r0:r0 + 64, :], waits=[ld_free[l3]], inc=ldsem[l3])
                        tkld = b.dma("sp", kb[l3][:, :], S["kte"][d_][r0:r0 + 64, :], waits=[ld_free[l3]], inc=ldsem[l3])
                        nxt = 1 - cur
                        if kind == "own":
                            q2 = nq % 2
                            nq += 1
                            b.dma("sp", qd[q2][:, :, :], kc_view(S["qdT"][d_])[:, :, r0:r0 + 64], waits=[q_free[q2]], inc=qsem[q2])
                            tkq = b.dma("sp", ki[q2][:, :, :], kc_view(S["kiT"][d_])[:, :, r0:r0 + 64], waits=[q_free[q2]], inc=qsem[q2])
                            for h in range(4):
                                for dc in range(2):
                                    tka = b.op("pe", lambda e, h=h, dc=dc, q2=q2: e.matmul(
                                        ps_a[0:64, h * 64:(h + 1) * 64], lhsT=ki[q2][:, h * 2 + dc, :], rhs=qd[q2][:, h * 2 + dc, :],
                                        start=(dc == 0), stop=(dc == 1)),
                                        waits=[tkq, a_free] if (h == 0 and dc == 0) else [], inc=cx.pe_sem if (h == 3 and dc == 1) else None)
                            tkab = b.op("dve", lambda e, d_=d_: e.tensor_tensor(out=Ab[:, :], in0=ps_a[0:64, 0:256], in1=mk[d_][:, :], op=ALU.mult),
                                        waits=[tka, tkmk, ab_free], inc=sD)
                            a_free = tkab
                            for h in range(4):
                                for dc in range(2):
                                    b.op("pe", lambda e, h=h, dc=dc, q2=q2, cur=cur: e.matmul(
                                        ps_o[h][0:64, :], lhsT=qd[q2][:, h * 2 + dc, :], rhs=Sb[cur][:, h * 2 + dc, :],
                                        start=(dc == 0), stop=False),
                                        waits=[sb_ready, o_freeP] if (h == 0 and dc == 0) else [])
                                tko_ = b.op("pe", lambda e, h=h, l3=l3: e.matmul(
                                    ps_o[h][0:64, :], lhsT=Ab[0:64, h * 64:(h + 1) * 64], rhs=vb[l3][0:64, h * 512:(h + 1) * 512],
                                    start=False, stop=True), waits=[tkab, tkld] if h == 0 else [], inc=cx.pe_sem if h == 3 else None)
                            ab_free = tko_
                            q_free[q2] = tko_
                            sb_rd[cur] = tko_
                            x2 = nx % 2
                            nx += 1
                            if d_ == 0:
                                for h in range(4):
                                    tkev = b.op("act", lambda e, h=h, x2=x2: e.activation(out=ost[x2][:, h * 512:(h + 1) * 512], in_=ps_o[h][0:64, :], func=AF.Copy),
                                                waits=[tko_, st_free[x2]] if h == 0 else [], inc=sA)
                                o_freeP = tkev
                                st_free[x2] = b.dma("sp", S["ofwd"][r0:r0 + 64, :], ost[x2][:, :], waits=[tkev], inc=stsem[x2])
                            else:
                                b.dma("sp", ofl[x2][:, :], S["ofwd"][r0:r0 + 64, :], waits=[x_free[x2]], inc=xsem[x2])
                                tkx = b.dma("sp", grl[x2][:, :], S["gr"][r0:r0 + 64, :], waits=[x_free[x2]], inc=xsem[x2])
                                for h in range(4):
                                    tkad = b.op("dve", lambda e, h=h, x2=x2: e.tensor_tensor(
                                        out=ost[x2][:, h * 512:(h + 1) * 512], in0=ps_o[h][0:64, :], in1=ofl[x2][:, h * 512:(h + 1) * 512], op=ALU.add),
                                        waits=[tko_, tkx, st_free[x2]] if h == 0 else [], inc=sD)
                                o_freeP = tkad
                                for h in range(4):
                                    tksq = b.op("act", lambda e, h=h, x2=x2: e.activation(out=junk[:, :], in_=ost[x2][:, h * 512:(h + 1) * 512],
                                                                                            func=AF.Square, accum_out=ssq[:, h:h + 1]),
                                                waits=[tkad] if h == 0 else [], inc=sA)
                                tksd = b.op("act", lambda e: e.activation(out=rr[:, :], in_=ssq[:, :], func=AF.Sqrt, bias=epsr[:, :], scale=1.0 / 512.0),
                                            waits=[tksq, tk_e], inc=sA)
                                tksl = b.op("act", lambda e, x2=x2: e.activation(out=grl[x2][:, :], in_=grl[x2][:, :], func=AF.Silu), waits=[tkx], inc=sA)
                                tkrc = b.op("dve", lambda e: e.reciprocal(out=rr[:, :], in_=rr[:, :]), waits=[tksd], inc=sD)
                                tkg2 = b.op("dve", lambda e, x2=x2: e.tensor_tensor(out=grl[x2][:, :], in0=grl[x2][:, :], in1=gnb[:, :], op=ALU.mult),
                                            waits=[tksl, tkgn], inc=sD)
                                for h in range(4):
                                    tkfin = b.op("dve", lambda e, h=h, x2=x2: e.scalar_tensor_tensor(
                                        out=ogb[x2][:, h * 512:(h + 1) * 512], in0=ost[x2][:, h * 512:(h + 1) * 512], scalar=rr[:, h:h + 1],
                                        in1=grl[x2][:, h * 512:(h + 1) * 512], op0=ALU.mult, op1=ALU.mult),
                                        waits=[tkrc, tkg2, og_free[x2]] if h == 0 else [], inc=sD)
                                x_free[x2] = tkfin
                                st_free[x2] = tkfin
                                og_free[x2] = b.dma("sp", S["ogla"][r0:r0 + 64, :], ogb[x2][:, :], waits=[tkfin], inc=stsem[x2])
                        tkc_last = None
                        for h in range(4):
                            for dc in range(2):
                                hd = h * 2 + dc
                                k3 = nkv % 3
                                nkv += 1
                                tkkv = b.op("pe", lambda e, h=h, hd=hd, l3=l3, k3=k3: e.matmul(
                                    ps_kv[k3][:, :], lhsT=kb[l3][0:64, hd * 128:(hd + 1) * 128], rhs=vb[l3][0:64, h * 512:(h + 1) * 512],
                                    start=True, stop=True), waits=[tkld, kv_free[k3]], inc=cx.pe_sem)
                                tks = b.op("dve", lambda e, hd=hd, k3=k3, d_=d_, gch=gch: e.scalar_tensor_tensor(
                                    out=Sf[:, hd, :], in0=Sf[:, hd, :], scalar=dec[:, d_, hd, gch:gch + 1], in1=ps_kv[k3][:, :],
                                    op0=ALU.mult, op1=ALU.add), waits=[tkkv, s_last, cast_prev[hd]], inc=sD)
                                kv_free[k3] = tks
                                tkc_last = b.op("act", lambda e, hd=hd, nxt=nxt: e.activation(out=Sb[nxt][:, hd, :], in_=Sf[:, hd, :], func=AF.Copy),
                                                waits=[tks, sb_rd[nxt]] if hd == 0 else [tks], inc=sA)
                                cast_prev[hd] = tkc_last
                                cast_last = tkc_last
                        s_last = tks
                        ld_free[l3] = tkkv
                        sb_ready = tkc_last
                        cur = nxt
                        b.maybe_flush(2500)
                b.wait("sp", [st_free, og_free])
                b.flush("p5b")
            phase_reset()
        if stop_after == "p5b":
            return nc

        def tm2fm(tag, src, ncols, dst, nrows=T):
            NC_ = ncols // 128
            with ExitStack() as st_:
                sbp = lambda n, s_, d: st_.enter_context(nc.sbuf_tensor(f"sb{tag}_{n}", s_, d))
                xin = [sbp(f"x{i}", [128, ncols], F32) for i in range(2)]
                xo = [sbp(f"o{i}", [128, NC_, 128], BF16) for i in range(2)]
                ls = [b.sem(f"{tag}_l{i}") for i in range(2)]
                ss = [b.sem(f"{tag}_s{i}") for i in range(2)]
                l_free = [None, None]
                o_free = [None, None]
                pfree = [None] * 8
                dstv = kc_view(dst)
                nb = 0
                for i in range(nrows // 128):
                    s2 = i % 2
                    tkl = b.dma("sp", xin[s2][:, :], src[i * 128:(i + 1) * 128, :], waits=[l_free[s2]], inc=ls[s2])
                    tke = None
                    for q4 in range(NC_ // 4):
                        pi = nb % 8
                        nb += 1
                        bank = cx.psum[pi]
                        for j in range(4):
                            fc = q4 * 4 + j
                            tkt = b.op("pe", lambda e, fc=fc, j=j, s2=s2, bank=bank: e.transpose(
                                out=bank[:, j * 128:(j + 1) * 128], in_=xin[s2][:, fc * 128:(fc + 1) * 128], identity=ident[:, :]),
                                waits=[tkl, tk_id, pfree[pi]] if j == 0 else [], inc=cx.pe_sem if j == 3 else None)
                        eng = "dve" if q4 % 2 == 0 else "act"
                        o_ = xo[s2][:, q4 * 4:(q4 + 1) * 4, :]
                        src_ = bank[:, :].rearrange("p (a c) -> p a c", a=4)
                        if eng == "dve":
                            tke = b.op("dve", lambda e, o_=o_, src_=src_: e.tensor_copy(out=o_, in_=src_), waits=[tkt, o_free[s2]], inc=sD)
                        else:
                            tke = b.op("act", lambda e, o_=o_, src_=src_: e.activation(out=o_, in_=src_, func=AF.Copy), waits=[tkt, o_free[s2]], inc=sA)
                        pfree[pi] = tke
                    l_free[s2] = tkt
                    o_free[s2] = b.dma("sp", dstv[:, :, i * 128:(i + 1) * 128], xo[s2][:, :, :], waits=[(sD, sD.v), (sA, sA.v)], inc=ss[s2])
                b.wait("sp", o_free)
                b.flush(tag)
            phase_reset()

        tm2fm("p5t", S["ogla"], 2048, S["oglaT"])
        if stop_after == "p5":
            return nc

        scratch("pre1T", [D, T], F32)
        scratch("pre2T", [D, T], F32)
        scratch("pre3T", [D, T], F32)

        def fn_axpy(gp, src_dram, scale, which="A"):
            def fn(eng, o, src, tmp, info, w, si, ci):
                vc0, cw, t0, nt = info
                ldt, tkl = gp.ld(which, si, ci, src_dram[vc0:vc0 + cw, t0:t0 + nt], cw, nt)
                tk = b.op("dve", lambda e: e.scalar_tensor_tensor(out=o, in0=ldt, scalar=scale, in1=src, op0=ALU.mult, op1=ALU.add),
                          waits=w + [tkl], inc=sD)
                gp.ld_done(which, si, ci, tk)
                return tk
            return fn

        gp = GP("p6")

        def fn_bm(eng, o, src, tmp, info, w, si, ci):
            vc0, cw, t0, nt = info
            g0, tkl = gp.ld("A", si, ci, S["gateT"][vc0:vc0 + cw, t0:t0 + nt], cw, nt)
            tk = b.op("dve", lambda e: e.tensor_tensor(out=o, in0=src, in1=g0, op=ALU.mult), waits=w + [tkl], inc=sD)
            gp.ld_done("A", si, ci, tk)
            return tk

        def fn_bg(eng, o, src, tmp, info, w, si, ci):
            vc0, cw, t0, nt = info
            g1, tkl = gp.ld("A", si, ci, S["gateT"][D + vc0:D + vc0 + cw, t0:t0 + nt], cw, nt)
            bm, tkl2 = gp.ld("B", si, ci, S["bmT"][vc0:vc0 + cw, t0:t0 + nt], cw, nt)
            tk = b.op("dve", lambda e: e.tensor_tensor(out=tmp, in0=src, in1=g1, op=ALU.mult), waits=w + [tkl], inc=sD)
            gp.ld_done("A", si, ci, tk)
            tk = b.op("dve", lambda e: e.tensor_tensor(out=o, in0=tmp, in1=bm, op=ALU.add), waits=[tk, tkl2], inc=sD)
            gp.ld_done("B", si, ci, tk)
            return tk
        gemm(b, cx, "br_mla", S["omlaT"], 2048, 0, T, [(I["w_br_mla"], 0, D)], "fm", gp.epi("fm", [(0, D, S["bmT"], 0)], fn=fn_bm))
        b.wait("sp", gp.stg_free)
        b.flush("p6mid")
        gemm(b, cx, "br_gla", S["oglaT"], 2048, 0, T, [(I["w_br_gla"], 0, D)], "fm",
             gp.epi("fm", [(0, D, S["mergedT"], 0)], dtype=BF16, fn=fn_bg))
        gp.close("p6")
        phase_reset()
        if stop_after == "p6":
            return nc

        gp = GP("p7")
        gemm(b, cx, "mix_out", S["mergedT"], D, 0, T, [(I["w_mix_out"], 0, D)], "fm",
             gp.epi("fm", [(0, D, S["pre1T"], 0)], fn=fn_axpy(gp, S["hT_f32"], ALPHA)))
        gp.close("p7")
        phase_reset()

        def ln_fm(tag, pre, gname, bname, out_f32, out_bf, out_tm=None):
            TL = 256
            with ExitStack() as st_:
                sbp = lambda n, s_, d: st_.enter_context(nc.sbuf_tensor(f"sb{tag}_{n}", s_, d))
                X = [sbp(f"X{i}", [128, 32, TL], F32) for i in range(2)]
                Q = sbp("Q", [128, 32, TL], F32)
                Yb = sbp("Yb", [128, 32, TL], BF16)
                onesF = sbp("ones", [128, 128], F32)
                m = sbp("m", [128, TL], F32)
                msq = sbp("msq", [128, TL], F32)
                var = sbp("var", [128, TL], F32)
                rstd = sbp("rstd", [128, TL], F32)
                gc = sbp("g", [128, 32], F32)
                bc = sbp("b", [128, 32], F32)
                eps = sbp("eps", [128, 1], F32)
                yst = [sbp(f"yst{i}", [128, D], F32) for i in range(2)] if out_tm is not None else None
                tk1 = b.op("dve", lambda e: e.memset(onesF[:, :], 1.0), inc=sD)
                tk1 = b.op("dve", lambda e: e.memset(eps[:, :], LN_EPS), inc=sD)
                tkg = [b.dma("sp", gc[:, :], I[gname][:, :], inc=b.sem(f"{tag}_cg")),
                       b.dma("sp", bc[:, :], I[bname][:, :], inc=b.sem(f"{tag}_cb"))]
                xs = [b.sem(f"{tag}_xs{i}") for i in range(2)]
                osm = b.sem(f"{tag}_os")
                ysm = [b.sem(f"{tag}_ys{i}") for i in range(2)]
                x_free = [None, None]
                q_free = None
                yb_free = None
                ps_sum, ps_sq = cx.psum[0], cx.psum[1]
                st_free_ = None
                sq_free = None
                y_free = [None, None]
                tp_free = [None] * 4
                ntp = 0
                prev = kc_view(pre)
                for i in range(T // TL):
                    s2 = i % 2
                    tsl = slice(i * TL, (i + 1) * TL)
                    tkx = b.dma("sp", X[s2][:, :, :], prev[:, :, tsl], waits=[x_free[s2]], inc=xs[s2])
                    tkq = b.op("act", lambda e, s2=s2: e.activation(out=Q[:, :, :], in_=X[s2][:, :, :], func=AF.Square), waits=[tkx, q_free], inc=sA)
                    for fc in range(32):
                        tks = b.op("pe", lambda e, fc=fc, s2=s2: e.matmul(ps_sum[:, 0:TL], lhsT=onesF[:, :], rhs=X[s2][:, fc, :],
                                                                           start=(fc == 0), stop=(fc == 31)),
                                   waits=[tkx, tk1, st_free_] if fc == 0 else [], inc=cx.pe_sem if fc == 31 else None)
                    for fc in range(32):
                        tks2 = b.op("pe", lambda e, fc=fc: e.matmul(ps_sq[:, 0:TL], lhsT=onesF[:, :], rhs=Q[:, fc, :],
                                                                     start=(fc == 0), stop=(fc == 31)),
                                    waits=[tkq, sq_free] if fc == 0 else [], inc=cx.pe_sem if fc == 31 else None)
                    tkm = b.op("act", lambda e: e.activation(out=m[:, :], in_=ps_sum[:, 0:TL], func=AF.Copy, scale=1.0 / D), waits=[tks, y_free], inc=sA)
                    st_free_ = tkm
                    tkm2 = b.op("dve", lambda e: e.tensor_tensor(out=msq[:, :], in0=m[:, :], in1=m[:, :], op=ALU.mult), waits=[tkm], inc=sD)
                    tkv = b.op("dve", lambda e: e.scalar_tensor_tensor(out=var[:, :], in0=ps_sq[:, 0:TL], scalar=1.0 / D, in1=msq[:, :],
                                                                        op0=ALU.mult, op1=ALU.subtract), waits=[tks2, tkm2], inc=sD)
                    sq_free = tkv
                    tksd = b.op("act", lambda e: e.activation(out=var[:, :], in_=var[:, :], func=AF.Sqrt, bias=eps[:, :], scale=1.0), waits=[tkv], inc=sA)
                    tkr = b.op("dve", lambda e: e.reciprocal(out=rstd[:, :], in_=var[:, :]), waits=[tksd], inc=sD)
                    tkn = tkr
                    for fc in range(32):
                        b.op("dve", lambda e, fc=fc, s2=s2: e.tensor_tensor(out=X[s2][:, fc, :], in0=X[s2][:, fc, :], in1=m[:, :], op=ALU.subtract),
                             waits=[tkr, tks] if fc == 0 else [])
                        tkn = b.op("dve", lambda e, fc=fc, s2=s2: e.tensor_tensor(out=X[s2][:, fc, :], in0=X[s2][:, fc, :], in1=rstd[:, :], op=ALU.mult), inc=sD)
                        tky = b.op("act", lambda e, fc=fc, s2=s2: e.activation(out=Q[:, fc, :], in_=X[s2][:, fc, :], func=AF.Identity,
                                                                                bias=bc[:, fc:fc + 1], scale=gc[:, fc:fc + 1]),
                                   waits=[tkn, tkg, tks2], inc=sA)
                    x_free[s2] = tky
                    y_free = tkn
                    outs_ = []
                    if out_f32 is not None:
                        outs_.append(b.dma("sp", kc_view(out_f32)[:, :, tsl], Q[:, :, :], waits=[tky], inc=osm))
                    if out_bf is not None:
                        tkc = b.op("pool", lambda e: e.tensor_copy(out=Yb[:, :, :], in_=Q[:, :, :]), waits=[tky, yb_free], inc=sP)
                        yb_free = b.dma("sp", kc_view(out_bf)[:, :, tsl], Yb[:, :, :], waits=[tkc], inc=osm)
                        outs_.append(yb_free)
                        outs_.append(tkc)
                    if out_tm is not None:
                        for hb_ in range(TL // 128):
                            y2 = (i * (TL // 128) + hb_) % 2
                            tke = None
                            for q4 in range(8):
                                pi = 2 + ntp % 4
                                ntp += 1
                                bank = cx.psum[pi]
                                for j in range(4):
                                    fc = q4 * 4 + j
                                    tkt = b.op("pe", lambda e, fc=fc, j=j, hb_=hb_, bank=bank: e.transpose(
                                        out=bank[:, j * 128:(j + 1) * 128], in_=Q[:, fc, hb_ * 128:(hb_ + 1) * 128], identity=ident[:, :]),
                                        waits=[tky, tk_id, tp_free[pi - 2]] if j == 0 else [], inc=cx.pe_sem if j == 3 else None)
                                o_ = yst[y2][:, q4 * 512:(q4 + 1) * 512]
                                if q4 % 2 == 0:
                                    tke = b.op("dve", lambda e, o_=o_, bank=bank: e.tensor_copy(out=o_, in_=bank[:, :]), waits=[tkt, y_free_st[y2]], inc=sD)
                                else:
                                    tke = b.op("act", lambda e, o_=o_, bank=bank: e.activation(out=o_, in_=bank[:, :], func=AF.Copy), waits=[tkt, y_free_st[y2]], inc=sA)
                                tp_free[pi - 2] = tke
                            r0 = i * TL + hb_ * 128
                            y_free_st[y2] = b.dma("sp", out_tm[r0:r0 + 128, :], yst[y2][:, :], waits=[(sD, sD.v), (sA, sA.v)], inc=ysm[y2])
                            outs_.append(tkt)
                            outs_.append(y_free_st[y2])
                    q_free = outs_
                b.wait("sp", q_free)
                b.flush(tag)
            phase_reset()

        y_free_st = [None, None]
        ln_fm("ln1", S["pre1T"], "ln1_g", "ln1_b", S["h1T_f32"], S["h1T_bf"])
        if stop_after == "p7":
            return nc

        scratch("memTm", [NMEM, D], F32)
        tm2fm("p8t", I["mem"], D, S["memT"], nrows=NMEM)
        gp = GP("p8")
        gemm(b, cx, "xa_q", S["h1T_bf"], D, 0, T, [(I["xa_wq"], 0, D)], "fm", gp.epi("fm", [(0, D, S["qxT"], 0)], dtype=BF16))
        gemm(b, cx, "xa_k", S["memT"], D, 0, NMEM, [(I["xa_wkv"], 0, D)], "fm", gp.epi("fm", [(0, D, S["kxT"], 0)], dtype=BF16), TM=NMEM)
        gemm(b, cx, "xa_v", S["memT"], D, 0, NMEM, [(I["xa_wkv"], D, D)], "tm", gp.epi("tm", (S["vx"], 0), dtype=BF16), TM=NMEM)
        gp.close("p8")
        phase_reset()
        attn("p8a", 4,
             lambda h: [(S["qxT"], h * 1024 + dc * 128, 128) for dc in range(8)],
             lambda h: [(S["kxT"], h * 1024 + dc * 128, 128) for dc in range(8)],
             lambda h: S["vx"][:, h * 1024:(h + 1) * 1024], NMEM, 8, 1024.0 ** -0.5, False, S["oxT"])
        gp = GP("p9")
        gemm(b, cx, "xa_o", S["oxT"], D, 0, T, [(I["xa_wo"], 0, D)], "fm",
             gp.epi("fm", [(0, D, S["pre2T"], 0)], fn=fn_axpy(gp, S["h1T_f32"], ALPHA)))
        gp.close("p9")
        phase_reset()
        ln_fm("ln2", S["pre2T"], "ln2_g", "ln2_b", S["h2T_f32"], S["h2T_bf"])
        if stop_after == "p9":
            return nc

        gp = GP("p10")

        def fn_relu2(eng, o, src, tmp, info, w, si, ci):
            tk = b.op("act", lambda e: e.activation(out=tmp, in_=src, func=AF.Relu), waits=w, inc=sA)
            return b.op("dve", lambda e: e.tensor_tensor(out=o, in0=tmp, in1=tmp, op=ALU.mult), waits=[tk], inc=sD)
        gemm(b, cx, "mlp1", S["h2T_bf"], D, 0, T, [(I["mlp_w1"], 0, DFF)], "fm",
             gp.epi("fm", [(0, DFF, S["ffT"], 0)], dtype=BF16, fn=fn_relu2))
        gp.close("p10")
        phase_reset()
        gp = GP("p11")
        for q_ in range(4):
            src_ = S["h2T_f32"] if q_ == 0 else S["pre3T"]
            gemm(b, cx, f"mlp2_{q_}", S["ffT"][q_ * D:(q_ + 1) * D, :], D, 0, T, [(I["mlp_w2"][q_ * D:(q_ + 1) * D, :], 0, D)], "fm",
                 gp.epi("fm", [(0, D, S["pre3T"], 0)], fn=fn_axpy(gp, src_, ALPHA if q_ == 0 else 1.0)))
            b.wait("sp", gp.stg_free)
            b.flush(f"p11_{q_}")
        gp.close("p11")
        phase_reset()
        ln_fm("ln3", S["pre3T"], "ln3_g", "ln3_b", None, None, out_tm=y)
    nc.flush_names = b.names
    return nc


def _cols(v):
    v = np.asarray(v, np.float32).reshape(-1)
    return np.ascontiguousarray(v.reshape(-1, 128).T)


def _rope_table(pos):
    inv = (1.0 / (10000.0 ** (np.arange(0, 64, 2, dtype=np.float32) / np.float32(64)))).astype(np.float32)
    ang = pos.astype(np.float32)[:, None] * inv[None, :]
    c = np.cos(ang).astype(np.float32).T
    s = np.sin(ang).astype(np.float32).T
    return (np.ascontiguousarray(np.concatenate([c, c, c, c], axis=0)),
            np.ascontiguousarray(np.concatenate([-s, s, -s, s], axis=0)))


def make_in_maps(inp):
    f = lambda a: np.ascontiguousarray(np.asarray(a, np.float32))
    shared = {
        "w_in": f(inp["w_in"][0]), "w_uq": f(inp["w_uq"][0]), "w_ukv": f(inp["w_ukv"][0]),
        "w_br_mla": f(inp["w_branch_mla"][0]), "w_br_gla": f(inp["w_branch_gla"][0]),
        "w_mix_out": f(inp["w_mix_out"][0]), "xa_wq": f(inp["xa_wq"][0]), "xa_wkv": f(inp["xa_wkv"][0]),
        "xa_wo": f(inp["xa_wo"][0]), "mlp_w1": f(inp["mlp_w1"][0]), "mlp_w2": f(inp["mlp_w2"][0]),
        "ln_in_g": _cols(inp["ln_in_g"]), "ln_in_b": _cols(inp["ln_in_b"]),
        "ln1_g": _cols(inp["ln1_g"]), "ln1_b": _cols(inp["ln1_b"]),
        "ln2_g": _cols(inp["ln2_g"]), "ln2_b": _cols(inp["ln2_b"]),
        "ln3_g": _cols(inp["ln3_g"]), "ln3_b": _cols(inp["ln3_b"]),
        "b_merge": _cols(inp["b_merge"]), "q_norm": _cols(inp["mla_q_norm"]), "kv_norm": _cols(inp["mla_kv_norm"]),
        "ident": np.eye(128, dtype=np.float32),
        "w2b": np.ascontiguousarray(np.concatenate([f(inp["gla_gate_w2"][0]), f(inp["gla_gate_b"][0])[:, None, :]], axis=1)),
        "gnorm_bc": np.ascontiguousarray(np.tile(f(inp["gla_norm"][0])[None, :], (64, 4))),
    }
    jj = np.arange(128)[:, None]
    cc = np.arange(128)[None, :]
    same = (jj // 64) == (cc // 64)
    M1 = np.zeros((2, 128, 130), np.float32)
    M2 = np.zeros((2, 128, 128), np.float32)
    M1[0, :, :128] = np.where(same & (jj <= cc), -1.0 / 16.0, 0.0)
    M1[1, :, :128] = np.where(same & (jj >= cc), -1.0 / 16.0, 0.0)
    for d_ in range(2):
        M1[d_, :64, 128] = -1.0 / 16.0
        M1[d_, 64:, 129] = -1.0 / 16.0
    M2[0] = np.where(same & (jj > cc), -1.0 / 16.0, 0.0)
    M2[1] = np.where(same & (jj < cc), -1.0 / 16.0, 0.0)
    j6 = np.arange(64)[:, None]
    c6 = np.arange(64)[None, :]
    mT = np.stack([np.tile((j6 <= c6).astype(np.float32), (1, 4)), np.tile((j6 >= c6).astype(np.float32), (1, 4))], axis=0)
    shared.update({"M1": M1, "M2": M2, "maskT": np.ascontiguousarray(mT)})
    maps = []
    ar = np.arange(T)
    for c in range(8):
        m = dict(shared)
        cv = np.zeros((128, 4), np.float32)
        if c < 4:
            own = f(inp["x_prompt"][c])
            m["x"] = np.ascontiguousarray(np.concatenate([own, own], axis=0))
            m["mem"] = f(inp["mem_prompt"][c])
            m["ropeC"], m["ropeS"] = _rope_table(np.concatenate([ar, ar]))
            cv[:, 0] = -30000.0
        else:
            s_, half = (c - 4) // 2, (c - 4) % 2
            xs = f(inp["x_sample"][s_])
            own = xs[half * T:(half + 1) * T]
            oth = xs[(1 - half) * T:(2 - half) * T]
            m["x"] = np.ascontiguousarray(np.concatenate([own, oth], axis=0))
            m["mem"] = f(inp["mem_sample"][s_])
            m["ropeC"], m["ropeS"] = _rope_table(np.concatenate([ar + half * T, ar + (1 - half) * T]))
            cv[:, 1] = 1.0 if half == 1 else 0.0
            cv[:, 2] = 1.0 if half == 0 else 0.0
        m["cvec"] = cv
        maps.append(m)
    return maps


def kernel(**inp):
    nc = build_program()
    maps = make_in_maps(inp)
    res = run_bass_kernel_spmd(nc, maps, core_ids=list(range(8)))
    ys = [np.asarray(r["y"], np.float32) for r in res.results]
    y_prompt = np.stack(ys[0:4], axis=0)
    y_sample = np.stack([np.concatenate([ys[4], ys[5]], axis=0), np.concatenate([ys[6], ys[7]], axis=0)], axis=0)
    return (y_prompt, y_sample)
```

```python
import numpy as np
from contextlib import ExitStack
import concourse.bass as bass
import concourse.mybir as mybir
from concourse.bass_utils import run_bass_kernel_spmd

F32 = mybir.dt.float32
BF16 = mybir.dt.bfloat16
AF = mybir.ActivationFunctionType
ALU = mybir.AluOpType

D = 4096
T = 2048
TT = 4096
NMEM = 256
DFF = 16384
LN_EPS = 1e-5
RMS_EPS = 1e-6
ALPHA = 2.0 ** 0.25
O_CQ, O_CKV, O_KR, O_GQ, O_GK, O_GV, O_GR, O_LR, O_MG = 0, 1024, 1536, 1600, 2624, 3648, 5696, 7744, 7776
N_IN = 15968

KN = {}
ENGS = ("pe", "act", "dve", "pool", "sp")


def flat(ws):
    if ws is None:
        return
    if isinstance(ws, tuple) and len(ws) == 2 and isinstance(ws[0], Sem):
        yield ws
        return
    for w in ws:
        yield from flat(w)


def dedupe(waits):
    d = {}
    for (s, v) in flat(waits):
        if v > 0 and (s.name not in d or d[s.name][1] < v):
            d[s.name] = (s.h, v)
    return list(d.values())


class Sem:
    def __init__(self, h, name):
        self.h = h
        self.v = 0
        self.name = name


class Builder:
    def __init__(self, nc, es):
        self.nc = nc
        self.es = es
        self.q = {k: [] for k in ENGS}
        self.phase_mode = False
        self.pool = []
        self.phase_sems = []
        self.bar = self.sem("bar")
        self.nbar = 0

    def sem(self, name):
        if self.phase_mode and self.pool:
            sm = self.pool.pop()
        else:
            sm = Sem(self.es.enter_context(self.nc.semaphore(name)), name)
        if self.phase_mode:
            self.phase_sems.append(sm)
        return sm

    def recycle(self):
        self.pool.extend(self.phase_sems)
        self.phase_sems = []

    def op(self, eng, fn, waits=(), inc=None, amt=1):
        ws = dedupe(waits)
        tick = None
        h = None
        if inc is not None:
            inc.v += amt
            tick = (inc, inc.v)
            h = inc.h

        def run(e, fn=fn, ws=ws, h=h, amt=amt):
            for sh, v in ws:
                e.wait_ge(sh, v)
            ins = fn(e)
            if h is not None:
                ins.then_inc(h, amt)
        self.q[eng].append(run)
        return tick

    def dma(self, eng, out, in_, waits=(), inc=None):
        return self.op(eng, lambda e: e.dma_start(out=out, in_=in_), waits=waits, inc=inc, amt=16)

    def wait(self, eng, waits):
        ws = dedupe(waits)

        def run(e, ws=ws):
            for sh, v in ws:
                e.wait_ge(sh, v)
        self.q[eng].append(run)

    def maybe_flush(self, limit=5000):
        if max(len(v) for v in self.q.values()) >= limit:
            self.flush("auto")

    def flush(self, name):
        nc = self.nc
        self.nbar += 1
        self.names = getattr(self, "names", []) + [name]
        target = 5 * self.nbar
        bar = self.bar.h
        q = self.q
        self.q = {k: [] for k in ENGS}
        self.bar.v = target

        def body(lst):
            def f(e):
                for r in lst:
                    r(e)
                e.drain().then_inc(bar, 1)
                e.wait_ge(bar, target)
            return f
        with nc.Block() as blk:
            blk.tensor(body(q["pe"]))
            blk.scalar(body(q["act"]))
            blk.vector(body(q["dve"]))
            blk.gpsimd(body(q["pool"]))
            blk.sync(body(q["sp"]))


def kc_view(ap2d):
    return ap2d.rearrange("(kc p) n -> p kc n", p=128)


class Ctx:
    pass


def gemm(b, cx, name, A, K, tok0, ntok, segs, orient, epi, TM=1024):
    KC = K // 128
    assert KC <= 32
    ncol = sum(s[2] for s in segs)
    GW = 256
    ngrp = (ncol + GW - 1) // GW
    ntt = ntok // TM
    tsw = min(512, TM)
    nts = TM // tsw
    Akc = kc_view(A)
    at = cx.at
    for tt in range(ntt):
        t0 = tok0 + tt * TM
        at_ticks = []
        step = 8
        for k0 in range(0, KC, step):
            k1 = min(KC, k0 + step)
            tk = b.dma("sp", at[:, k0:k1, 0:TM], Akc[:, k0:k1, t0:t0 + TM], waits=[cx.at_free], inc=cx.at_sem)
            at_ticks.append((k0, k1, tk))
        for g in range(ngrp):
            c0 = g * GW
            c1 = min(ncol, c0 + GW)
            u = cx.wu
            cx.wu += 1
            slot = u % cx.WR
            wt = cx.wt[slot]
            wsem = cx.wsem[slot]
            wfree = cx.wfree[slot]
            off = 0
            pos = 0
            wtick = None
            for (W, sc0, sn) in segs:
                lo = max(c0, pos)
                hi = min(c1, pos + sn)
                if lo < hi:
                    Wk = kc_view(W)
                    src0 = sc0 + (lo - pos)
                    ln = hi - lo
                    kstep = 16 if ln * 4 >= 512 else 32
                    for k0 in range(0, KC, kstep):
                        k1 = min(KC, k0 + kstep)
                        wtick = b.dma("pool", wt[:, k0:k1, off:off + ln], Wk[:, k0:k1, src0:src0 + ln],
                                      waits=[wfree], inc=wsem)
                    off += ln
                pos += sn
            grp = cx.pu % 2
            cx.pu += 1
            banks = cx.psum[grp * 4:(grp + 1) * 4]
            pfree = cx.pfree[grp]
            chunks = []
            first = True
            ngc = (c1 - c0 + 127) // 128
            last_tick = None
            if orient == "fm":
                for ci in range(ngc):
                    cw = min(128, c1 - c0 - ci * 128)
                    for ts in range(nts):
                        ps = banks[ci * nts + ts]
                        for kc in range(KC):
                            waits = []
                            if first:
                                waits = [wtick, pfree] + [t for (_, _, t) in at_ticks]
                                first = False
                            is_last = (ci == ngc - 1 and ts == nts - 1 and kc == KC - 1)
                            last_tick = b.op(
                                "pe",
                                lambda e, ps=ps, wt=wt, kc=kc, ci=ci, cw=cw, ts=ts: e.matmul(
                                    ps[0:cw, 0:tsw], lhsT=wt[:, kc, ci * 128:ci * 128 + cw],
                                    rhs=at[:, kc, ts * tsw:(ts + 1) * tsw], start=(kc == 0), stop=(kc == KC - 1)),
                                waits=waits, inc=(cx.pe_sem if is_last else None))
                        chunks.append((ps, c0 + ci * 128, cw, t0 + ts * tsw, tsw))
            else:
                gw = c1 - c0
                ntc = TM // 128
                for tc in range(ntc):
                    ps = banks[tc // 2]
                    hh = (tc % 2) * 256
                    for kc in range(KC):
                        waits = []
                        if first:
                            waits = [wtick, pfree] + [t for (_, _, t) in at_ticks]
                            first = False
                        is_last = (tc == ntc - 1 and kc == KC - 1)
                        last_tick = b.op(
                            "pe",
                            lambda e, ps=ps, wt=wt, kc=kc, tc=tc, hh=hh, gw=gw: e.matmul(
                                ps[:, hh:hh + gw], lhsT=at[:, kc, tc * 128:(tc + 1) * 128],
                                rhs=wt[:, kc, 0:gw], start=(kc == 0), stop=(kc == KC - 1)),
                            waits=waits, inc=(cx.pe_sem if is_last else None))
                    chunks.append((ps[:, hh:hh + gw], c0, gw, t0 + tc * 128, 128))
            cx.wfree[slot] = last_tick
            at_ticks_done = last_tick
            b.maybe_flush()
            cx.pfree[grp] = epi(last_tick, chunks)
        cx.at_free = at_ticks_done


def build_program(dbg=None, stop_after=None, p0_tiles=None):
    nc = bass.Bass("TRN2", target_bir_lowering=False)
    dt = lambda n, s, d, k="ExternalInput": nc.dram_tensor(n, s, d, kind=k).ap()
    I = {}
    I["x"] = dt("x", [TT, D], F32)
    I["mem"] = dt("mem", [NMEM, D], F32)
    I["ropeC"] = dt("ropeC", [128, TT], F32)
    I["ropeS"] = dt("ropeS", [128, TT], F32)
    I["cvec"] = dt("cvec", [128, 4], F32)
    I["ident"] = dt("ident", [128, 128], F32)
    I["w2b"] = dt("w2b", [2, 17, 1024], F32)
    I["M1"] = dt("M1", [2, 128, 130], F32)
    I["M2"] = dt("M2", [2, 128, 128], F32)
    I["maskT"] = dt("maskT", [2, 64, 256], F32)
    I["gnorm_bc"] = dt("gnorm_bc", [64, 2048], F32)
    I["w_in"] = dt("w_in", [D, N_IN], F32)
    I["w_uq"] = dt("w_uq", [1024, 3072], F32)
    I["w_ukv"] = dt("w_ukv", [512, 4096], F32)
    I["w_br_mla"] = dt("w_br_mla", [2048, D], F32)
    I["w_br_gla"] = dt("w_br_gla", [2048, D], F32)
    I["w_mix_out"] = dt("w_mix_out", [D, D], F32)
    I["xa_wq"] = dt("xa_wq", [D, D], F32)
    I["xa_wkv"] = dt("xa_wkv", [D, 2 * D], F32)
    I["xa_wo"] = dt("xa_wo", [D, D], F32)
    I["mlp_w1"] = dt("mlp_w1", [D, DFF], F32)
    I["mlp_w2"] = dt("mlp_w2", [DFF, D], F32)
    for nm, n in (("ln_in_g", D), ("ln_in_b", D), ("ln1_g", D), ("ln1_b", D), ("ln2_g", D), ("ln2_b", D),
                  ("ln3_g", D), ("ln3_b", D), ("b_merge", 2 * D), ("q_norm", 1024), ("kv_norm", 512)):
        I[nm] = dt(nm, [128, n // 128], F32)
    y = dt("y", [T, D], F32, "ExternalOutput")

    S = {}

    def scratch(n, s, d):
        kind = "ExternalOutput" if (dbg and n in dbg) else "Internal"
        S[n] = nc.dram_tensor(n, s, d, kind=kind).ap()
        return S[n]
    scratch("hT_bf", [D, TT], BF16)
    scratch("hT_f32", [D, T], F32)
    scratch("cqT", [1024, T], F32)
    scratch("gqT", [1024, T], F32)
    scratch("ckvT", [512, TT], F32)
    scratch("krT", [64, TT], F32)
    scratch("gkT", [1024, T], F32)
    scratch("glrT", [64, TT], F32)
    scratch("gk", [TT, 1024], F32)
    scratch("gv", [TT, 2048], BF16)
    scratch("gr", [T, 2048], F32)

    with ExitStack() as es:
        b = Builder(nc, es)
        cx = Ctx()
        cx.WR = 3
        cx.wsem = [b.sem(f"wsem{i}") for i in range(cx.WR)]
        cx.wfree = [None] * cx.WR
        cx.wu = 0
        cx.pu = 0
        cx.psum = [es.enter_context(nc.psum_tensor(f"ps{i}", [128, 512], F32)) for i in range(8)]
        cx.psbf = None
        cx.pfree = [None, None]
        cx.pe_sem = b.sem("pe_sem")
        cx.at_sem = b.sem("at_sem")
        cx.at_free = None
        ident = es.enter_context(nc.sbuf_tensor("sb_ident", [128, 128], F32))
        cvec = es.enter_context(nc.sbuf_tensor("sb_cvec", [128, 4], F32))
        eps_ln = es.enter_context(nc.sbuf_tensor("eps_ln", [128, 1], F32))
        csem = b.sem("csem")
        tk_id = b.dma("sp", ident[:, :], I["ident"][:, :], inc=b.sem("c_id"))
        tk_cv = b.dma("sp", cvec[:, :], I["cvec"][:, :], inc=b.sem("c_cv"))
        sA = b.sem("sA")
        sD = b.sem("sD")
        sP = b.sem("sP")
        sT = b.sem("sT")
        tk_eps = b.op("dve", lambda e: e.memset(eps_ln[:, :], LN_EPS), inc=sD)
        b.phase_mode = True

        with ExitStack() as ps_:
            sb = lambda n, s, d: ps_.enter_context(nc.sbuf_tensor(n, s, d))
            xt = [sb(f"p0_xt{i}", [128, D], F32) for i in range(2)]
            xn = [sb(f"p0_xn{i}", [128, D], F32) for i in range(2)]
            hf = [sb(f"p0_hf{i}", [128, 32, 128], F32) for i in range(2)]
            hb = [sb(f"p0_hb{i}", [128, 32, 128], BF16) for i in range(2)]
            gcol = sb("p0_g", [128, 32], F32)
            bcol = sb("p0_b", [128, 32], F32)
            stats = sb("p0_stats", [128, 8, 6], F32)
            mv = sb("p0_mv", [128, 2], F32)
            sd = sb("p0_sd", [128, 1], F32)
            rstd = sb("p0_rstd", [128, 1], F32)
            xsem = [b.sem(f"p0_xsem{i}") for i in range(2)]
            stsem = [b.sem(f"p0_stsem{i}") for i in range(2)]
            tk_g = b.dma("sp", gcol[:, :], I["ln_in_g"][:, :], inc=b.sem("c_g0"))
            tk_b = b.dma("sp", bcol[:, :], I["ln_in_b"][:, :], inc=b.sem("c_b0"))
            xt_free = [None, None]
            xn_free = [None, None]
            st_free = [None, None]
            ev_free = [None, None]
            hT_bf_v = kc_view(S["hT_bf"])
            hT_f_v = kc_view(S["hT_f32"])
            NTILE = p0_tiles or (TT // 128)
            stats2 = [stats, sb("p0_stats1", [128, 8, 6], F32)]
            mv2 = [mv, sb("p0_mv1", [128, 2], F32)]
            sd2 = [sd, sb("p0_sd1", [128, 1], F32)]
            rstd2 = [rstd, sb("p0_rstd1", [128, 1], F32)]
            tkxn_of = {}

            def front(i):
                s2 = i % 2
                st_, mv_, sd_, rs_ = stats2[s2], mv2[s2], sd2[s2], rstd2[s2]
                tkx = b.dma("sp", xt[s2][:, :], I["x"][i * 128:(i + 1) * 128, :], waits=[xt_free[s2]], inc=xsem[s2])
                for c in range(8):
                    tks = b.op("dve", lambda e, c=c, s2=s2: e.bn_stats(out=st_[:, c, :], in_=xt[s2][:, c * 512:(c + 1) * 512]),
                               waits=[tkx] if c == 0 else [], inc=sD if c == 7 else None)
                tka = b.op("dve", lambda e: e.bn_aggr(out=mv_[:, :], in_=st_[:, :, :]), waits=[tks], inc=sD)
                tksd = b.op("act", lambda e: e.activation(out=sd_[:, :], in_=mv_[:, 1:2], func=AF.Sqrt, bias=eps_ln[:, :], scale=1.0),
                            waits=[tka, tk_eps], inc=sA)
                tkr = b.op("dve", lambda e: e.reciprocal(out=rs_[:, :], in_=sd_[:, :]), waits=[tksd], inc=sD)
                tkxn = b.op("dve", lambda e, s2=s2: e.tensor_scalar(out=xn[s2][:, :], in0=xt[s2][:, :], scalar1=mv_[:, 0:1],
                                                                     scalar2=rs_[:, :], op0=ALU.subtract, op1=ALU.mult),
                            waits=[tkr, xn_free[s2]], inc=sD)
                xt_free[s2] = tkxn
                tkxn_of[i] = tkxn

            def back(i):
                s2 = i % 2
                tkxn = tkxn_of[i]
                evt = None
                tkp = None
                for hh in range(2):
                    banks = cx.psum[hh * 4:(hh + 1) * 4]
                    for j in range(16):
                        fc = hh * 16 + j
                        w = []
                        if j == 0:
                            w = [tkxn, ev_free[hh], tk_id]
                        tkp = b.op("pe", lambda e, fc=fc, j=j, banks=banks, s2=s2: e.transpose(
                            out=banks[j // 4][:, (j % 4) * 128:(j % 4 + 1) * 128],
                            in_=xn[s2][:, fc * 128:(fc + 1) * 128], identity=ident[:, :]),
                            waits=w, inc=cx.pe_sem if j == 15 else None)
                    for j in range(16):
                        fc = hh * 16 + j
                        w = []
                        if j == 0:
                            w = [tkp, tk_g, tk_b]
                            if hh == 0:
                                w.append(st_free[s2])
                        evt = b.op("act", lambda e, fc=fc, j=j, banks=banks, s2=s2: e.activation(
                            out=hf[s2][:, fc, :], in_=banks[j // 4][:, (j % 4) * 128:(j % 4 + 1) * 128],
                            func=AF.Identity, bias=bcol[:, fc:fc + 1], scale=gcol[:, fc:fc + 1]),
                            waits=w, inc=sA if j == 15 else None)
                    ev_free[hh] = evt
                xn_free[s2] = tkp
                tkc = b.op("pool", lambda e, s2=s2: e.tensor_copy(out=hb[s2][:, :, :], in_=hf[s2][:, :, :]), waits=[evt], inc=sP)
                tst = b.dma("sp", hT_bf_v[:, :, i * 128:(i + 1) * 128], hb[s2][:, :, :], waits=[tkc], inc=stsem[s2])
                if i < T // 128:
                    tst = b.dma("sp", hT_f_v[:, :, i * 128:(i + 1) * 128], hf[s2][:, :, :], waits=[tkc], inc=stsem[s2])
                st_free[s2] = tst

            front(0)
            for i in range(NTILE):
                if i + 1 < NTILE:
                    front(i + 1)
                back(i)
            b.wait("sp", [st_free[0], st_free[1]])
            cx.pfree = [ev_free[0], ev_free[1]]
            b.flush("p0")
        b.recycle()
        if stop_after == "p0":
            return nc

        Win = I["w_in"]
        scratch("ckvnT", [512, TT], BF16)
        scratch("cqnT", [1024, T], BF16)
        scratch("kropeT", [64, TT], BF16)
        scratch("qnT", [2048, T], BF16)
        scratch("qrT", [1024, T], BF16)
        scratch("knT", [2048, TT], BF16)
        scratch("vmla", [TT, 2048], BF16)
        scratch("omlaT", [2048, T], BF16)
        scratch("oglaT", [2048, T], BF16)
        scratch("gateT", [2 * D, T], F32)
        scratch("bmT", [D, T], F32)
        scratch("bgT", [D, T], F32)
        scratch("mergedT", [D, T], BF16)
        scratch("mixT", [D, T], F32)
        scratch("h1T_f32", [D, T], F32)
        scratch("h1T_bf", [D, T], BF16)
        scratch("qxT", [D, T], BF16)
        scratch("memT", [D, NMEM], BF16)
        scratch("kxT", [D, NMEM], BF16)
        scratch("vx", [NMEM, D], BF16)
        scratch("oxT", [D, T], BF16)
        scratch("aoT", [D, T], F32)
        scratch("h2T_f32", [D, T], F32)
        scratch("h2T_bf", [D, T], BF16)
        scratch("ffT", [DFF, T], BF16)
        for q_ in range(4):
            scratch(f"mlpP{q_}", [D, T], F32)

        class GP:
            def __init__(self, tag, extra=None):
                self.stack = ExitStack()
                self.tag = tag
                sbp = lambda n, s_, d: self.stack.enter_context(nc.sbuf_tensor(f"{tag}_{n}", s_, d))
                self.sb = sbp
                cx.at = sbp("at", [128, 32, 1024], BF16)
                cx.wt = [sbp(f"wt{i}", [128, 32, 256], BF16) for i in range(cx.WR)]
                cx.wfree = [None] * cx.WR
                cx.at_free = None
                self.stg = [sbp(f"stg{i}", [128, 2048], F32) for i in range(2)]
                self.stb = [sbp(f"stb{i}", [128, 2048], BF16) for i in range(2)]
                self.stg_sem = [b.sem(f"{tag}_stgsem{i}") for i in range(2)]
                self.stg_free = [None, None]
                self.n = 0
                self.ldb = {}
                self.ld_free = {}
                self.ld_sem = {}

            def ld(self, which, si, ci, src_ap, cw, nt):
                key = (which, si)
                if key not in self.ldb:
                    self.ldb[key] = self.sb(f"ld{which}{si}", [128, 2048], F32)
                    self.ld_sem[key] = [b.sem(f"{self.tag}_ld{which}{si}_{c}") for c in range(4)]
                    self.ld_free[key] = [None] * 4
                dst = self.ldb[key][0:cw, ci * 512:ci * 512 + nt]
                tk = b.dma("sp", dst, src_ap, waits=[self.ld_free[key][ci]], inc=self.ld_sem[key][ci])
                return dst, tk

            def ld_done(self, which, si, ci, tk):
                self.ld_free[(which, si)][ci] = tk

            def epi(self, orient, dests, dtype=F32, fn=None):
                me = self

                def epi(pe_tick, chunks):
                    si = me.n % 2
                    me.n += 1
                    st, stb = me.stg[si], me.stb[si]
                    ev = []
                    outs = []
                    for ci, (ps, vc0, cw, t0, nt) in enumerate(chunks):
                        eng = ("act" if ci % 2 == 0 else "dve") if orient == "fm" else ("act" if (ci // 2) % 2 == 0 else "dve")
                        w = [pe_tick, me.stg_free[si]]
                        buf = stb if dtype == BF16 else st
                        if orient == "fm":
                            o = buf[0:cw, ci * 512:ci * 512 + nt]
                            tmp = st[0:cw, ci * 512:ci * 512 + nt]
                            src = ps[0:cw, 0:nt]
                        else:
                            o = buf[:, ci * 256:ci * 256 + cw]
                            tmp = st[:, ci * 256:ci * 256 + cw]
                            src = ps
                        if fn is not None:
                            tk = fn(eng, o, src, tmp, (vc0, cw, t0, nt), w, si, ci)
                        elif eng == "act":
                            tk = b.op("act", lambda e, o=o, src=src: e.activation(out=o, in_=src, func=AF.Copy), waits=w, inc=sA)
                        else:
                            tk = b.op("dve", lambda e, o=o, src=src: e.tensor_copy(out=o, in_=src), waits=w, inc=sD)
                        ev.append(tk)
                        outs.append(o)
                    tst = None
                    for ci, (ps, vc0, cw, t0, nt) in enumerate(chunks):
                        if orient == "fm":
                            for (dv0, dn, dap, drow0) in dests:
                                lo = max(vc0, dv0)
                                hi = min(vc0 + cw, dv0 + dn)
                                if lo < hi:
                                    buf = stb if dtype == BF16 else st
                                    srcv = buf[lo - vc0:hi - vc0, ci * 512:ci * 512 + nt]
                                    tst = b.dma("sp", dap[drow0 + lo - dv0:drow0 + hi - dv0, t0:t0 + nt], srcv,
                                                waits=ev, inc=me.stg_sem[si])
                        else:
                            dap, cb = dests
                            tst = b.dma("sp", dap[t0:t0 + nt, vc0 - cb:vc0 - cb + cw], outs[ci], waits=ev, inc=me.stg_sem[si])
                    me.stg_free[si] = tst
                    return ev
                return epi

            def close(self, name):
                b.wait("sp", [self.stg_free[0], self.stg_free[1]])
                b.flush(name)
                self.stack.close()

        gp = GP("p1")
        if KN.get("only_tm"):
            gemm(b, cx, "in_tm_gk", S["hT_bf"], D, 0, KN.get("tm_tok", TT), [(Win, O_GK, KN.get("tm_cols", 1024))], "tm", gp.epi("tm", (S["gk"], 0)))
            gp.close("p1")
            return nc
        gemm(b, cx, "in_fm_own", S["hT_bf"], D, 0, KN.get("p1_tok", T),
             [(Win, O_CQ, 1024), (Win, O_GQ, 1024), (Win, O_GK, 1024)], "fm",
             gp.epi("fm", [(0, 1024, S["cqT"], 0), (1024, 1024, S["gqT"], 0), (2048, 1024, S["gkT"], 0)]))
        if stop_after == "p1a":
            gp.close("p1")
            return nc
        gemm(b, cx, "in_fm_all", S["hT_bf"], D, 0, TT,
             [(Win, O_CKV, 512), (Win, O_KR, 64), (Win, O_LR, 32)], "fm",
             gp.epi("fm", [(0, 512, S["ckvT"], 0), (512, 64, S["krT"], 0), (576, 16, S["glrT"], 0), (592, 16, S["glrT"], 32)]))
        if stop_after == "p1b":
            gp.close("p1")
            return nc
        if not KN.get("skip_tm"):
            gemm(b, cx, "in_tm_gk", S["hT_bf"], D, 0, TT, [(Win, O_GK, 1024)], "tm", gp.epi("tm", (S["gk"], 0)))
        if not KN.get("skip_tm") and not KN.get("skip_tm2"):
            gemm(b, cx, "in_tm_gv", S["hT_bf"], D, 0, TT, [(Win, O_GV, 2048)], "tm", gp.epi("tm", (S["gv"], 0), dtype=BF16))
            gemm(b, cx, "in_tm_gr", S["hT_bf"], D, 0, T, [(Win, O_GR, 2048)], "tm", gp.epi("tm", (S["gr"], 0)))
        bmg = gp.sb("bmg", [128, 64], F32)
        tk_bmg = b.dma("sp", bmg[:, :], I["b_merge"][:, :], inc=b.sem("c_bmg"))

        def fn_sig(eng, o, src, tmp, info, w, si, ci):
            fc = info[0] // 128
            return b.op("act", lambda e: e.activation(out=o, in_=src, func=AF.Sigmoid, bias=bmg[:, fc:fc + 1], scale=1.0),
                        waits=w + [tk_bmg], inc=sA)
        gemm(b, cx, "in_gate", S["hT_bf"], D, 0, T, [(Win, O_MG, 2 * D)], "fm",
             gp.epi("fm", [(0, 2 * D, S["gateT"], 0)], fn=fn_sig))
        gp.close("p1")
        b.recycle()
        if stop_after == "p1":
            return nc

        def phase_reset():
            b.recycle()
            cx.pfree = [None, None]
            cx.wfree = [None] * cx.WR
            cx.at_free = None

        def rms_fm(tag, src, F, N, gname, dst):
            FC = F // 128
            with ExitStack() as st_:
                sbp = lambda n, s_, d: st_.enter_context(nc.sbuf_tensor(f"sb{tag}_{n}", s_, d))
                xin = [sbp(f"x{i}", [128, FC, 512], F32) for i in range(2)]
                sq = sbp("sq", [128, FC, 512], BF16)
                ones = sbp("ones", [128, 128], BF16)
                sd = sbp("sd", [128, 512], F32)
                R = sbp("R", [128, 512], F32)
                ob = [sbp(f"o{i}", [128, FC, 512], BF16) for i in range(2)]
                gc = sbp("g", [128, FC], F32)
                eps = sbp("eps", [128, 1], F32)
                xs = [b.sem(f"{tag}_xs{i}") for i in range(2)]
                os_ = [b.sem(f"{tag}_os{i}") for i in range(2)]
                tkg = b.dma("sp", gc[:, :], I[gname][:, :], inc=b.sem(f"{tag}_cg"))
                tk1 = b.op("dve", lambda e: e.memset(ones[:, :], 1.0), inc=sD)
                tk1 = b.op("dve", lambda e: e.memset(eps[:, :], RMS_EPS), inc=sD)
                x_free = [None, None]
                o_free = [None, None]
                sq_free = None
                ps_free = None
                sd_free = None
                srcv, dstv = kc_view(src), kc_view(dst)
                ps = cx.psum[0]
                for i in range(N // 512):
                    s2 = i % 2
                    tkx = b.dma("sp", xin[s2][:, :, :], srcv[:, :, i * 512:(i + 1) * 512], waits=[x_free[s2]], inc=xs[s2])
                    tksq = b.op("act", lambda e, s2=s2: e.activation(out=sq[:, :, :], in_=xin[s2][:, :, :], func=AF.Square),
                                waits=[tkx, sq_free], inc=sA)
                    for fc in range(FC):
                        tkp = b.op("pe", lambda e, fc=fc: e.matmul(ps[:, :], lhsT=ones[:, :], rhs=sq[:, fc, :],
                                                                    start=(fc == 0), stop=(fc == FC - 1)),
                                   waits=[tksq, tk1, ps_free] if fc == 0 else [], inc=cx.pe_sem if fc == FC - 1 else None)
                    sq_free = tkp
                    tksd = b.op("act", lambda e: e.activation(out=sd[:, :], in_=ps[:, :], func=AF.Sqrt, bias=eps[:, :], scale=1.0 / F),
                                waits=[tkp, sd_free], inc=sA)
                    ps_free = tksd
                    tkr = b.op("dve", lambda e: e.reciprocal(out=R[:, :], in_=sd[:, :]), waits=[tksd], inc=sD)
                    sd_free = tkr
                    for fc in range(FC):
                        tko = b.op("dve", lambda e, fc=fc, s2=s2: e.scalar_tensor_tensor(
                            out=ob[s2][:, fc, :], in0=xin[s2][:, fc, :], scalar=gc[:, fc:fc + 1], in1=R[:, :],
                            op0=ALU.mult, op1=ALU.mult),
                            waits=[tkr, tkg, o_free[s2]] if fc == 0 else [], inc=sD if fc == FC - 1 else None)
                    x_free[s2] = tko
                    o_free[s2] = b.dma("sp", dstv[:, :, i * 512:(i + 1) * 512], ob[s2][:, :, :], waits=[tko], inc=os_[s2])
                b.wait("sp", o_free)
                b.flush(tag)
            phase_reset()

        rms_fm("p2q", S["cqT"], 1024, T, "q_norm", S["cqnT"])
        rms_fm("p2kv", S["ckvT"], 512, TT, "kv_norm", S["ckvnT"])
        with ExitStack() as st_:
            sbp = lambda n, s_, d: st_.enter_context(nc.sbuf_tensor(f"p2r_{n}", s_, d))
            kr = [sbp(f"kr{i}", [64, 512], F32) for i in range(2)]
            rc = [sbp(f"rc{i}", [64, 512], F32) for i in range(2)]
            rs = [sbp(f"rs{i}", [64, 512], F32) for i in range(2)]
            pr = sbp("pr", [64, 512], F32)
            t2 = sbp("t2", [64, 512], F32)
            ko = [sbp(f"ko{i}", [64, 512], BF16) for i in range(2)]
            ls = [b.sem(f"p2r_ls{i}") for i in range(2)]
            ss = [b.sem(f"p2r_ss{i}") for i in range(2)]
            l_free = [None, None]
            o_free = [None, None]
            for i in range(TT // 512):
                s2 = i % 2
                sl = slice(i * 512, (i + 1) * 512)
                b.dma("sp", kr[s2][:, :], S["krT"][:, sl], waits=[l_free[s2]], inc=ls[s2])
                b.dma("sp", rc[s2][:, :], I["ropeC"][0:64, sl], waits=[l_free[s2]], inc=ls[s2])
                tkl = b.dma("sp", rs[s2][:, :], I["ropeS"][0:64, sl], waits=[l_free[s2]], inc=ls[s2])
                tk = b.op("dve", lambda e, s2=s2: e.tensor_tensor(out=pr[:, :], in0=kr[s2][:, :], in1=rc[s2][:, :], op=ALU.mult),
                          waits=[tkl], inc=sD)
                tk = b.op("dve", lambda e, s2=s2: e.tensor_copy(out=t2[0:32, :], in_=kr[s2][32:64, :]), waits=[tk], inc=sD)
                tk = b.op("dve", lambda e, s2=s2: e.tensor_copy(out=t2[32:64, :], in_=kr[s2][0:32, :]), waits=[tk], inc=sD)
                tk = b.op("dve", lambda e, s2=s2: e.tensor_tensor(out=t2[:, :], in0=t2[:, :], in1=rs[s2][:, :], op=ALU.mult),
                          waits=[tk], inc=sD)
                l_free[s2] = tk
                tk = b.op("dve", lambda e, s2=s2: e.tensor_tensor(out=ko[s2][:, :], in0=pr[:, :], in1=t2[:, :], op=ALU.add),
                          waits=[tk, o_free[s2]], inc=sD)
                o_free[s2] = b.dma("sp", S["kropeT"][:, sl], ko[s2][:, :], waits=[tk], inc=ss[s2])
            b.wait("sp", o_free)
            b.flush("p2r")
        phase_reset()
        if stop_after == "p2":
            return nc

        gp = GP("p3")
        ropeC = gp.sb("ropeC", [128, T], F32)
        ropeS = gp.sb("ropeS", [128, T], F32)
        rtmp = gp.sb("rtmp", [128, 2048], F32)
        tk_rq = b.dma("sp", ropeC[:, :], I["ropeC"][:, 0:T], inc=b.sem("c_rqc"))
        tk_rq2 = b.dma("sp", ropeS[:, :], I["ropeS"][:, 0:T], inc=b.sem("c_rqs"))
        Wq, Wkv = I["w_uq"], I["w_ukv"]

        def fn_q(eng, o, src, tmp, info, w, si, ci):
            vc0, cw, t0, nt = info
            ci = (t0 % 1024) // 512 + 2 * ((vc0 // 128) % 2)
            tk = None
            tks_ = []
            for r0 in (0, 64):
                if (vc0 + r0) % 192 < 128:
                    tk = b.op("act", lambda e, r0=r0: e.activation(out=o[r0:r0 + 64, :], in_=src[r0:r0 + 64, :], func=AF.Copy),
                              waits=w, inc=sA)
                else:
                    sw = rtmp[:, ci * 512:ci * 512 + nt]
                    tk = b.op("dve", lambda e, r0=r0: e.tensor_tensor(out=tmp[r0:r0 + 64, :], in0=src[r0:r0 + 64, :],
                                                                       in1=ropeC[r0:r0 + 64, t0:t0 + nt], op=ALU.mult),
                              waits=w + [tk_rq, tk_rq2], inc=sD)
                    tk = b.op("dve", lambda e, r0=r0, sw=sw: e.tensor_copy(out=sw[r0:r0 + 32, :], in_=src[r0 + 32:r0 + 64, :]), waits=[tk], inc=sD)
                    tk = b.op("dve", lambda e, r0=r0, sw=sw: e.tensor_copy(out=sw[r0 + 32:r0 + 64, :], in_=src[r0:r0 + 32, :]), waits=[tk], inc=sD)
                    tk = b.op("dve", lambda e, r0=r0, sw=sw: e.tensor_tensor(out=sw[r0:r0 + 64, :], in0=sw[r0:r0 + 64, :],
                                                                              in1=ropeS[r0:r0 + 64, t0:t0 + nt], op=ALU.mult), waits=[tk], inc=sD)
                    tk = b.op("dve", lambda e, r0=r0, sw=sw: e.tensor_tensor(out=o[r0:r0 + 64, :], in0=tmp[r0:r0 + 64, :],
                                                                              in1=sw[r0:r0 + 64, :], op=ALU.add), waits=[tk], inc=sD)
                tks_.append(tk)
            return tks_
        qd = []
        for h in range(16):
            qd += [(h * 192, 128, S["qnT"], h * 128), (h * 192 + 128, 64, S["qrT"], h * 64)]
        gemm(b, cx, "q_up", S["cqnT"], 1024, 0, T, [(Wq, 0, 3072)], "fm", gp.epi("fm", qd, dtype=BF16, fn=fn_q))
        gemm(b, cx, "kn_up", S["ckvnT"], 512, 0, TT, [(Wkv, h * 256, 128) for h in range(16)], "fm",
             gp.epi("fm", [(0, 2048, S["knT"], 0)], dtype=BF16))
        gemm(b, cx, "v_up", S["ckvnT"], 512, 0, TT, [(Wkv, h * 256 + 128, 128) for h in range(16)], "tm",
             gp.epi("tm", (S["vmla"], 0), dtype=BF16))
        gp.close("p3")
        phase_reset()
        if stop_after == "p3":
            return nc

        def attn(tag, nh, qparts, kparts, vsrc, nk, ndv, scale, masked, out_ap):
            NKC = nk // 128
            NP = len(qparts(0))
            with ExitStack() as st_:
                sbp = lambda n, s_, d: st_.enter_context(nc.sbuf_tensor(f"sb{tag}_{n}", s_, d))
                kt = [sbp(f"kt{i}", [128, NP, nk], BF16) for i in range(2)]
                vt = [sbp(f"vt{i}", [128, NKC, ndv * 128], BF16) for i in range(2)]
                qt = [sbp(f"qt{i}", [128, NP, 512], BF16) for i in range(2)]
                pb = [sbp(f"pb{i}", [128, NKC, 512], BF16) for i in range(2)]
                rl = sbp("rl", [128, 512], F32)
                ob = [sbp(f"ob{i}", [128, ndv, 512], BF16) for i in range(2)]
                ones = sbp("ones", [128, 128], BF16)
                zero = sbp("zero", [128, 1], F32)
                tk1 = b.op("dve", lambda e: e.memset(ones[:, :], 1.0), inc=sD)
                tk1 = b.op("dve", lambda e: e.memset(zero[:, :], 0.0), inc=sD)
                ks = [b.sem(f"{tag}_ks{i}") for i in range(2)]
                qs = [b.sem(f"{tag}_qs{i}") for i in range(2)]
                oss = [b.sem(f"{tag}_os{i}") for i in range(2)]
                ps_s = cx.psum[0:3]
                ps_o = cx.psum[3:5]
                ps_l = cx.psum[5:7]
                s_free = [None] * 3
                o_freeP = [None, None]
                l_freeP = [None, None]
                k_free = [None, None]
                q_free = [None, None]
                pb_free = [None, None]
                ob_free = [None, None]
                rl_free = None
                nS = 0
                nO = 0
                nQ = 0
                for h in range(nh):
                    hs = h % 2
                    tkk = None
                    for p, (ap, r0, rows) in enumerate(kparts(h)):
                        tkk = b.dma("sp", kt[hs][0:rows, p, :], ap[r0:r0 + rows, 0:nk], waits=[k_free[hs]], inc=ks[hs])
                    vv = vsrc(h).rearrange("(kc p) n -> p kc n", p=128)
                    for k0 in range(0, NKC, 8):
                        k1 = min(NKC, k0 + 8)
                        tkk = b.dma("sp", vt[hs][:, k0:k1, :], vv[:, k0:k1, :], waits=[k_free[hs]], inc=ks[hs])
                    last_pe = None
                    for qi in range(T // 512):
                        q2 = nQ % 2
                        nQ += 1
                        tkq = None
                        for p, (ap, r0, rows) in enumerate(qparts(h)):
                            tkq = b.dma("sp", qt[q2][0:rows, p, :], ap[r0:r0 + rows, qi * 512:(qi + 1) * 512],
                                        waits=[q_free[q2]], inc=qs[q2])
                        tkp_last = None
                        for kc in range(NKC):
                            sb_ = nS % 3
                            nS += 1
                            for p, (ap, r0, rows) in enumerate(kparts(h)):
                                tks = b.op("pe", lambda e, p=p, rows=rows, kc=kc, sb_=sb_, hs=hs, q2=q2: e.matmul(
                                    ps_s[sb_][:, :], lhsT=kt[hs][0:rows, p, kc * 128:(kc + 1) * 128],
                                    rhs=qt[q2][0:rows, p, :], start=(p == 0), stop=(p == NP - 1)),
                                    waits=[tkk, tkq, s_free[sb_]] if p == 0 else [],
                                    inc=cx.pe_sem if p == NP - 1 else None)
                            bias = cvec[:, 0:1] if (masked and kc >= NKC // 2) else zero[:, :]
                            tkp_last = b.op("act", lambda e, kc=kc, sb_=sb_, q2=q2, bias=bias: e.activation(
                                out=pb[q2][:, kc, :], in_=ps_s[sb_][:, :], func=AF.Exp, bias=bias, scale=scale),
                                waits=[tks, tk1, tk_cv, pb_free[q2]] if kc == 0 else [tks], inc=sA)
                            s_free[sb_] = tkp_last
                        q_free[q2] = tks
                        l2 = nO % 2
                        for kc in range(NKC):
                            tkl = b.op("pe", lambda e, kc=kc, l2=l2, q2=q2: e.matmul(
                                ps_l[l2][:, :], lhsT=ones[:, :], rhs=pb[q2][:, kc, :], start=(kc == 0), stop=(kc == NKC - 1)),
                                waits=[tkp_last, l_freeP[l2]] if kc == 0 else [], inc=cx.pe_sem if kc == NKC - 1 else None)
                        tkrl = b.op("dve", lambda e, l2=l2: e.reciprocal(out=rl[:, :], in_=ps_l[l2][:, :]), waits=[tkl, rl_free], inc=sD)
                        l_freeP[l2] = tkrl
                        o2 = nQ % 2
                        tko = None
                        for dvc in range(ndv):
                            ob2 = nO % 2
                            nO += 1
                            for kc in range(NKC):
                                tkpv = b.op("pe", lambda e, kc=kc, dvc=dvc, ob2=ob2, hs=hs, q2=q2: e.matmul(
                                    ps_o[ob2][:, :], lhsT=vt[hs][:, kc, dvc * 128:(dvc + 1) * 128], rhs=pb[q2][:, kc, :],
                                    start=(kc == 0), stop=(kc == NKC - 1)),
                                    waits=[o_freeP[ob2]] if kc == 0 else [], inc=cx.pe_sem if kc == NKC - 1 else None)
                            tko = b.op("dve", lambda e, dvc=dvc, ob2=ob2, o2=o2: e.tensor_tensor(
                                out=ob[o2][:, dvc, :], in0=ps_o[ob2][:, :], in1=rl[:, :], op=ALU.mult),
                                waits=[tkpv, tkrl, ob_free[o2]] if dvc == 0 else [tkpv], inc=sD)
                            o_freeP[ob2] = tko
                        rl_free = tko
                        pb_free[q2] = tkpv
                        last_pe = tkpv
                        outv = out_ap[h * ndv * 128:(h + 1) * ndv * 128, qi * 512:(qi + 1) * 512].rearrange("(c p) t -> p c t", p=128)
                        ob_free[o2] = b.dma("sp", outv, ob[o2][:, :, :], waits=[tko], inc=oss[o2])
                    k_free[hs] = last_pe
                b.wait("sp", ob_free)
                b.flush(tag)
            phase_reset()

        def attn_sp(tag, nh, qparts, kparts, vsrc, nk, scale, masked, out_ap):
            NKC = nk // 128
            NP = len(qparts(0))
            NQ = T // 512
            with ExitStack() as st_:
                sbp = lambda n, s_, d: st_.enter_context(nc.sbuf_tensor(f"sb{tag}_{n}", s_, d))
                kt = [sbp(f"kt{i}", [128, NP, nk], BF16) for i in range(2)]
                vt = [sbp(f"vt{i}", [128, NKC, 128], BF16) for i in range(2)]
                qt = [sbp(f"qt{i}", [128, NP, 512], BF16) for i in range(2)]
                pb = [sbp(f"pb{i}", [128, NKC, 512], BF16) for i in range(2)]
                rl = sbp("rl", [128, 512], F32)
                ob = [sbp(f"ob{i}", [128, 512], BF16) for i in range(2)]
                ones = sbp("ones", [128, 128], BF16)
                zero = sbp("zero", [128, 1], F32)
                tk1 = b.op("dve", lambda e: e.memset(ones[:, :], 1.0), inc=sD)
                tk1 = b.op("dve", lambda e: e.memset(zero[:, :], 0.0), inc=sD)
                ks = [b.sem(f"{tag}_ks{i}") for i in range(2)]
                qs = [b.sem(f"{tag}_qs{i}") for i in range(2)]
                oss = [b.sem(f"{tag}_os{i}") for i in range(2)]
                ps_s = cx.psum[0:3]
                ps_o = cx.psum[3:5]
                ps_l = cx.psum[5:7]
                s_free = [None] * 3
                o_freeP = [None, None]
                l_freeP = [None, None]
                k_free = [None, None]
                q_free = [None, None]
                pb_free = [None, None]
                ob_free = [None, None]
                rl_free = None
                nS = 0
                N = nh * NQ
                tkk_h = {}
                exp_last = {}
                for n in range(N + 1):
                    if n < N:
                        h, qi = divmod(n, NQ)
                        hs, q2 = h % 2, n % 2
                        if qi == 0:
                            tkk = None
                            for p, (ap, r0, rows) in enumerate(kparts(h)):
                                tkk = b.dma("sp", kt[hs][0:rows, p, :], ap[r0:r0 + rows, 0:nk], waits=[k_free[hs]], inc=ks[hs])
                            vv = vsrc(h).rearrange("(kc p) n -> p kc n", p=128)
                            for k0 in range(0, NKC, 8):
                                k1 = min(NKC, k0 + 8)
                                tkk = b.dma("sp", vt[hs][:, k0:k1, :], vv[:, k0:k1, :], waits=[k_free[hs]], inc=ks[hs])
                            tkk_h[h] = tkk
                        tkk = tkk_h[h]
                        tkq = None
                        for p, (ap, r0, rows) in enumerate(qparts(h)):
                            tkq = b.dma("sp", qt[q2][0:rows, p, :], ap[r0:r0 + rows, qi * 512:(qi + 1) * 512],
                                        waits=[q_free[q2]], inc=qs[q2])
                    if n >= 1:
                        m = n - 1
                        hm, qm = divmod(m, NQ)
                        hsm, m2 = hm % 2, m % 2
                    tks = tkpv = tkl = None
                    for kc in range(NKC):
                        if n < N:
                            sb_ = nS % 3
                            nS += 1
                            for p, (ap, r0, rows) in enumerate(kparts(h)):
                                tks = b.op("pe", lambda e, p=p, rows=rows, kc=kc, sb_=sb_, hs=hs, q2=q2: e.matmul(
                                    ps_s[sb_][:, :], lhsT=kt[hs][0:rows, p, kc * 128:(kc + 1) * 128],
                                    rhs=qt[q2][0:rows, p, :], start=(p == 0), stop=(p == NP - 1)),
                                    waits=([tkk, tkq, s_free[sb_]] if kc == 0 else [s_free[sb_]]) if p == 0 else [],
                                    inc=cx.pe_sem if p == NP - 1 else None)
                            bias = cvec[:, 0:1] if (masked and kc >= NKC // 2) else zero[:, :]
                            tke = b.op("act", lambda e, kc=kc, sb_=sb_, q2=q2, bias=bias: e.activation(
                                out=pb[q2][:, kc, :], in_=ps_s[sb_][:, :], func=AF.Exp, bias=bias, scale=scale),
                                waits=[tks, tk1, tk_cv, pb_free[q2]] if kc == 0 else [tks], inc=sA)
                            s_free[sb_] = tke
                            exp_last[n] = tke
                        if n >= 1:
                            tkl = b.op("pe", lambda e, kc=kc, m2=m2: e.matmul(
                                ps_l[m2][:, :], lhsT=ones[:, :], rhs=pb[m2][:, kc, :], start=(kc == 0), stop=(kc == NKC - 1)),
                                waits=[exp_last[m], l_freeP[m2]] if kc == 0 else [], inc=cx.pe_sem if kc == NKC - 1 else None)
                            tkpv = b.op("pe", lambda e, kc=kc, m2=m2, hsm=hsm: e.matmul(
                                ps_o[m2][:, :], lhsT=vt[hsm][:, kc, :], rhs=pb[m2][:, kc, :], start=(kc == 0), stop=(kc == NKC - 1)),
                                waits=[o_freeP[m2]] if kc == 0 else [], inc=cx.pe_sem if kc == NKC - 1 else None)
                    if n < N:
                        q_free[q2] = tks
                    if n >= 1:
                        tkrl = b.op("dve", lambda e, m2=m2: e.reciprocal(out=rl[:, :], in_=ps_l[m2][:, :]), waits=[tkl, rl_free], inc=sD)
                        l_freeP[m2] = tkrl
                        tko = b.op("dve", lambda e, m2=m2: e.tensor_tensor(out=ob[m2][:, :], in0=ps_o[m2][:, :], in1=rl[:, :], op=ALU.mult),
                                   waits=[tkpv, tkrl, ob_free[m2]], inc=sD)
                        o_freeP[m2] = tko
                        rl_free = tko
                        pb_free[m2] = tkpv
                        if qm == NQ - 1:
                            k_free[hsm] = tkpv
                        ob_free[m2] = b.dma("sp", out_ap[hm * 128:(hm + 1) * 128, qm * 512:(qm + 1) * 512], ob[m2][:, :],
                                            waits=[tko], inc=oss[m2])
                    b.maybe_flush(4000)
                b.wait("sp", ob_free)
                b.flush(tag)
            phase_reset()

        attn_sp("p4", 16,
                lambda h: [(S["qnT"], h * 128, 128), (S["qrT"], h * 64, 64)],
                lambda h: [(S["knT"], h * 128, 128), (S["kropeT"], 0, 64)],
                lambda h: S["vmla"][:, h * 128:(h + 1) * 128], TT, 192.0 ** -0.5, True, S["omlaT"])
        if stop_after == "p4":
            return nc

        scratch("qdT", [2, 1024, T], BF16)
        scratch("kiT", [2, 1024, T], BF16)
        scratch("kte", [2, TT, 1024], BF16)
        scratch("ofwd", [T, 2048], F32)
        scratch("ogla", [T, 2048], F32)
        with ExitStack() as st5:
            sb5 = lambda n, s_, d: st5.enter_context(nc.sbuf_tensor(f"p5_{n}", s_, d))
            dec = sb5("dec", [128, 2, 8, 64], F32)
            with ExitStack() as st_:
                sbp = lambda n, s_, d: st_.enter_context(nc.sbuf_tensor(f"p5a_{n}", s_, d))
                lrA = [sbp(f"lrA{d_}", [32, TT], F32) for d_ in range(2)]
                w2b = [sbp(f"w2b{d_}", [32, 1024], F32) for d_ in range(2)]
                M1 = [sbp(f"M1{d_}", [128, 130], F32) for d_ in range(2)]
                M2 = [sbp(f"M2{d_}", [128, 128], F32) for d_ in range(2)]
                onec = sbp("onec", [128, 1], F32)
                la = sbp("la", [128, 1024], F32)
                E1 = sbp("E1", [128, 8, 128], F32)
                E2 = sbp("E2", [128, 8, 128], F32)
                E3 = sbp("E3", [128, 1024], F32)
                gqb = [sbp(f"gqb{i}", [128, 8, 128], F32) for i in range(2)]
                gkb = [sbp(f"gkb{i}", [128, 8, 128], F32) for i in range(2)]
                gkt = [sbp(f"gkt{i}", [128, 1024], F32) for i in range(2)]
                qdo = [sbp(f"qdo{i}", [128, 8, 128], BF16) for i in range(2)]
                kio = [sbp(f"kio{i}", [128, 8, 128], BF16) for i in range(2)]
                kto = [sbp(f"kto{i}", [128, 1024], BF16) for i in range(2)]
                lsem = [b.sem(f"p5a_l{i}") for i in range(2)]
                osem = [b.sem(f"p5a_o{i}") for i in range(2)]
                tkc = b.op("dve", lambda e: e.memset(onec[:, :], 1.0), inc=sD)
                tkm = []
                for d_ in range(2):
                    t_ = b.op("dve", lambda e, d_=d_: e.memset(lrA[d_][:, :], 1.0), inc=sD)
                    tkm.append(b.dma("sp", lrA[d_][0:16, :], S["glrT"][d_ * 32:d_ * 32 + 16, :], waits=[t_], inc=b.sem(f"c5lr{d_}")))
                    tkm.append(b.dma("sp", w2b[d_][0:17, :], I["w2b"][d_, :, :], inc=b.sem(f"c5w{d_}")))
                    tkm.append(b.dma("sp", M1[d_][:, :], I["M1"][d_, :, :], inc=b.sem(f"c5m1{d_}")))
                    tkm.append(b.dma("sp", M2[d_][:, :], I["M2"][d_, :, :], inc=b.sem(f"c5m2{d_}")))
                psg = cx.psum[0:2]
                psc = cx.psum[2:6]
                psd = cx.psum[6:8]
                l_free = [None, None]
                o_free = [None, None]
                psg_free = None
                psc_free = None
                psd_free = None
                la_free = None
                e_free = None
                e3_free = None
                gqT_v, gkT_v = kc_view(S["gqT"]), kc_view(S["gkT"])
                it = 0
                for blk in range(TT // 128):
                    own = blk < T // 128
                    s2 = blk % 2
                    tsl = slice(blk * 128, (blk + 1) * 128)
                    tkl = None
                    if own:
                        b.dma("sp", gqb[s2][:, :, :], gqT_v[:, :, tsl], waits=[l_free[s2]], inc=lsem[s2])
                        b.dma("sp", gkb[s2][:, :, :], gkT_v[:, :, tsl], waits=[l_free[s2]], inc=lsem[s2])
                    tkl = b.dma("sp", gkt[s2][:, :], S["gk"][tsl, :], waits=[l_free[s2]], inc=lsem[s2])
                    for d_ in range(2):
                        o2 = it % 2
                        it += 1
                        for hf_ in range(2):
                            tkg = b.op("pe", lambda e, d_=d_, hf_=hf_, tsl=tsl: e.matmul(
                                psg[hf_][:, :], lhsT=lrA[d_][0:17, tsl], rhs=w2b[d_][0:17, hf_ * 512:(hf_ + 1) * 512],
                                start=True, stop=True), waits=[tkm, psg_free] if hf_ == 0 else [], inc=cx.pe_sem if hf_ == 1 else None)
                        for hf_ in range(2):
                            tke = b.op("act", lambda e, hf_=hf_: e.activation(out=la[:, hf_ * 512:(hf_ + 1) * 512], in_=psg[hf_][:, :],
                                                                               func=AF.Exp, scale=-1.0),
                                       waits=[tkg, la_free] if hf_ == 0 else [], inc=sA)
                        psg_free = tke
                        tkla = b.op("act", lambda e: e.activation(out=la[:, :], in_=la[:, :], func=AF.Ln, bias=onec[:, :], scale=1.0),
                                    waits=[tke, tkc], inc=sA)
                        for fc in range(8):
                            tkcm = b.op("pe", lambda e, d_=d_, fc=fc: e.matmul(
                                psc[fc // 2][:, (fc % 2) * 256:(fc % 2) * 256 + 130], lhsT=la[:, fc * 128:(fc + 1) * 128], rhs=M1[d_][:, :],
                                start=True, stop=True), waits=[tkla, psc_free] if fc == 0 else [], inc=cx.pe_sem if fc == 7 else None)
                        for hf_ in range(2):
                            tkdm = b.op("pe", lambda e, d_=d_, hf_=hf_: e.matmul(
                                psd[hf_][:, :], lhsT=M2[d_][:, :], rhs=la[:, hf_ * 512:(hf_ + 1) * 512], start=True, stop=True),
                                waits=[psd_free] if hf_ == 0 else [], inc=cx.pe_sem if hf_ == 1 else None)
                        la_free = tkdm
                        for bk in range(4):
                            tkdc = b.op("act", lambda e, d_=d_, bk=bk, blk=blk: e.activation(
                                out=dec[:, d_, 2 * bk:2 * bk + 2, 2 * blk:2 * blk + 2],
                                in_=psc[bk][:, :].rearrange("p (a c) -> p a c", a=2)[:, :, 128:130], func=AF.Exp),
                                waits=[tkcm] if bk == 0 else [], inc=sA)
                        last_c = tkdc
                        if own:
                            for bk in range(4):
                                src = psc[bk][:, :].rearrange("p (a c) -> p a c", a=2)[:, :, 0:128]
                                b.op("act", lambda e, bk=bk, src=src: e.activation(out=E1[:, 2 * bk:2 * bk + 2, :], in_=src, func=AF.Exp),
                                     waits=[e_free] if bk == 0 else [])
                                last_c = b.op("act", lambda e, bk=bk, src=src: e.activation(out=E2[:, 2 * bk:2 * bk + 2, :], in_=src, func=AF.Exp, scale=-1.0),
                                              inc=sA)
                        psc_free = last_c
                        for hf_ in range(2):
                            tke3 = b.op("act", lambda e, hf_=hf_: e.activation(out=E3[:, hf_ * 512:(hf_ + 1) * 512], in_=psd[hf_][:, :], func=AF.Exp),
                                        waits=[tkdm, e3_free] if hf_ == 0 else [], inc=sA)
                        psd_free = tke3
                        tko = None
                        if own:
                            b.op("dve", lambda e, s2=s2, o2=o2: e.scalar_tensor_tensor(
                                out=qdo[o2][:, :, :], in0=gqb[s2][:, :, :], scalar=1.0 / 16.0, in1=E1[:, :, :], op0=ALU.mult, op1=ALU.mult),
                                waits=[last_c, tkl, o_free[o2]])
                            tko = b.op("dve", lambda e, s2=s2, o2=o2: e.tensor_tensor(
                                out=kio[o2][:, :, :], in0=gkb[s2][:, :, :], in1=E2[:, :, :], op=ALU.mult), inc=sD)
                            e_free = tko
                        tko = b.op("dve", lambda e, s2=s2, o2=o2: e.tensor_tensor(
                            out=kto[o2][:, :], in0=gkt[s2][:, :], in1=E3[:, :], op=ALU.mult),
                            waits=[tke3, tkl, o_free[o2]], inc=sD)
                        e3_free = tko
                        if own:
                            b.dma("sp", kc_view(S["qdT"][d_])[:, :, tsl], qdo[o2][:, :, :], waits=[tko], inc=osem[o2])
                            b.dma("sp", kc_view(S["kiT"][d_])[:, :, tsl], kio[o2][:, :, :], waits=[tko], inc=osem[o2])
                        o_free[o2] = b.dma("sp", S["kte"][d_][tsl, :], kto[o2][:, :], waits=[tko], inc=osem[o2])
                    l_free[s2] = tko
                    b.maybe_flush(2000)
                b.wait("sp", o_free)
                b.flush("p5a")
            phase_reset()
            if stop_after == "p5a":
                return nc

            with ExitStack() as st_:
                sbp = lambda n, s_, d: st_.enter_context(nc.sbuf_tensor(f"p5b_{n}", s_, d))
                Sf = sbp("S", [128, 8, 512], F32)
                Sb = [sbp(f"Sb{i}", [128, 8, 512], BF16) for i in range(2)]
                vb = [sbp(f"v{i}", [64, 2048], BF16) for i in range(3)]
                kb = [sbp(f"k{i}", [64, 1024], BF16) for i in range(3)]
                qd = [sbp(f"qd{i}", [128, 8, 64], BF16) for i in range(2)]
                ki = [sbp(f"ki{i}", [128, 8, 64], BF16) for i in range(2)]
                Ab = sbp("Ab", [64, 256], BF16)
                mk = [sbp(f"mk{d_}", [64, 256], F32) for d_ in range(2)]
                ost = [sbp(f"ost{i}", [64, 2048], F32) for i in range(2)]
                ofl = [sbp(f"ofl{i}", [64, 2048], F32) for i in range(2)]
                grl = [sbp(f"grl{i}", [64, 2048], F32) for i in range(2)]
                gnb = sbp("gnb", [64, 2048], F32)
                ogb = [sbp(f"ogb{i}", [64, 2048], F32) for i in range(2)]
                ssq = sbp("ssq", [64, 4], F32)
                rr = sbp("rr", [64, 4], F32)
                junk = sbp("junk", [64, 512], F32)
                epsr = sbp("epsr", [64, 1], F32)
                tk_e = b.op("dve", lambda e: e.memset(epsr[:, :], RMS_EPS), inc=sD)
                tkmk = [b.dma("sp", mk[d_][:, :], I["maskT"][d_, :, :], inc=b.sem(f"c5mk{d_}")) for d_ in range(2)]
                tkgn = b.dma("sp", gnb[:, :], I["gnorm_bc"][:, :], inc=b.sem("c5gn"))
                ldsem = [b.sem(f"p5b_ld{i}") for i in range(3)]
                qsem = [b.sem(f"p5b_q{i}") for i in range(2)]
                xsem = [b.sem(f"p5b_x{i}") for i in range(2)]
                stsem = [b.sem(f"p5b_st{i}") for i in range(2)]
                ps_a = cx.psum[0]
                ps_o = cx.psum[1:5]
                ps_kv = cx.psum[5:8]
                ld_free = [None] * 3
                q_free = [None, None]
                x_free = [None, None]
                st_free = [None, None]
                og_free = [None, None]
                a_free = None
                ab_free = None
                o_freeP = None
                kv_free = [None] * 3
                sb_ready = None
                sb_rd = [None, None]
                s_last = None
                cast_prev = [None] * 8
                cast_last = None
                nld = 0
                nq = 0
                nkv = 0
                nx = 0
                cur = 0
                for d_ in range(2):
                    tkz = b.op("dve", lambda e: e.memset(Sf[:, :, :], 0.0), waits=[s_last, cast_last], inc=sD)
                    tkz2 = b.op("dve", lambda e, cur=cur: e.memset(Sb[cur][:, :, :], 0.0), waits=[sb_rd[cur], cast_last], inc=sD)
                    s_last = tkz2
                    sb_ready = tkz2
                    cast_prev = [None] * 8
                    order = list(range(32)) if d_ == 0 else list(range(31, -1, -1))
                    seq = [("other", c) for c in order] + [("flag", 0)] + [("own", c) for c in order]
                    for (kind, c) in seq:
                        if kind == "flag":
                            tkf = b.op("dve", lambda e, d_=d_: e.tensor_scalar(out=Sf[:, :, :], in0=Sf[:, :, :], scalar1=cvec[:, 1 + d_:2 + d_],
                                                                               scalar2=None, op0=ALU.mult), waits=[s_last, tk_cv, cast_last], inc=sD)
                            s_last = tkf
                            tkf2 = b.op("act", lambda e, cur=cur: e.activation(out=Sb[cur][:, :, :], in_=Sf[:, :, :], func=AF.Copy),
                                        waits=[tkf, sb_rd[cur]], inc=sA)
                            sb_ready = tkf2
                            cast_last = tkf2
                            cast_prev = [tkf2] * 8
                            continue
                        gch = c + (32 if kind == "other" else 0)
                        r0 = gch * 64
                        l3 = nld % 3
                        nld += 1
                        b.dma("sp", vb[l3][:, :], S["gv"][r0:r0 + 64, :], waits=[ld_free[l3]], inc=ldsem[l3])
                        tkld = b.dma("sp", kb[l3][:, :], S["kte"][d_][r0:r0 + 64, :], waits=[ld_free[l3]], inc=ldsem[l3])
                        nxt = 1 - cur
                        def state_update(l3=l3, gch=gch, nxt=nxt, d_=d_, tkld=tkld):
                            nonlocal nkv, s_last, cast_last, sb_ready
                            tkc_last = tkkv = tks = None
                            for h in range(4):
                                for dc in range(2):
                                    hd = h * 2 + dc
                                    k3 = nkv % 3
                                    nkv += 1
                                    tkkv = b.op("pe", lambda e, h=h, hd=hd, l3=l3, k3=k3: e.matmul(
                                        ps_kv[k3][:, :], lhsT=kb[l3][0:64, hd * 128:(hd + 1) * 128], rhs=vb[l3][0:64, h * 512:(h + 1) * 512],
                                        start=True, stop=True), waits=[tkld, kv_free[k3]], inc=cx.pe_sem)
                                    tks = b.op("dve", lambda e, hd=hd, k3=k3, d_=d_, gch=gch: e.scalar_tensor_tensor(
                                        out=Sf[:, hd, :], in0=Sf[:, hd, :], scalar=dec[:, d_, hd, gch:gch + 1], in1=ps_kv[k3][:, :],
                                        op0=ALU.mult, op1=ALU.add), waits=[tkkv, s_last, cast_prev[hd]], inc=sD)
                                    kv_free[k3] = tks
                                    tkc_last = b.op("act", lambda e, hd=hd, nxt=nxt: e.activation(out=Sb[nxt][:, hd, :], in_=Sf[:, hd, :], func=AF.Copy),
                                                    waits=[tks, sb_rd[nxt]] if hd == 0 else [tks], inc=sA)
                                    cast_prev[hd] = tkc_last
                                    cast_last = tkc_last
                            s_last = tks
                            sb_ready = tkc_last
                            return tkkv
                        if kind == "own":
                            q2 = nq % 2
                            nq += 1
                            b.dma("sp", qd[q2][:, :, :], kc_view(S["qdT"][d_])[:, :, r0:r0 + 64], waits=[q_free[q2]], inc=qsem[q2])
                            tkq = b.dma("sp", ki[q2][:, :, :], kc_view(S["kiT"][d_])[:, :, r0:r0 + 64], waits=[q_free[q2]], inc=qsem[q2])
                            for h in range(4):
                                for dc in range(2):
                                    tka = b.op("pe", lambda e, h=h, dc=dc, q2=q2: e.matmul(
                                        ps_a[0:64, h * 64:(h + 1) * 64], lhsT=ki[q2][:, h * 2 + dc, :], rhs=qd[q2][:, h * 2 + dc, :],
                                        start=(dc == 0), stop=(dc == 1)),
                                        waits=[tkq, a_free] if (h == 0 and dc == 0) else [], inc=cx.pe_sem if (h == 3 and dc == 1) else None)
                            tkab = b.op("dve", lambda e, d_=d_: e.tensor_tensor(out=Ab[:, :], in0=ps_a[0:64, 0:256], in1=mk[d_][:, :], op=ALU.mult),
                                        waits=[tka, tkmk, ab_free], inc=sD)
                            a_free = tkab
                            sb_ready_prev = sb_ready
                            state_update()
                            for h in range(4):
                                for dc in range(2):
                                    b.op("pe", lambda e, h=h, dc=dc, q2=q2, cur=cur: e.matmul(
                                        ps_o[h][0:64, :], lhsT=qd[q2][:, h * 2 + dc, :], rhs=Sb[cur][:, h * 2 + dc, :],
                                        start=(dc == 0), stop=False),
                                        waits=[sb_ready_prev, o_freeP] if (h == 0 and dc == 0) else [])
                                tko_ = b.op("pe", lambda e, h=h, l3=l3: e.matmul(
                                    ps_o[h][0:64, :], lhsT=Ab[0:64, h * 64:(h + 1) * 64], rhs=vb[l3][0:64, h * 512:(h + 1) * 512],
                                    start=False, stop=True), waits=[tkab, tkld] if h == 0 else [], inc=cx.pe_sem if h == 3 else None)
                            ab_free = tko_
                            q_free[q2] = tko_
                            sb_rd[cur] = tko_
                            ld_free[l3] = tko_
                            x2 = nx % 2
                            nx += 1
                            if d_ == 0:
                                for h in range(4):
                                    tkev = b.op("act", lambda e, h=h, x2=x2: e.activation(out=ost[x2][:, h * 512:(h + 1) * 512], in_=ps_o[h][0:64, :], func=AF.Copy),
                                                waits=[tko_, st_free[x2]] if h == 0 else [], inc=sA)
                                o_freeP = tkev
                                st_free[x2] = b.dma("sp", S["ofwd"][r0:r0 + 64, :], ost[x2][:, :], waits=[tkev], inc=stsem[x2])
                            else:
                                b.dma("sp", ofl[x2][:, :], S["ofwd"][r0:r0 + 64, :], waits=[x_free[x2]], inc=xsem[x2])
                                tkx = b.dma("sp", grl[x2][:, :], S["gr"][r0:r0 + 64, :], waits=[x_free[x2]], inc=xsem[x2])
                                for h in range(4):
                                    tkad = b.op("dve", lambda e, h=h, x2=x2: e.tensor_tensor(
                                        out=ost[x2][:, h * 512:(h + 1) * 512], in0=ps_o[h][0:64, :], in1=ofl[x2][:, h * 512:(h + 1) * 512], op=ALU.add),
                                        waits=[tko_, tkx, st_free[x2]] if h == 0 else [], inc=sD)
                                o_freeP = tkad
                                for h in range(4):
                                    tksq = b.op("act", lambda e, h=h, x2=x2: e.activation(out=junk[:, :], in_=ost[x2][:, h * 512:(h + 1) * 512],
                                                                                            func=AF.Square, accum_out=ssq[:, h:h + 1]),
                                                waits=[tkad] if h == 0 else [], inc=sA)
                                tksd = b.op("act", lambda e: e.activation(out=rr[:, :], in_=ssq[:, :], func=AF.Sqrt, bias=epsr[:, :], scale=1.0 / 512.0),
                                            waits=[tksq, tk_e], inc=sA)
                                tksl = b.op("act", lambda e, x2=x2: e.activation(out=grl[x2][:, :], in_=grl[x2][:, :], func=AF.Silu), waits=[tkx], inc=sA)
                                tkrc = b.op("dve", lambda e: e.reciprocal(out=rr[:, :], in_=rr[:, :]), waits=[tksd], inc=sD)
                                tkg2 = b.op("dve", lambda e, x2=x2: e.tensor_tensor(out=grl[x2][:, :], in0=grl[x2][:, :], in1=gnb[:, :], op=ALU.mult),
                                            waits=[tksl, tkgn], inc=sD)
                                for h in range(4):
                                    tkfin = b.op("dve", lambda e, h=h, x2=x2: e.scalar_tensor_tensor(
                                        out=ogb[x2][:, h * 512:(h + 1) * 512], in0=ost[x2][:, h * 512:(h + 1) * 512], scalar=rr[:, h:h + 1],
                                        in1=grl[x2][:, h * 512:(h + 1) * 512], op0=ALU.mult, op1=ALU.mult),
                                        waits=[tkrc, tkg2, og_free[x2]] if h == 0 else [], inc=sD)
                                x_free[x2] = tkfin
                                st_free[x2] = tkfin
                                og_free[x2] = b.dma("sp", S["ogla"][r0:r0 + 64, :], ogb[x2][:, :], waits=[tkfin], inc=stsem[x2])
                        if kind == "other":
                            ld_free[l3] = state_update()
                        cur = nxt
                        b.maybe_flush(2500)
                b.wait("sp", [st_free, og_free])
                b.flush("p5b")
            phase_reset()
        if stop_after == "p5b":
            return nc

        def tm2fm(tag, src, ncols, dst, nrows=T):
            NC_ = ncols // 128
            with ExitStack() as st_:
                sbp = lambda n, s_, d: st_.enter_context(nc.sbuf_tensor(f"sb{tag}_{n}", s_, d))
                xin = [sbp(f"x{i}", [128, ncols], F32) for i in range(2)]
                xo = [sbp(f"o{i}", [128, NC_, 128], BF16) for i in range(2)]
                ls = [b.sem(f"{tag}_l{i}") for i in range(2)]
                ss = [b.sem(f"{tag}_s{i}") for i in range(2)]
                l_free = [None, None]
                o_free = [None, None]
                pfree = [None] * 8
                dstv = kc_view(dst)
                nb = 0
                for i in range(nrows // 128):
                    s2 = i % 2
                    tkl = b.dma("sp", xin[s2][:, :], src[i * 128:(i + 1) * 128, :], waits=[l_free[s2]], inc=ls[s2])
                    tke = None
                    for q4 in range(NC_ // 4):
                        pi = nb % 8
                        nb += 1
                        bank = cx.psum[pi]
                        for j in range(4):
                            fc = q4 * 4 + j
                            tkt = b.op("pe", lambda e, fc=fc, j=j, s2=s2, bank=bank: e.transpose(
                                out=bank[:, j * 128:(j + 1) * 128], in_=xin[s2][:, fc * 128:(fc + 1) * 128], identity=ident[:, :]),
                                waits=[tkl, tk_id, pfree[pi]] if j == 0 else [], inc=cx.pe_sem if j == 3 else None)
                        eng = "dve" if q4 % 2 == 0 else "act"
                        o_ = xo[s2][:, q4 * 4:(q4 + 1) * 4, :]
                        src_ = bank[:, :].rearrange("p (a c) -> p a c", a=4)
                        if eng == "dve":
                            tke = b.op("dve", lambda e, o_=o_, src_=src_: e.tensor_copy(out=o_, in_=src_), waits=[tkt, o_free[s2]], inc=sD)
                        else:
                            tke = b.op("act", lambda e, o_=o_, src_=src_: e.activation(out=o_, in_=src_, func=AF.Copy), waits=[tkt, o_free[s2]], inc=sA)
                        pfree[pi] = tke
                    l_free[s2] = tkt
                    o_free[s2] = b.dma("sp", dstv[:, :, i * 128:(i + 1) * 128], xo[s2][:, :, :], waits=[(sD, sD.v), (sA, sA.v)], inc=ss[s2])
                b.wait("sp", o_free)
                b.flush(tag)
            phase_reset()

        tm2fm("p5t", S["ogla"], 2048, S["oglaT"])
        if stop_after == "p5":
            return nc

        scratch("pre1T", [D, T], F32)
        scratch("pre2T", [D, T], F32)
        scratch("pre3T", [D, T], F32)

        def fn_axpy(gp, src_dram, scale, which="A"):
            def fn(eng, o, src, tmp, info, w, si, ci):
                vc0, cw, t0, nt = info
                ldt, tkl = gp.ld(which, si, ci, src_dram[vc0:vc0 + cw, t0:t0 + nt], cw, nt)
                tk = b.op("dve", lambda e: e.scalar_tensor_tensor(out=o, in0=ldt, scalar=scale, in1=src, op0=ALU.mult, op1=ALU.add),
                          waits=w + [tkl], inc=sD)
                gp.ld_done(which, si, ci, tk)
                return tk
            return fn

        gp = GP("p6")

        def fn_bm(eng, o, src, tmp, info, w, si, ci):
            vc0, cw, t0, nt = info
            g0, tkl = gp.ld("A", si, ci, S["gateT"][vc0:vc0 + cw, t0:t0 + nt], cw, nt)
            tk = b.op("dve", lambda e: e.tensor_tensor(out=o, in0=src, in1=g0, op=ALU.mult), waits=w + [tkl], inc=sD)
            gp.ld_done("A", si, ci, tk)
            return tk

        def fn_bg(eng, o, src, tmp, info, w, si, ci):
            vc0, cw, t0, nt = info
            g1, tkl = gp.ld("A", si, ci, S["gateT"][D + vc0:D + vc0 + cw, t0:t0 + nt], cw, nt)
            bm, tkl2 = gp.ld("B", si, ci, S["bmT"][vc0:vc0 + cw, t0:t0 + nt], cw, nt)
            tk = b.op("dve", lambda e: e.tensor_tensor(out=tmp, in0=src, in1=g1, op=ALU.mult), waits=w + [tkl], inc=sD)
            gp.ld_done("A", si, ci, tk)
            tk = b.op("dve", lambda e: e.tensor_tensor(out=o, in0=tmp, in1=bm, op=ALU.add), waits=[tk, tkl2], inc=sD)
            gp.ld_done("B", si, ci, tk)
            return tk
        gemm(b, cx, "br_mla", S["omlaT"], 2048, 0, T, [(I["w_br_mla"], 0, D)], "fm", gp.epi("fm", [(0, D, S["bmT"], 0)], fn=fn_bm))
        b.wait("sp", gp.stg_free)
        b.flush("p6mid")
        gemm(b, cx, "br_gla", S["oglaT"], 2048, 0, T, [(I["w_br_gla"], 0, D)], "fm",
             gp.epi("fm", [(0, D, S["mergedT"], 0)], dtype=BF16, fn=fn_bg))
        gp.close("p6")
        phase_reset()
        if stop_after == "p6":
            return nc

        gp = GP("p7")
        gemm(b, cx, "mix_out", S["mergedT"], D, 0, T, [(I["w_mix_out"], 0, D)], "fm",
             gp.epi("fm", [(0, D, S["pre1T"], 0)], fn=fn_axpy(gp, S["hT_f32"], ALPHA)))
        gp.close("p7")
        phase_reset()

        def ln_fm(tag, pre, gname, bname, out_f32, out_bf, out_tm=None):
            TL = 256
            NT = T // TL
            with ExitStack() as st_:
                sbp = lambda n, s_, d: st_.enter_context(nc.sbuf_tensor(f"sb{tag}_{n}", s_, d))
                X = [sbp(f"X{i}", [128, 32, TL], F32) for i in range(2)]
                Q = sbp("Q", [128, 32, TL], F32)
                Yf = sbp("Yf", [128, 32, TL], F32)
                Yb = sbp("Yb", [128, 32, TL], BF16) if out_bf is not None else None
                onesF = sbp("ones", [128, 128], F32)
                m = [sbp(f"m{i}", [128, TL], F32) for i in range(2)]
                msq = sbp("msq", [128, TL], F32)
                var = sbp("var", [128, TL], F32)
                rstd = [sbp(f"rstd{i}", [128, TL], F32) for i in range(2)]
                gc = sbp("g", [128, 32], F32)
                bc = sbp("b", [128, 32], F32)
                eps = sbp("eps", [128, 1], F32)
                yst = [sbp(f"yst{i}", [128, D], F32) for i in range(2)] if out_tm is not None else None
                tk1 = b.op("dve", lambda e: e.memset(onesF[:, :], 1.0), inc=sD)
                tk1 = b.op("dve", lambda e: e.memset(eps[:, :], LN_EPS), inc=sD)
                tkg = [b.dma("sp", gc[:, :], I[gname][:, :], inc=b.sem(f"{tag}_cg")),
                       b.dma("sp", bc[:, :], I[bname][:, :], inc=b.sem(f"{tag}_cb"))]
                xs = [b.sem(f"{tag}_xs{i}") for i in range(2)]
                osm = b.sem(f"{tag}_os")
                osm2 = b.sem(f"{tag}_os2")
                ysm = [b.sem(f"{tag}_ys{i}") for i in range(2)]
                ps_sum, ps_sq = cx.psum[0], cx.psum[1]
                st = {"x_free": [None, None], "q_free": None, "sum_free": None, "sq_free": None, "yf_free": None, "yb_free": None,
                      "m_free": [None, None], "ntp": 0}
                tkx_of, tks_of, tks2_of, tkr_of = {}, {}, {}, {}
                tp_free = [None] * 4
                yst_free = [None, None]
                prev = kc_view(pre)

                def frontA(i):
                    s2 = i % 2
                    tsl = slice(i * TL, (i + 1) * TL)
                    tkx = b.dma("sp", X[s2][:, :, :], prev[:, :, tsl], waits=[st["x_free"][s2]], inc=xs[s2])
                    tkx_of[i] = tkx
                    tkq = b.op("act", lambda e, s2=s2: e.activation(out=Q[:, :, :], in_=X[s2][:, :, :], func=AF.Square),
                               waits=[tkx, st["q_free"]], inc=sA)
                    tks = None
                    for fc in range(32):
                        tks = b.op("pe", lambda e, fc=fc, s2=s2: e.matmul(ps_sum[:, 0:TL], lhsT=onesF[:, :], rhs=X[s2][:, fc, :],
                                                                           start=(fc == 0), stop=(fc == 31)),
                                   waits=[tkx, tk1, st["sum_free"]] if fc == 0 else [], inc=cx.pe_sem if fc == 31 else None)
                    tks2 = None
                    for fc in range(32):
                        tks2 = b.op("pe", lambda e, fc=fc: e.matmul(ps_sq[:, 0:TL], lhsT=onesF[:, :], rhs=Q[:, fc, :],
                                                                     start=(fc == 0), stop=(fc == 31)),
                                    waits=[tkq, st["sq_free"]] if fc == 0 else [], inc=cx.pe_sem if fc == 31 else None)
                    st["q_free"] = tks2
                    tks_of[i], tks2_of[i] = tks, tks2

                def frontB(i):
                    s2 = i % 2
                    tkm = b.op("act", lambda e, s2=s2: e.activation(out=m[s2][:, :], in_=ps_sum[:, 0:TL], func=AF.Copy, scale=1.0 / D),
                               waits=[tks_of[i], st["m_free"][s2]], inc=sA)
                    st["sum_free"] = tkm
                    tkm2 = b.op("dve", lambda e, s2=s2: e.tensor_tensor(out=msq[:, :], in0=m[s2][:, :], in1=m[s2][:, :], op=ALU.mult), waits=[tkm], inc=sD)
                    tkv = b.op("dve", lambda e: e.scalar_tensor_tensor(out=var[:, :], in0=ps_sq[:, 0:TL], scalar=1.0 / D, in1=msq[:, :],
                                                                        op0=ALU.mult, op1=ALU.subtract), waits=[tks2_of[i], tkm2], inc=sD)
                    st["sq_free"] = tkv
                    tksd = b.op("act", lambda e: e.activation(out=var[:, :], in_=var[:, :], func=AF.Sqrt, bias=eps[:, :], scale=1.0), waits=[tkv], inc=sA)
                    tkr_of[i] = b.op("dve", lambda e, s2=s2: e.reciprocal(out=rstd[s2][:, :], in_=var[:, :]), waits=[tksd], inc=sD)

                def back(i):
                    s2 = i % 2
                    tsl = slice(i * TL, (i + 1) * TL)
                    tkr = tkr_of[i]
                    tkn = tky = None
                    for fc in range(32):
                        b.op("dve", lambda e, fc=fc, s2=s2: e.tensor_tensor(out=X[s2][:, fc, :], in0=X[s2][:, fc, :], in1=m[s2][:, :], op=ALU.subtract),
                             waits=[tkr, tks_of[i]] if fc == 0 else [])
                        tkn = b.op("dve", lambda e, fc=fc, s2=s2: e.tensor_tensor(out=X[s2][:, fc, :], in0=X[s2][:, fc, :], in1=rstd[s2][:, :], op=ALU.mult), inc=sD)
                        tky = b.op("act", lambda e, fc=fc, s2=s2: e.activation(out=Yf[:, fc, :], in_=X[s2][:, fc, :], func=AF.Identity,
                                                                                bias=bc[:, fc:fc + 1], scale=gc[:, fc:fc + 1]),
                                   waits=[tkn, tkg, st["yf_free"]] if fc == 0 else [tkn], inc=sA)
                    st["x_free"][s2] = tky
                    st["m_free"][s2] = tkn
                    outs_ = []
                    if out_f32 is not None:
                        outs_.append(b.dma("sp", kc_view(out_f32)[:, :, tsl], Yf[:, :, :], waits=[tky], inc=osm))
                    if out_bf is not None:
                        tkc = b.op("pool", lambda e: e.tensor_copy(out=Yb[:, :, :], in_=Yf[:, :, :]), waits=[tky, st["yb_free"]], inc=sP)
                        st["yb_free"] = b.dma("sp", kc_view(out_bf)[:, :, tsl], Yb[:, :, :], waits=[tkc], inc=osm2)
                        outs_.append(tkc)
                    if out_tm is not None:
                        for hb_ in range(TL // 128):
                            y2 = (i * (TL // 128) + hb_) % 2
                            tke = tkt = None
                            evs = []
                            for q4 in range(8):
                                pi = 2 + st["ntp"] % 4
                                st["ntp"] += 1
                                bank = cx.psum[pi]
                                for j in range(4):
                                    fc = q4 * 4 + j
                                    tkt = b.op("pe", lambda e, fc=fc, j=j, hb_=hb_, bank=bank: e.transpose(
                                        out=bank[:, j * 128:(j + 1) * 128], in_=Yf[:, fc, hb_ * 128:(hb_ + 1) * 128], identity=ident[:, :]),
                                        waits=[tky, tk_id, tp_free[pi - 2]] if j == 0 else [], inc=cx.pe_sem if j == 3 else None)
                                o_ = yst[y2][:, q4 * 512:(q4 + 1) * 512]
                                if q4 % 2 == 0:
                                    tke = b.op("dve", lambda e, o_=o_, bank=bank: e.tensor_copy(out=o_, in_=bank[:, :]), waits=[tkt, yst_free[y2]], inc=sD)
                                else:
                                    tke = b.op("act", lambda e, o_=o_, bank=bank: e.activation(out=o_, in_=bank[:, :], func=AF.Copy), waits=[tkt, yst_free[y2]], inc=sA)
                                tp_free[pi - 2] = tke
                                evs.append(tke)
                            r0 = i * TL + hb_ * 128
                            yst_free[y2] = b.dma("sp", out_tm[r0:r0 + 128, :], yst[y2][:, :], waits=evs, inc=ysm[y2])
                            outs_.append(tkt)
                    st["yf_free"] = outs_
                    st["last"] = outs_ + [st["yb_free"], yst_free]

                frontA(0)
                frontB(0)
                for i in range(NT):
                    if i + 1 < NT:
                        frontA(i + 1)
                    back(i)
                    if i + 1 < NT:
                        frontB(i + 1)
                b.wait("sp", st["last"])
                b.flush(tag)
            phase_reset()

        y_free_st = [None, None]
        ln_fm("ln1", S["pre1T"], "ln1_g", "ln1_b", S["h1T_f32"], S["h1T_bf"])
        if stop_after == "p7":
            return nc

        scratch("memTm", [NMEM, D], F32)
        tm2fm("p8t", I["mem"], D, S["memT"], nrows=NMEM)
        gp = GP("p8")
        gemm(b, cx, "xa_q", S["h1T_bf"], D, 0, T, [(I["xa_wq"], 0, D)], "fm", gp.epi("fm", [(0, D, S["qxT"], 0)], dtype=BF16))
        gemm(b, cx, "xa_k", S["memT"], D, 0, NMEM, [(I["xa_wkv"], 0, D)], "fm", gp.epi("fm", [(0, D, S["kxT"], 0)], dtype=BF16), TM=NMEM)
        gemm(b, cx, "xa_v", S["memT"], D, 0, NMEM, [(I["xa_wkv"], D, D)], "tm", gp.epi("tm", (S["vx"], 0), dtype=BF16), TM=NMEM)
        gp.close("p8")
        phase_reset()
        attn("p8a", 4,
             lambda h: [(S["qxT"], h * 1024 + dc * 128, 128) for dc in range(8)],
             lambda h: [(S["kxT"], h * 1024 + dc * 128, 128) for dc in range(8)],
             lambda h: S["vx"][:, h * 1024:(h + 1) * 1024], NMEM, 8, 1024.0 ** -0.5, False, S["oxT"])
        gp = GP("p9")
        gemm(b, cx, "xa_o", S["oxT"], D, 0, T, [(I["xa_wo"], 0, D)], "fm",
             gp.epi("fm", [(0, D, S["pre2T"], 0)], fn=fn_axpy(gp, S["h1T_f32"], ALPHA)))
        gp.close("p9")
        phase_reset()
        ln_fm("ln2", S["pre2T"], "ln2_g", "ln2_b", S["h2T_f32"], S["h2T_bf"])
        if stop_after == "p9":
            return nc

        gp = GP("p10")

        def fn_relu2(eng, o, src, tmp, info, w, si, ci):
            tk = b.op("act", lambda e: e.activation(out=tmp, in_=src, func=AF.Relu), waits=w, inc=sA)
            return b.op("dve", lambda e: e.tensor_tensor(out=o, in0=tmp, in1=tmp, op=ALU.mult), waits=[tk], inc=sD)
        gemm(b, cx, "mlp1", S["h2T_bf"], D, 0, T, [(I["mlp_w1"], 0, DFF)], "fm",
             gp.epi("fm", [(0, DFF, S["ffT"], 0)], dtype=BF16, fn=fn_relu2))
        gp.close("p10")
        phase_reset()
        gp = GP("p11")
        for q_ in range(4):
            src_ = S["h2T_f32"] if q_ == 0 else S["pre3T"]
            gemm(b, cx, f"mlp2_{q_}", S["ffT"][q_ * D:(q_ + 1) * D, :], D, 0, T, [(I["mlp_w2"][q_ * D:(q_ + 1) * D, :], 0, D)], "fm",
                 gp.epi("fm", [(0, D, S["pre3T"], 0)], fn=fn_axpy(gp, src_, ALPHA if q_ == 0 else 1.0)))
            b.wait("sp", gp.stg_free)
            b.flush(f"p11_{q_}")
        gp.close("p11")
        phase_reset()
        ln_fm("ln3", S["pre3T"], "ln3_g", "ln3_b", None, None, out_tm=y)
    nc.flush_names = b.names
    return nc


def _cols(v):
    v = np.asarray(v, np.float32).reshape(-1)
    return np.ascontiguousarray(v.reshape(-1, 128).T)


def _rope_table(pos):
    inv = (1.0 / (10000.0 ** (np.arange(0, 64, 2, dtype=np.float32) / np.float32(64)))).astype(np.float32)
    ang = pos.astype(np.float32)[:, None] * inv[None, :]
    c = np.cos(ang).astype(np.float32).T
    s = np.sin(ang).astype(np.float32).T
    return (np.ascontiguousarray(np.concatenate([c, c, c, c], axis=0)),
            np.ascontiguousarray(np.concatenate([-s, s, -s, s], axis=0)))


def make_in_maps(inp):
    f = lambda a: np.ascontiguousarray(np.asarray(a, np.float32))
    shared = {
        "w_in": f(inp["w_in"][0]), "w_uq": f(inp["w_uq"][0]), "w_ukv": f(inp["w_ukv"][0]),
        "w_br_mla": f(inp["w_branch_mla"][0]), "w_br_gla": f(inp["w_branch_gla"][0]),
        "w_mix_out": f(inp["w_mix_out"][0]), "xa_wq": f(inp["xa_wq"][0]), "xa_wkv": f(inp["xa_wkv"][0]),
        "xa_wo": f(inp["xa_wo"][0]), "mlp_w1": f(inp["mlp_w1"][0]), "mlp_w2": f(inp["mlp_w2"][0]),
        "ln_in_g": _cols(inp["ln_in_g"]), "ln_in_b": _cols(inp["ln_in_b"]),
        "ln1_g": _cols(inp["ln1_g"]), "ln1_b": _cols(inp["ln1_b"]),
        "ln2_g": _cols(inp["ln2_g"]), "ln2_b": _cols(inp["ln2_b"]),
        "ln3_g": _cols(inp["ln3_g"]), "ln3_b": _cols(inp["ln3_b"]),
        "b_merge": _cols(inp["b_merge"]), "q_norm": _cols(inp["mla_q_norm"]), "kv_norm": _cols(inp["mla_kv_norm"]),
        "ident": np.eye(128, dtype=np.float32),
        "w2b": np.ascontiguousarray(np.concatenate([f(inp["gla_gate_w2"][0]), f(inp["gla_gate_b"][0])[:, None, :]], axis=1)),
        "gnorm_bc": np.ascontiguousarray(np.tile(f(inp["gla_norm"][0])[None, :], (64, 4))),
    }
    jj = np.arange(128)[:, None]
    cc = np.arange(128)[None, :]
    same = (jj // 64) == (cc // 64)
    M1 = np.zeros((2, 128, 130), np.float32)
    M2 = np.zeros((2, 128, 128), np.float32)
    M1[0, :, :128] = np.where(same & (jj <= cc), -1.0 / 16.0, 0.0)
    M1[1, :, :128] = np.where(same & (jj >= cc), -1.0 / 16.0, 0.0)
    for d_ in range(2):
        M1[d_, :64, 128] = -1.0 / 16.0
        M1[d_, 64:, 129] = -1.0 / 16.0
    M2[0] = np.where(same & (jj > cc), -1.0 / 16.0, 0.0)
    M2[1] = np.where(same & (jj < cc), -1.0 / 16.0, 0.0)
    j6 = np.arange(64)[:, None]
    c6 = np.arange(64)[None, :]
    mT = np.stack([np.tile((j6 <= c6).astype(np.float32), (1, 4)), np.tile((j6 >= c6).astype(np.float32), (1, 4))], axis=0)
    shared.update({"M1": M1, "M2": M2, "maskT": np.ascontiguousarray(mT)})
    maps = []
    ar = np.arange(T)
    for c in range(8):
        m = dict(shared)
        cv = np.zeros((128, 4), np.float32)
        if c < 4:
            own = f(inp["x_prompt"][c])
            m["x"] = np.ascontiguousarray(np.concatenate([own, own], axis=0))
            m["mem"] = f(inp["mem_prompt"][c])
            m["ropeC"], m["ropeS"] = _rope_table(np.concatenate([ar, ar]))
            cv[:, 0] = -30000.0
        else:
            s_, half = (c - 4) // 2, (c - 4) % 2
            xs = f(inp["x_sample"][s_])
            own = xs[half * T:(half + 1) * T]
            oth = xs[(1 - half) * T:(2 - half) * T]
            m["x"] = np.ascontiguousarray(np.concatenate([own, oth], axis=0))
            m["mem"] = f(inp["mem_sample"][s_])
            m["ropeC"], m["ropeS"] = _rope_table(np.concatenate([ar + half * T, ar + (1 - half) * T]))
            cv[:, 1] = 1.0 if half == 1 else 0.0
            cv[:, 2] = 1.0 if half == 0 else 0.0
        m["cvec"] = cv
        maps.append(m)
    return maps


def kernel(**inp):
    nc = build_program()
    maps = make_in_maps(inp)
    res = run_bass_kernel_spmd(nc, maps, core_ids=list(range(8)))
    ys = [np.asarray(r["y"], np.float32) for r in res.results]
    y_prompt = np.stack(ys[0:4], axis=0)
    y_sample = np.stack([np.concatenate([ys[4], ys[5]], axis=0), np.concatenate([ys[6], ys[7]], axis=0)], axis=0)
    return (y_prompt, y_sample)
```
